# Optimizing a Trainium2 kernel written in Bass

```python
import jax, jax.numpy as jnp
from jax import lax
import numpy as np

D_MODEL = 4096
BATCH = 4
SEQ = 2048
DEPTH = 1

GRID_W = 64
CTX_LEN = 256
M_HEADS = 8
M_DQK = 256
M_DV = 512
M_CHUNK = 64
A_HEADS = 32
A_KV_HEADS = 8
A_HEAD_DIM = 128
WINDOW = 128
A_BLOCK = 128
ROPE_BASE = 10000.0
D_FF = 4 * D_MODEL
EPS = 1e-6

M_QK_W = M_HEADS * M_DQK
M_V_W = M_HEADS * M_DV
A_Q_W = A_HEADS * A_HEAD_DIM
A_KV_W = A_KV_HEADS * A_HEAD_DIM
SPLITS = (M_QK_W, M_QK_W, M_V_W, M_V_W, 4 * M_HEADS, A_Q_W, A_KV_W, A_KV_W, D_MODEL, D_MODEL)
PROJ_W = sum(SPLITS)

kernel_name = "hybrid_mlstm_swa_dit_block"


def rmsnorm(x, g):
    xf = x.astype(jnp.float32)
    y = xf * lax.rsqrt(jnp.mean(xf * xf, axis=-1, keepdims=True) + EPS)
    return (y * g.astype(jnp.float32)).astype(x.dtype)


def modulate(h, shift, scale):
    return h * (1.0 + scale) + shift


def split_cols(a):
    idx = [int(i) for i in np.cumsum(SPLITS)[:-1]]
    return jnp.split(a, idx, axis=-1)


def _rotate(x, ang):
    x1, x2 = jnp.split(x, 2, axis=-1)
    cos = jnp.cos(ang)[:, None, :].astype(x.dtype)
    sin = jnp.sin(ang)[:, None, :].astype(x.dtype)
    return jnp.concatenate([x1 * cos - x2 * sin, x1 * sin + x2 * cos], axis=-1)


def rope_2d(x, row, col):
    half = x.shape[-1] // 2
    nf = half // 2
    freqs = ROPE_BASE ** (-jnp.arange(nf, dtype=jnp.float32) / nf)
    ang_r = row.astype(jnp.float32)[:, None] * freqs
    ang_c = col.astype(jnp.float32)[:, None] * freqs
    return jnp.concatenate([_rotate(x[..., :half], ang_r), _rotate(x[..., half:], ang_c)], axis=-1)


def mlstm_scan(q, k, v, ig, lf, state):
    B, S, H, _ = q.shape
    nc = S // M_CHUNK
    L = M_CHUNK
    tril = jnp.tril(jnp.ones((L, L), dtype=bool))

    def chunks(a):
        return jnp.moveaxis(a.reshape((B, nc, L) + a.shape[2:]), 1, 0)

    def step(carry, xs):
        C, n, m = carry
        qc, kc, vc, igc, lfc = xs
        b = jnp.cumsum(lfc, axis=1)
        a = b + m[:, None, :]
        Dm = b[:, :, None, :] - b[:, None, :, :] + igc[:, None, :, :]
        Dm = jnp.where(tril[None, :, :, None], Dm, -jnp.inf)
        m_t = jnp.maximum(a, jnp.max(Dm, axis=2))
        s = jnp.einsum('bthd,bshd->btsh', qc, kc) * jnp.exp(Dm - m_t[:, :, None, :])
        inter = jnp.exp(a - m_t)
        num = inter[..., None] * jnp.einsum('bthd,bhvd->bthv', qc, C) + jnp.einsum('btsh,bshv->bthv', s, vc)
        den = inter * jnp.einsum('bthd,bhd->bth', qc, n) + jnp.sum(s, axis=2)
        h = num / jnp.maximum(jnp.abs(den), jnp.exp(-m_t))[..., None]
        bL = b[:, -1]
        w = bL[:, None, :] - b + igc
        m_new = jnp.maximum(bL + m, jnp.max(w, axis=1))
        decay = jnp.exp(bL + m - m_new)
        wexp = jnp.exp(w - m_new[:, None, :])
        C_new = decay[:, :, None, None] * C + jnp.einsum('bsh,bshv,bshd->bhvd', wexp, vc, kc)
        n_new = decay[:, :, None] * n + jnp.einsum('bsh,bshd->bhd', wexp, kc)
        return (C_new, n_new, m_new), h

    state, hs = lax.scan(step, state, (chunks(q), chunks(k), chunks(v), chunks(ig), chunks(lf)))
    h = jnp.moveaxis(hs, 0, 1).reshape(B, S, H, v.shape[-1])
    return h, state


def mlstm_bidir(qm, km, vm, gm, qmc, kmc, vmc, gmc, gate_b):
    f32 = jnp.float32
    B, S = qm.shape[:2]
    Cn = qmc.shape[1]

    def prep(q, k, v, g, n_tok):
        q = q.astype(f32).reshape(B, n_tok, M_HEADS, M_DQK) * (M_DQK ** -0.5)
        k = k.astype(f32).reshape(B, n_tok, M_HEADS, M_DQK)
        v = v.astype(f32).reshape(B, n_tok, M_HEADS, M_DV)
        g = g.astype(f32).reshape(B, n_tok, 4, M_HEADS) + gate_b.astype(f32)
        return q, k, v, g[:, :, 0], jax.nn.log_sigmoid(g[:, :, 1]), g[:, :, 2], jax.nn.log_sigmoid(g[:, :, 3])

    q, k, v, ig_f, lf_f, ig_b, lf_b = prep(qm, km, vm, gm, S)
    qc, kc, vc, igc_f, lfc_f, igc_b, lfc_b = prep(qmc, kmc, vmc, gmc, Cn)
    zero = (jnp.zeros((B, M_HEADS, M_DV, M_DQK), f32), jnp.zeros((B, M_HEADS, M_DQK), f32),
            jnp.zeros((B, M_HEADS), f32))
    rev = lambda a: jnp.flip(a, axis=1)
    hc_f, st_f = mlstm_scan(qc, kc, vc, igc_f, lfc_f, zero)
    h_f, _ = mlstm_scan(q, k, v, ig_f, lf_f, st_f)
    hc_b, st_b = mlstm_scan(rev(qc), rev(kc), rev(vc), rev(igc_b), rev(lfc_b), zero)
    h_b, _ = mlstm_scan(rev(q), rev(k), rev(v), rev(ig_b), rev(lf_b), st_b)
    return h_f + rev(h_b), hc_f + rev(hc_b)


def mlstm_readout(h, o, norm_g):
    B, N = h.shape[:2]
    h = h * lax.rsqrt(jnp.mean(h * h, axis=-1, keepdims=True) + EPS)
    h = h.reshape(B, N, M_V_W) * norm_g.astype(jnp.float32) * jax.nn.sigmoid(o.astype(jnp.float32))
    return h.astype(o.dtype)


def window_attention(q, k, v, kc, vc, sink):
    B, S, Hq, d = q.shape
    Hkv = k.shape[2]
    G = Hq // Hkv
    Cn = kc.shape[1]
    nb = S // A_BLOCK
    scale = d ** -0.5
    pad = ((0, 0), (A_BLOCK, A_BLOCK), (0, 0), (0, 0))
    kp = jnp.pad(k, pad)
    vp = jnp.pad(v, pad)
    qg = q.reshape(B, S, Hkv, G, d)
    sink_l = sink.astype(jnp.float32).reshape(Hkv, G)
    offs_q = jnp.arange(A_BLOCK)
    offs_k = jnp.arange(3 * A_BLOCK) - A_BLOCK

    def block(n):
        start = n * A_BLOCK
        qb = lax.dynamic_slice_in_dim(qg, start, A_BLOCK, axis=1)
        kb = lax.dynamic_slice_in_dim(kp, start, 3 * A_BLOCK, axis=1)
        vb = lax.dynamic_slice_in_dim(vp, start, 3 * A_BLOCK, axis=1)
        qpos = start + offs_q
        kpos = start + offs_k
        valid = (jnp.abs(qpos[:, None] - kpos[None, :]) <= WINDOW) & (kpos >= 0)[None, :] & (kpos < S)[None, :]
        s_w = jnp.einsum('bqhgd,bkhd->bhgqk', qb, kb).astype(jnp.float32) * scale
        s_w = jnp.where(valid, s_w, -jnp.inf)
        s_c = jnp.einsum('bqhgd,bchd->bhgqc', qb, kc).astype(jnp.float32) * scale
        s_s = jnp.broadcast_to(sink_l[None, :, :, None, None], (B, Hkv, G, A_BLOCK, 1))
        p = jax.nn.softmax(jnp.concatenate([s_w, s_c, s_s], axis=-1), axis=-1)
        pw = p[..., :3 * A_BLOCK].astype(v.dtype)
        pc = p[..., 3 * A_BLOCK:3 * A_BLOCK + Cn].astype(v.dtype)
        o = jnp.einsum('bhgqk,bkhd->bqhgd', pw, vb) + jnp.einsum('bhgqc,bchd->bqhgd', pc, vc)
        return o.reshape(B, A_BLOCK, Hq * d)

    out = lax.map(block, jnp.arange(nb))
    return jnp.moveaxis(out, 0, 1).reshape(B, S, Hq * d)


def context_attention(qc, kc, vc, sink):
    B, Cn, Hq, d = qc.shape
    Hkv = kc.shape[2]
    G = Hq // Hkv
    qg = qc.reshape(B, Cn, Hkv, G, d)
    s = jnp.einsum('bqhgd,bkhd->bhgqk', qg, kc).astype(jnp.float32) * (d ** -0.5)
    s_s = jnp.broadcast_to(sink.astype(jnp.float32).reshape(Hkv, G)[None, :, :, None, None], (B, Hkv, G, Cn, 1))
    p = jax.nn.softmax(jnp.concatenate([s, s_s], axis=-1), axis=-1)[..., :Cn]
    o = jnp.einsum('bhgqk,bkhd->bqhgd', p.astype(vc.dtype), vc)
    return o.reshape(B, Cn, Hq * d)


def token_mixers(u, uc, row, col, w_in, m_gate_b, m_norm_g, sink, w_out_m, w_out_a, w_o, ctx_out):
    B, S, _ = u.shape
    Cn = uc.shape[1]
    qm, km, vm, om, gm, qa, ka, va, bg_m, bg_a = split_cols(u @ w_in)
    qmc, kmc, vmc, omc, gmc, qac, kac, vac, bgc_m, bgc_a = split_cols(uc @ w_in)
    hm, hm_c = mlstm_bidir(qm, km, vm, gm, qmc, kmc, vmc, gmc, m_gate_b)
    y_m = mlstm_readout(hm, om, m_norm_g) @ w_out_m
    q = rope_2d(qa.reshape(B, S, A_HEADS, A_HEAD_DIM), row, col)
    k = rope_2d(ka.reshape(B, S, A_KV_HEADS, A_HEAD_DIM), row, col)
    v = va.reshape(B, S, A_KV_HEADS, A_HEAD_DIM)
    kc = kac.reshape(B, Cn, A_KV_HEADS, A_HEAD_DIM)
    vc = vac.reshape(B, Cn, A_KV_HEADS, A_HEAD_DIM)
    y_a = window_attention(q, k, v, kc, vc, sink) @ w_out_a
    mix = (jax.nn.sigmoid(bg_m) * y_m + jax.nn.sigmoid(bg_a) * y_a) @ w_o
    mix_c = None
    if ctx_out:
        y_m_c = mlstm_readout(hm_c, omc, m_norm_g) @ w_out_m
        y_a_c = context_attention(qac.reshape(B, Cn, A_HEADS, A_HEAD_DIM), kc, vc, sink) @ w_out_a
        mix_c = (jax.nn.sigmoid(bgc_m) * y_m_c + jax.nn.sigmoid(bgc_a) * y_a_c) @ w_o
    return mix, mix_c


def sq_relu_mlp(u, w1, w2):
    return jnp.square(jax.nn.relu(u @ w1)) @ w2


def setup_inputs(seed: int = 0) -> dict:
    key = jax.random.key(seed)
    ks = jax.random.split(key, 16)
    D = D_MODEL

    def nrm(k, shape, scale):
        return jax.random.normal(k, shape, jnp.float32) * scale

    gate_base = jnp.array([0.0, 3.0, 0.0, 3.0], jnp.float32)[None, :, None]
    return {
        "x": nrm(ks[0], (BATCH, SEQ, D), 1.0),
        "c": nrm(ks[1], (BATCH, D), 1.0),
        "ctx": nrm(ks[2], (BATCH, CTX_LEN, D), 1.0),
        "c_ctx": nrm(ks[3], (D,), 1.0),
        "w_mod": nrm(ks[4], (DEPTH, D, 6 * D), 0.5 * D ** -0.5),
        "b_mod": nrm(ks[5], (DEPTH, 6 * D), 0.02),
        "norm_g": 1.0 + nrm(ks[6], (DEPTH, 4, D), 0.02),
        "w_in": nrm(ks[7], (DEPTH, D, PROJ_W), D ** -0.5),
        "m_gate_b": gate_base + nrm(ks[8], (DEPTH, 4, M_HEADS), 0.1),
        "m_norm_g": 1.0 + nrm(ks[9], (DEPTH, M_V_W), 0.02),
        "attn_sink": nrm(ks[10], (DEPTH, A_HEADS), 0.5),
        "w_out_m": nrm(ks[11], (DEPTH, M_V_W, D), M_V_W ** -0.5),
        "w_out_a": nrm(ks[12], (DEPTH, A_Q_W, D), A_Q_W ** -0.5),
        "w_o": nrm(ks[13], (DEPTH, D, D), D ** -0.5),
        "w_ff1": nrm(ks[14], (DEPTH, D, D_FF), D ** -0.5),
        "w_ff2": nrm(ks[15], (DEPTH, D_FF, D), D_FF ** -0.5),
    }


def reference(x, c, ctx, c_ctx, w_mod, b_mod, norm_g, w_in, m_gate_b, m_norm_g, attn_sink,
              w_out_m, w_out_a, w_o, w_ff1, w_ff2):
    B, S, _ = x.shape
    ROWS = S // GRID_W
    row, col = jnp.meshgrid(jnp.arange(ROWS), jnp.arange(GRID_W), indexing='ij')
    row = row.reshape(-1)
    col = col.reshape(-1)
    sc = jax.nn.silu(c)
    scc = jax.nn.silu(c_ctx)
    h, hc = x, ctx
    for l in range(DEPTH):
        ctx_out = l < DEPTH - 1
        mod = (sc @ w_mod[l] + b_mod[l])[:, None, :]
        mod_c = scc @ w_mod[l] + b_mod[l]
        sh1, s1, g1, sh2, s2, g2 = jnp.split(mod, 6, axis=-1)
        csh1, cs1, cg1, csh2, cs2, cg2 = jnp.split(mod_c, 6, axis=-1)
        u = modulate(rmsnorm(h, norm_g[l, 0]), sh1, s1)
        uc = modulate(rmsnorm(hc, norm_g[l, 0]), csh1, cs1)
        mix, mix_c = token_mixers(u, uc, row, col, w_in[l], m_gate_b[l], m_norm_g[l], attn_sink[l],
                                  w_out_m[l], w_out_a[l], w_o[l], ctx_out)
        h = h + g1 * rmsnorm(mix, norm_g[l, 1])
        u2 = modulate(rmsnorm(h, norm_g[l, 2]), sh2, s2)
        h = h + g2 * rmsnorm(sq_relu_mlp(u2, w_ff1[l], w_ff2[l]), norm_g[l, 3])
        if ctx_out:
            hc = hc + cg1 * rmsnorm(mix_c, norm_g[l, 1])
            uc2 = modulate(rmsnorm(hc, norm_g[l, 2]), csh2, cs2)
            hc = hc + cg2 * rmsnorm(sq_relu_mlp(uc2, w_ff1[l], w_ff2[l]), norm_g[l, 3])
    return h
```

```python
import contextlib
import numpy as np
import ml_dtypes
import concourse.bass as bass
import concourse.mybir as mybir
from concourse.bass_utils import run_bass_kernel_spmd

F32 = mybir.dt.float32
BF16 = mybir.dt.bfloat16
AF = mybir.ActivationFunctionType
ALU = mybir.AluOpType
AX = mybir.AxisListType

ENGS = ("pe", "act", "dve", "pool", "sp")

D = 4096
KC = 32
T = 1024
NT = 8
TP = 1280
NTP = 10
DFF = 16384
EPS = 1e-6
O_QM, O_KM, O_VM, O_OM, O_GM, O_QA, O_KA, O_VA, O_BM, O_BA, O_END = (
    0, 2048, 4096, 8192, 12288, 12320, 16416, 17440, 18464, 22560, 26656)


class Prog:
    def __init__(self, nc):
        self.nc = nc
        self.q = {e: [] for e in ENGS}
        self.cnt = {}
        self.waited = {e: {} for e in ENGS}
        self.sems = {}
        self.dma_keys = []

    def _waits(self, eng, waits):
        out = []
        w = self.waited[eng]
        for t in waits:
            if t is None:
                continue
            k, v = t
            if v <= 0:
                continue
            if w.get(k, 0) >= v:
                continue
            w[k] = v
            out.append((k, v))
        return out

    def op(self, eng, fn, waits=(), inc=True):
        ws = self._waits(eng, waits)
        if inc:
            self.cnt[eng] = self.cnt.get(eng, 0) + 1
        self.q[eng].append((ws, fn, eng if inc else None, 1))
        return (eng, self.cnt.get(eng, 0))

    def dma(self, eng, semkey, out, in_, waits=()):
        ws = self._waits(eng, waits)
        if semkey not in self.cnt:
            self.cnt[semkey] = 0
            self.dma_keys.append(semkey)
        self.cnt[semkey] += 16

        def fn(e, out=out, in_=in_):
            return e.dma_start(out=out, in_=in_)
        self.q[eng].append((ws, fn, semkey, 16))
        return (semkey, self.cnt[semkey])


    def mm(self, out, lhsT, rhs, start=True, stop=True, waits=(), inc=True):
        return self.op("pe", lambda e: e.matmul(out, lhsT=lhsT, rhs=rhs, start=start, stop=stop), waits, inc)

    def tr(self, out, in_, ident, waits=(), inc=True):
        return self.op("pe", lambda e: e.transpose(out, in_, ident), waits, inc)

    def act(self, out, in_, func, scale=1.0, bias=0.0, accum_out=None, waits=(), inc=True):
        def fn(e):
            kw = {}
            if accum_out is not None:
                kw["accum_out"] = accum_out
            return e.activation(out=out, in_=in_, func=func, bias=bias, scale=scale, **kw)
        return self.op("act", fn, waits, inc)

    def tt(self, out, in0, in1, op, waits=(), eng="dve", inc=True):
        return self.op(eng, lambda e: e.tensor_tensor(out=out, in0=in0, in1=in1, op=op), waits, inc)

    def ts(self, out, in0, s1, op0, s2=None, op1=None, waits=(), eng="dve", inc=True, accum_out=None):
        def fn(e):
            kw = {}
            if op1 is not None:
                kw["op1"] = op1
            if accum_out is not None:
                kw["accum_out"] = accum_out
            return e.tensor_scalar(out=out, in0=in0, scalar1=s1, scalar2=s2, op0=op0, **kw)
        return self.op(eng, fn, waits, inc)

    def stt(self, out, in0, scalar, in1, op0, op1, waits=(), inc=True):
        return self.op("dve", lambda e: e.scalar_tensor_tensor(out=out, in0=in0, scalar=scalar, in1=in1, op0=op0, op1=op1), waits, inc)

    def cp(self, out, in_, waits=(), eng="dve", inc=True):
        return self.op(eng, lambda e: e.tensor_copy(out=out, in_=in_), waits, inc)

    def recip(self, out, in_, waits=(), inc=True):
        return self.op("dve", lambda e: e.reciprocal(out=out, in_=in_), waits, inc)

    def fence(self, eng, src):
        dst = self.fscr[eng]
        if eng == "act":
            return self.op("act", lambda e: e.activation(out=dst, in_=src, func=AF.Copy, bias=0.0, scale=1.0))
        return self.op(eng, lambda e: e.tensor_copy(out=dst, in_=src))

    def barrier(self):
        toks = [(k, v) for k, v in self.cnt.items()]
        for e in ENGS:
            ws = self._waits(e, toks)
            if ws:
                self.q[e].append((ws, None, None, 0))

    def tok(self, key):
        return (key, self.cnt.get(key, 0))

    def emit(self, final_waits):
        nc = self.nc
        keys = list(ENGS) + self.dma_keys
        with contextlib.ExitStack() as st:
            for k in keys:
                self.sems[k] = st.enter_context(nc.semaphore("s_" + k))
            block = st.enter_context(nc.Block())
            engmap = {"pe": block.tensor, "act": block.scalar, "dve": block.vector,
                      "pool": block.gpsimd, "sp": block.sync}
            self.q["sp"].append((self._waits("sp", final_waits), None, None, 0))

            def mk(ename):
                def body(e):
                    for ws, fn, sk, n in self.q[ename]:
                        for (k, v) in ws:
                            e.wait_ge(self.sems[k], v)
                        if fn is None:
                            continue
                        ins = fn(e)
                        if sk is not None:
                            ins.then_inc(self.sems[sk], n)
                return body
            for ename in ENGS:
                engmap[ename](mk(ename))


class Ring:
    def __init__(self, tiles, name):
        self.tiles = tiles
        self.free = [[] for _ in tiles]
        self.i = 0
        self.name = name

    def next(self):
        s = self.i % len(self.tiles)
        self.i += 1
        fr = self.free[s]
        self.free[s] = []
        return s, self.tiles[s], fr

    def release(self, s, tok):
        self.free[s].append(tok)


class WStream:
    def __init__(self, P, ring):
        self.P = P
        self.ring = ring
        self.pending = []

    def fetch(self, src_ap, view):
        s, t, fr = self.ring.next()
        tok = self.P.dma("pool", f"{self.ring.name}{s}", view(t), src_ap, waits=fr)
        return s, t, tok


def build_program(debug=False, stop_after=None):
    nc = bass.Bass("TRN2", target_bir_lowering=False)
    kind_s = "ExternalOutput" if debug else "Internal"

    early = stop_after in ("S1", "G1", "G2", "M", "A")
    skip_names = ("w_out_m", "w_out_a", "w_o", "w_ff1", "w_ff2") if early else ()

    def din(name, shape, dt=F32):
        if name in skip_names:
            nc.dram_tensor(name + "_dummy", [128, 128], dt, kind="ExternalInput")
            return None
        return nc.dram_tensor(name, list(shape), dt, kind="ExternalInput").ap()

    def dscr(name, shape, dt=F32):
        return nc.dram_tensor(name, list(shape), dt, kind=kind_s).ap()

    x_own = din("x_own", [T, D])
    x_pre = din("x_pre", [TP, D])
    cvec = din("cvec", [128, KC, 2])
    w_mod = din("w_mod", [D, 6 * D])
    b_mod = din("b_mod", [6 * D])
    norm_g = din("norm_g", [4, D])
    w_in = din("w_in", [D, O_END])
    w_gate = din("w_gate", [D, 32])
    gate_b = din("gate_b", [32])
    m_norm_g = din("m_norm_g", [D])
    sink = din("sink", [32])
    w_out_m = din("w_out_m", [D, D])
    w_out_a = din("w_out_a", [D, D])
    w_o = din("w_o", [D, D])
    w_ff1 = din("w_ff1", [D, DFF])
    w_ff2 = din("w_ff2", [DFF, D])
    cf = din("cf", [128, 5 * 128])
    cb = din("cb", [128, 2 * 128 + 2 * 512], BF16)
    ropec = din("ropec", [128, 2, T + 128])

    out = nc.dram_tensor("out", [T, D], F32, kind="ExternalOutput").ap()

    modrow = dscr("modrow", [2, 6 * D])
    QmT = dscr("QmT", [2048, T], BF16)
    KmT = dscr("KmT", [2048, T], BF16)
    Km = dscr("Km", [T, 2048], BF16)
    Vm = dscr("Vm", [T, D], BF16)
    Gm = dscr("Gm", [T, D])
    QaT = dscr("QaT", [D, T], BF16)
    KaT = dscr("KaT", [1024, T + 384], BF16)
    Va = dscr("Va", [T + 384, 1024], BF16)
    sgmT = dscr("sgmT", [D, T])
    sgaT = dscr("sgaT", [D, T])
    Kp = dscr("Kp", [TP, 2048], BF16)
    Vp = dscr("Vp", [TP, D], BF16)
    hmT = dscr("hmT", [D, T], BF16)
    haT = dscr("haT", [D, T], BF16)
    t1T = dscr("t1T", [D, T])
    mixd = dscr("mixd", [T, D])
    hbuf = dscr("hbuf", [T, D])
    h1T = dscr("h1T", [DFF, T], BF16)
    ffn = dscr("ffn", [T, D])
    dbgden = dscr("dbgden", [8, 2, 8, 128, 8]) if debug else None

    P = Prog(nc)
    st = contextlib.ExitStack()
    with st:
        def sb(name, shape, dt=F32):
            return st.enter_context(nc.sbuf_tensor(name, list(shape), dt))

        ps = [st.enter_context(nc.psum_tensor(f"ps{i}", [128, 512], F32)) for i in range(8)]
        psfree = [[] for _ in range(8)]

        def ps_take(i):
            fr = psfree[i]
            psfree[i] = []
            return fr

        cf_t = sb("cf_t", [128, 640])
        cb_t = sb("cb_t", [128, 1280], BF16)
        t_cf = P.dma("sp", "c0", cf_t[:], cf)
        t_cb = P.dma("sp", "c1", cb_t[:], cb)
        ident_f = cf_t[:, 0:128]
        triU = cf_t[:, 128:256]
        triL = cf_t[:, 256:384]
        ones_f = cf_t[:, 384:512]
        RT = cf_t[:, 512:640]
        ident_b = cb_t[:, 0:128]
        ones_b = cb_t[:, 128:256]
        maskU4 = cb_t[:, 256:768]
        maskL4 = cb_t[:, 768:1280]
        CONST = [t_cf, t_cb]
        fscr_t = sb("fscr", [128, 8])
        P.fscr = {"act": fscr_t[:, 0:1], "dve": fscr_t[:, 2:3], "pool": fscr_t[:, 4:5]}

        wring = Ring([sb(f"wr{i}", [128, KC, 256], BF16) for i in range(3)], "w")
        WS = WStream(P, wring)

        def wblock(W, c0, ncols=256, r0=0):
            src = W[r0:r0 + D, c0:c0 + ncols].rearrange("(kc p) n -> p kc n", p=128)
            return WS.fetch(src, lambda t: t[:, :, 0:ncols])

        sc0 = contextlib.ExitStack()
        sb0 = lambda name, shape, dt=F32: sc0.enter_context(nc.sbuf_tensor(name, list(shape), dt))
        sc32 = sb0("sc32", [128, KC, 2])
        scb = sb0("scb", [128, KC, 2], BF16)
        sig = sb0("sig_tmp", [128, KC, 2])
        t_c = P.dma("sp", "c2", sc32[:], cvec)
        t_sg = P.op("act", lambda e: e.activation(out=sig[:], in_=sc32[:], func=AF.Sigmoid), waits=[t_c])
        t_scb = P.op("dve", lambda e: e.tensor_tensor(out=scb[:], in0=sc32[:], in1=sig[:], op=ALU.mult), waits=[t_sg])
        mrow = [sb0(f"mrow{i}", [2, 512]) for i in range(2)]
        brow = [sb0(f"brow{i}", [2, 512]) for i in range(2)]
        mrow_free = [[], []]
        nblk = 6 * D // 256
        for nb in range(nblk):
            s, wt, tw = wblock(w_mod, nb * 256)
            half = nb % 2
            j = (nb // 2) % 2
            pi = j
            if half == 0:
                fr = ps_take(pi)
                tb = P.dma("sp", f"brow{j}", brow[j][:], b_mod[nb * 256:nb * 256 + 512].partition_broadcast(2),
                           waits=mrow_free[j])
            for kc in range(KC):
                tmm = P.op("pe", lambda e, pi=pi, wt=wt, kc=kc, half=half: e.matmul(
                    ps[pi][0:2, half * 256:(half + 1) * 256], lhsT=scb[:, kc, :], rhs=wt[:, kc, 0:256],
                    start=(kc == 0), stop=(kc == KC - 1)),
                    waits=([t_scb, tw] + (fr if half == 0 else [])) if kc == 0 else [], inc=(kc == KC - 1))
            wring.release(s, tmm)
            if half == 1:
                n0 = (nb - 1) * 256
                ta = P.op("dve", lambda e, pi=pi, j=j: e.tensor_tensor(out=mrow[j][:], in0=ps[pi][0:2, :], in1=brow[j][:], op=ALU.add),
                          waits=[tmm, tb] + mrow_free[j])
                psfree[pi].append(ta)
                td = P.dma("sp", "modst", modrow[:, n0:n0 + 512], mrow[j][:], waits=[ta])
                mrow_free[j] = [td]
        T_MOD = P.tok("modst")
        P.barrier()
        sc0.close()

        if stop_after == "S1":
            P.emit([T_MOD])
            return nc


        P.barrier()

        def alloc(sc, name, shape, dt=F32):
            return sc.enter_context(nc.sbuf_tensor(name, list(shape), dt))

        gates_all = sb("gates_all", [128, 18, 32])
        small = sb("small", [128, 64])
        gemm_banks = [0, 1, 2, 3]
        gb_i = [0]

        def next_bank(bset=gemm_banks, ctr=gb_i):
            b = bset[ctr[0] % len(bset)]
            ctr[0] += 1
            return b, ps_take(b)

        ev_i = [0]
        ev_banks = [4, 5, 6, 7]

        def build_uT(sc, actT, src, ntiles, mod_r, lat_tiles, sh_col, s_col, ng_row, pfx):
            a_bc = alloc(sc, pfx + "a_bc", [128, D])
            sh_bc = alloc(sc, pfx + "sh_bc", [128, D])
            xt = [alloc(sc, pfx + f"xt{i}", [128, D]) for i in range(2)]
            ub = alloc(sc, pfx + "ub", [128, D], BF16)
            ssq = alloc(sc, pfx + "ssq", [128, 4])
            xfree = [[], []]
            bc_tok = None
            ub_free = []
            last = []
            for tt in range(ntiles):
                r = 0 if tt < lat_tiles else 1
                if tt == 0 or tt == lat_tiles:
                    w0 = last + [T_MOD]
                    t1_ = P.dma("sp", pfx + "bc0", a_bc[:], modrow[r, s_col * D:(s_col + 1) * D].partition_broadcast(128), waits=w0)
                    t2_ = P.dma("sp", pfx + "bc1", sh_bc[:], modrow[r, sh_col * D:(sh_col + 1) * D].partition_broadcast(128), waits=w0)
                    t3_ = P.dma("sp", pfx + "bc2", xt[1][:], norm_g[ng_row, :].partition_broadcast(128), waits=w0 + xfree[1])
                    bc_tok = P.stt(a_bc[:], a_bc[:], 1.0, xt[1][:], ALU.add, ALU.mult, waits=[t1_, t3_])
                    xfree[1] = [bc_tok]
                    bc_tok2 = t2_
                xs = tt % 2
                tl = P.dma("sp", pfx + f"x{xs}", xt[xs][:], src[tt * 128:(tt + 1) * 128, :], waits=xfree[xs])
                tsq = P.act(ub[:], xt[xs][:], AF.Square, accum_out=ssq[:, 0:1], waits=[tl] + ub_free)
                tsq = P.fence("act", ssq[:, 0:1])
                tms = P.ts(ssq[:, 1:2], ssq[:, 0:1], 1.0 / D, ALU.mult, EPS, ALU.add, waits=[tsq])
                tms = P.fence("dve", ssq[:, 1:2])
                tq = P.act(ssq[:, 2:3], ssq[:, 1:2], AF.Sqrt, waits=[tms])
                tq = P.fence("act", ssq[:, 2:3])
                trc = P.recip(ssq[:, 3:4], ssq[:, 2:3], waits=[tq])
                trc = P.fence("dve", ssq[:, 3:4])
                tst = P.stt(xt[xs][:], xt[xs][:], ssq[:, 3:4], a_bc[:], ALU.mult, ALU.mult, waits=[bc_tok, trc])
                tu = P.tt(ub[:], xt[xs][:], sh_bc[:], ALU.add, waits=[bc_tok2, tst])
                xfree[xs] = [tu]
                for q4 in range(4):
                    b, fr = next_bank()
                    pb = ps[b][:].bitcast(BF16)
                    for i in range(8):
                        kc = q4 * 8 + i
                        tp = P.tr(pb[:, i * 128:(i + 1) * 128], ub[:, kc * 128:(kc + 1) * 128], ident_b,
                                  waits=[tu] + fr + CONST if i == 0 else [], inc=(i == 7))
                    dst = actT[:, q4 * 8:(q4 + 1) * 8, tt * 128:(tt + 1) * 128]
                    srcp = pb.rearrange("p (a b) -> p a b", a=8)
                    if q4 % 2 == 0:
                        te = P.act(dst, srcp, AF.Copy, waits=[tp])
                    else:
                        te = P.cp(dst, srcp, waits=[tp])
                    psfree[b].append(te)
                    last = [te]
                ub_free = [tp]
            return last

        class Stage:
            def __init__(self, sc, name, shape, dt, n):
                self.ring = Ring([alloc(sc, f"{name}{i}", shape, dt) for i in range(n)], name)

            def get(self):
                return self.ring.next()

            def store(self, s, dst, src, tok, key):
                td = P.dma("sp", f"st_{self.ring.name}{s}", dst, src, waits=[tok])
                self.ring.release(s, td)
                return td

        def gemm_fm(W, c0, ncols, actT, tsegs, act_toks, evac):
            nblk = (ncols + 255) // 256
            for bi in range(nblk):
                nb = min(256, ncols - bi * 256)
                s, wt, tw = wblock(W, c0 + bi * 256, nb)
                tlast = None
                for j in range(nb // 128):
                    for si, segs in enumerate(tsegs):
                        b, fr = next_bank()
                        first = True
                        for (t0, n, o0) in segs:
                            for kc in range(KC):
                                tlast = P.mm(ps[b][:, o0:o0 + n], wt[:, kc, j * 128:(j + 1) * 128], actT[:, kc, t0:t0 + n],
                                             start=(kc == 0), stop=(kc == KC - 1),
                                             waits=([tw] + fr + act_toks) if first else [], inc=(kc == KC - 1))
                                first = False
                        toks = evac(b, bi * 2 + j, si, tlast)
                        psfree[b].extend(toks)
                wring.release(s, tlast)

        def gemm_tm(W, c0, ncols, actT, tiles, act_toks, evac, blk=256):
            nblk = (ncols + blk - 1) // blk
            for bi in range(nblk):
                nb = min(blk, ncols - bi * blk)
                s, wt, tw = wblock(W, c0 + bi * blk, nb)
                tlast = None
                for tt in tiles:
                    b, fr = next_bank()
                    for kc in range(KC):
                        tlast = P.mm(ps[b][:, 0:nb], actT[:, kc, tt * 128:(tt + 1) * 128], wt[:, kc, 0:nb],
                                     start=(kc == 0), stop=(kc == KC - 1),
                                     waits=([tw] + fr + act_toks) if kc == 0 else [], inc=(kc == KC - 1))
                    toks = evac(b, bi * blk, nb, tt, tlast)
                    psfree[b].extend(toks)
                wring.release(s, tlast)

        def rope_evac(b, n, tok, xq_st, tmp_st, out_st, cs_t, c0, scale, dst, key):
            s1, xq, fr1 = xq_st.get()
            t_x = P.act(xq[:, 0:n], ps[b][:, 0:n], AF.Copy, scale=scale, waits=[tok] + fr1)
            eb = ev_banks[ev_i[0] % 4]
            ev_i[0] += 1
            fr = ps_take(eb)
            t_r = P.mm(ps[eb][:, 0:n], RT, xq[:, 0:n], waits=[t_x] + fr + CONST)
            s2, tmp, fr2 = tmp_st.get()
            t_a = P.tt(tmp[:, 0:n], xq[:, 0:n], cs_t[:, 0, c0:c0 + n], ALU.mult, waits=[t_x] + fr2)
            xq_st.ring.release(s1, t_r)
            xq_st.ring.release(s1, t_a)
            s3, tmp2, fr3 = tmp_st.get()
            t_b = P.tt(tmp2[:, 0:n], ps[eb][:, 0:n], cs_t[:, 1, c0:c0 + n], ALU.mult, waits=[t_r] + fr3)
            psfree[eb].append(t_b)
            s4, ob, fr4 = out_st.get()
            t_o = P.tt(ob[:, 0:n], tmp[:, 0:n], tmp2[:, 0:n], ALU.add, waits=fr4 + [t_a, t_b])
            tmp_st.ring.release(s2, t_o)
            tmp_st.ring.release(s3, t_o)
            out_st.store(s4, dst, ob[:, 0:n], t_o, key)
            return [t_x]

        with contextlib.ExitStack() as sc1:
            actP = alloc(sc1, "actP", [128, KC, TP], BF16)
            with contextlib.ExitStack() as sc:
                t_act = build_uT(sc, actP, x_pre, NTP, 0, 8, 0, 1, 0, "p_")
            P.barrier()
            with contextlib.ExitStack() as sc:
                stB = Stage(sc, "g1b", [128, 512], BF16, 4)
                xq_st = Stage(sc, "g1xq", [128, 512], F32, 2)
                tmp_st = Stage(sc, "g1tmp", [128, 512], F32, 4)
                cs_t = alloc(sc, "g1cs", [128, 2, T + 128])
                gbias = alloc(sc, "g1gb", [128, 32])
                t_cs = P.dma("sp", "g1c", cs_t[:], ropec)
                t_gb = P.dma("sp", "g1c", gbias[:], gate_b.partition_broadcast(128))
                T_G1C = P.tok("g1c")

                def ev_gate(b, c0, nb, tt, tok):
                    return [P.tt(gates_all[:, 8 + tt, :], ps[b][:, 0:32], gbias[:], ALU.add, waits=[tok, T_G1C])]
                gemm_tm(w_gate, 0, 32, actP, range(NTP), [], ev_gate)

                def ev_tm_bf(dst_dram, key, use_act=True):
                    def ev(b, c0, nb, tt, tok):
                        s, ob, fr = stB.get()
                        if (tt % 2) == 0:
                            t = P.act(ob[:, 0:nb], ps[b][:, 0:nb], AF.Copy, waits=[tok] + fr)
                        else:
                            t = P.cp(ob[:, 0:nb], ps[b][:, 0:nb], waits=[tok] + fr)
                        stB.store(s, dst_dram(tt, c0, nb), ob[:, 0:nb], t, key)
                        return [t]
                    return ev
                gemm_tm(w_in, O_KM, 2048, actP, range(NTP), [],
                        ev_tm_bf(lambda tt, c0, nb: Kp[tt * 128:(tt + 1) * 128, c0:c0 + nb], "stKp"))
                gemm_tm(w_in, O_VM, 4096, actP, range(NTP), [],
                        ev_tm_bf(lambda tt, c0, nb: Vp[tt * 128:(tt + 1) * 128, c0:c0 + nb], "stVp"))
                vmap = {0: 0, 8: 1, 9: 2}
                gemm_tm(w_in, O_VA, 1024, actP, [0, 8, 9], [],
                        ev_tm_bf(lambda tt, c0, nb: Va[T + vmap[tt] * 128:T + (vmap[tt] + 1) * 128, c0:c0 + nb], "stVa"))

                def ev_ka(b, ch, si, tok):
                    toks = rope_evac(b, 128, tok, xq_st, tmp_st, stB, cs_t, T, 1.0,
                                     KaT[ch * 128:(ch + 1) * 128, T:T + 128], "stKa")
                    s, ob, fr = stB.get()
                    t = P.cp(ob[:, 0:256], ps[b][:, 128:384], waits=[tok] + fr)
                    stB.store(s, KaT[ch * 128:(ch + 1) * 128, T + 128:T + 384], ob[:, 0:256], t, "stKa")
                    return toks + [t]
                gemm_fm(w_in, O_KA, 1024, actP, [[(0, 128, 0), (1024, 256, 128)]], [], ev_ka)
            P.barrier()

        if stop_after == "G1":
            P.emit([P.tok(k) for k in ("stKp", "stVp", "stVa", "stKa")])
            return nc

        with contextlib.ExitStack() as sc1:
            actO = alloc(sc1, "actO", [128, KC, T], BF16)
            with contextlib.ExitStack() as sc:
                build_uT(sc, actO, x_own, NT, 0, NT, 0, 1, 0, "o_")
            P.barrier()
            with contextlib.ExitStack() as sc:
                stB = Stage(sc, "g2b", [128, 512], BF16, 4)
                stF = Stage(sc, "g2f", [128, 512], F32, 4)
                xq_st = Stage(sc, "g2xq", [128, 512], F32, 2)
                tmp_st = Stage(sc, "g2tmp", [128, 512], F32, 4)
                cs_t = alloc(sc, "g2cs", [128, 2, T + 128])
                gbias = alloc(sc, "g2gb", [128, 32])
                mng = alloc(sc, "g2mng", [128, D])
                P.dma("sp", "g2c", cs_t[:], ropec)
                P.dma("sp", "g2c", gbias[:], gate_b.partition_broadcast(128))
                P.dma("sp", "g2c", mng[:], m_norm_g.partition_broadcast(128))
                T_G2C = P.tok("g2c")
                segs2 = [[(0, 512, 0)], [(512, 512, 0)]]

                def ev_gate2(b, c0, nb, tt, tok):
                    return [P.tt(gates_all[:, tt, :], ps[b][:, 0:32], gbias[:], ALU.add, waits=[tok, T_G2C])]
                gemm_tm(w_gate, 0, 32, actO, range(NT), [], ev_gate2)

                def ev_fm_bf(dstT, scale, key):
                    def ev(b, ch, si, tok):
                        s, ob, fr = stB.get()
                        if (ch + si) % 2 == 0:
                            t = P.act(ob[:], ps[b][:], AF.Copy, scale=scale, waits=[tok] + fr)
                        else:
                            t = P.ts(ob[:], ps[b][:], scale, ALU.mult, waits=[tok] + fr)
                        stB.store(s, dstT[ch * 128:(ch + 1) * 128, si * 512:(si + 1) * 512], ob[:], t, key)
                        return [t]
                    return ev

                def ev_fm_sig(dstT, key):
                    def ev(b, ch, si, tok):
                        s, ob, fr = stF.get()
                        t = P.act(ob[:], ps[b][:], AF.Sigmoid, waits=[tok] + fr)
                        stF.store(s, dstT[ch * 128:(ch + 1) * 128, si * 512:(si + 1) * 512], ob[:], t, key)
                        return [t]
                    return ev

                def ev_fm_rope(dstT, scale, key):
                    def ev(b, ch, si, tok):
                        return rope_evac(b, 512, tok, xq_st, tmp_st, stB, cs_t, si * 512, scale,
                                         dstT[ch * 128:(ch + 1) * 128, si * 512:(si + 1) * 512], key)
                    return ev

                def ev_tm_bf2(dst, key):
                    def ev(b, c0, nb, tt, tok):
                        s, ob, fr = stB.get()
                        if (tt % 2) == 0:
                            t = P.act(ob[:, 0:nb], ps[b][:, 0:nb], AF.Copy, waits=[tok] + fr)
                        else:
                            t = P.cp(ob[:, 0:nb], ps[b][:, 0:nb], waits=[tok] + fr)
                        stB.store(s, dst[tt * 128:(tt + 1) * 128, c0:c0 + nb], ob[:, 0:nb], t, key)
                        return [t]
                    return ev

                def ev_tm_og(b, c0, nb, tt, tok):
                    s, ob, fr = stF.get()
                    t = P.act(ob[:, 0:nb], ps[b][:, 0:nb], AF.Sigmoid, waits=[tok] + fr)
                    t2 = P.tt(ob[:, 0:nb], ob[:, 0:nb], mng[:, c0:c0 + nb], ALU.mult, waits=[t, T_G2C])
                    stF.store(s, Gm[tt * 128:(tt + 1) * 128, c0:c0 + nb], ob[:, 0:nb], t2, "stGm")
                    return [t]

                gemm_fm(w_in, O_QM, 2048, actO, segs2, [], ev_fm_bf(QmT, 1.0 / 16.0, "stQm"))
                gemm_fm(w_in, O_KM, 2048, actO, segs2, [], ev_fm_bf(KmT, 1.0, "stKmT"))
                gemm_tm(w_in, O_KM, 2048, actO, range(NT), [], ev_tm_bf2(Km, "stKm"))
                gemm_tm(w_in, O_VM, 4096, actO, range(NT), [], ev_tm_bf2(Vm, "stVm"))
                gemm_tm(w_in, O_OM, 4096, actO, range(NT), [], ev_tm_og)
                gemm_fm(w_in, O_QA, 4096, actO, segs2, [], ev_fm_rope(QaT, 128.0 ** -0.5, "stQa"))
                gemm_fm(w_in, O_KA, 1024, actO, segs2, [], ev_fm_rope(KaT, 1.0, "stKa"))
                gemm_tm(w_in, O_VA, 1024, actO, range(NT), [], ev_tm_bf2(Va, "stVa"))
                gemm_fm(w_in, O_BM, 4096, actO, segs2, [], ev_fm_sig(sgmT, "stSgm"))
                gemm_fm(w_in, O_BA, 4096, actO, segs2, [], ev_fm_sig(sgaT, "stSga"))
            P.barrier()

        if stop_after == "G2":
            P.emit([P.tok(k) for k in P.dma_keys if k.startswith("st")])
            return nc
        with contextlib.ExitStack() as sc:
            NG = 2 * 18 * 8
            L1 = alloc(sc, "m_L1", [128, 18, 32])
            Call = alloc(sc, "m_C", [128, 2, 18, 8])
            CLall = alloc(sc, "m_CL", [128, 2, 18, 8])
            Aall = alloc(sc, "m_A", [128, 2, 18, 8])
            MEAN = alloc(sc, "m_MEAN", [128, 2, 18, 8])
            MU = alloc(sc, "m_MU", [128, 2, 18, 8])
            DD = alloc(sc, "m_DD", [128, 2, 18, 8])
            WEX = alloc(sc, "m_WEX", [128, 2, 18, 8])
            WEXb = alloc(sc, "m_WEXb", [128, 2, 18, 8], BF16)
            LB = alloc(sc, "m_LB", [128, 2, 18, 8])
            DEC = alloc(sc, "m_DEC", [128, 2, 18, 8])
            mcur = [alloc(sc, f"m_mcur{i}", [128, 8]) for i in range(2)]
            flat = lambda t: t[:].rearrange("p a b c -> p (a b c)")
            t0_ = P.act(L1[:], gates_all[:], AF.Exp, scale=-1.0)
            t_l1 = P.act(L1[:], L1[:], AF.Ln, bias=1.0, waits=[t0_])
            t_z1 = P.op("dve", lambda e: e.memset(MU[:], 0.0))
            t_z2 = P.op("dve", lambda e: e.memset(DD[:], 0.0))
            frA = ps_take(0) + ps_take(1) + ps_take(2)
            tl = None
            for d in range(2):
                tri = triU if d == 0 else triL
                for i in range(18):
                    o = (d * 18 + i) * 8
                    rhs = L1[:, i, d * 16 + 8:d * 16 + 16]
                    P.mm(ps[0][:, o:o + 8], tri, rhs, waits=[t_l1] + frA + CONST, inc=False)
                    tl = P.mm(ps[1][:, o:o + 8], ones_f, rhs)
            t_c = P.cp(flat(Call), ps[0][:, 0:NG], waits=[tl])
            t_cl = P.cp(flat(CLall), ps[1][:, 0:NG], waits=[tl])
            ta_ = None
            for d in range(2):
                ta_ = P.tt(Aall[:, d, :, :], gates_all[:, :, d * 16:d * 16 + 8], Call[:, d, :, :], ALU.add, waits=[t_c])
            tmn = P.mm(ps[2][:, 0:NG], ones_f, flat(Aall), waits=[ta_])
            t_mean = P.ts(flat(MEAN), ps[2][:, 0:NG], 1.0 / 128.0, ALU.mult, waits=[tmn])
            psfree[0].append(t_c); psfree[1].append(t_cl); psfree[2].append(t_mean)
            seqs = {0: [16, 17] + list(range(8)), 1: [17, 16] + list(range(15, 7, -1)) + list(range(7, -1, -1))}
            tlast = [t_mean, t_cl, t_z1, t_z2]
            for d in range(2):
                tprev = P.op("dve", lambda e: e.memset(mcur[0][:], 0.0))
                cur = 0
                for i in seqs[d]:
                    tmu = P.tt(MU[:, d, i, :], mcur[cur][:], MEAN[:, d, i, :], ALU.max, waits=[tprev] + tlast)
                    P.tt(DD[:, d, i, :], mcur[cur][:], MU[:, d, i, :], ALU.subtract, waits=[tmu])
                    tprev = P.tt(mcur[1 - cur][:], MU[:, d, i, :], CLall[:, d, i, :], ALU.subtract, waits=[tmu])
                    cur = 1 - cur
                    tlast = []
            t_dd = P.tok("dve")
            t_s1 = P.tt(flat(WEX), flat(Aall), flat(MU), ALU.subtract, waits=[t_dd])
            t_s2 = P.tt(flat(LB), flat(Call), flat(MU), ALU.subtract, waits=[t_dd])
            t_e1 = P.act(flat(WEX), flat(WEX), AF.Exp, waits=[t_s1])
            t_e2 = P.act(flat(LB), flat(LB), AF.Exp, waits=[t_s2])
            t_e3 = P.act(flat(DEC), flat(DD), AF.Exp, waits=[t_dd])
            t_wb = P.cp(flat(WEXb), flat(WEX), waits=[t_e1])
            GATES = [t_e1, t_e2, t_e3, t_wb]

            QT = [alloc(sc, f"m_QT{i}", [128, 2, T], BF16) for i in range(2)]
            KT = [alloc(sc, f"m_KT{i}", [128, 2, T], BF16) for i in range(2)]
            KO = [alloc(sc, f"m_KO{i}", [128, NT, 256], BF16) for i in range(2)]
            VO = [alloc(sc, f"m_VO{i}", [128, NT, 512], BF16) for i in range(2)]
            KPp = [alloc(sc, "m_KP", [128, NTP, 256], BF16)] * 2
            VPp = [alloc(sc, "m_VP", [128, NTP, 512], BF16)] * 2
            GH = [alloc(sc, "m_G", [128, NT, 512])] * 2
            hbuf_free = [[], []]
            hF = alloc(sc, "m_hF", [128, NT, 512])
            hmTs = [alloc(sc, "m_hmTs", [128, 4, T], BF16)] * 2
            hmTs_free = [[]]
            C32 = [alloc(sc, f"m_C32{d}", [128, 2, 512]) for d in range(2)]
            Cb = [alloc(sc, f"m_Cb{d}", [128, 2, 512], BF16) for d in range(2)]
            n32 = [alloc(sc, f"m_n32{d}", [128, 2]) for d in range(2)]
            nbf = [alloc(sc, f"m_nb{d}", [128, 2], BF16) for d in range(2)]
            VPr = [Ring([alloc(sc, f"m_vp{d}{i}", [128, 512], BF16) for i in range(3)], f"vp{d}") for d in range(2)]
            SMr = [Ring([alloc(sc, f"m_sm{d}{i}", [128, 128], BF16) for i in range(2)], f"sm{d}") for d in range(2)]
            sml = [alloc(sc, f"m_sml{d}", [128, 8]) for d in range(2)]
            HSr = Ring([alloc(sc, f"m_hs{i}", [128, 512]) for i in range(2)], "hs")
            HMr = Ring([alloc(sc, f"m_hm{i}", [128, 512], BF16) for i in range(2)], "hm")
            junk = alloc(sc, "m_junk", [128, 512], BF16)

            def load_head(h):
                s = h % 2
                fr = hbuf_free[s]
                hbuf_free[s] = []
                k = f"mld{s}"
                P.dma("sp", k, QT[s][:], QmT[h * 256:(h + 1) * 256, :].rearrange("(dc p) t -> p dc t", p=128), waits=fr)
                P.dma("sp", k, KT[s][:], KmT[h * 256:(h + 1) * 256, :].rearrange("(dc p) t -> p dc t", p=128))
                P.dma("sp", k, KO[s][:], Km[:, h * 256:(h + 1) * 256].rearrange("(tt p) d -> p tt d", p=128))
                P.dma("sp", k, VO[s][:], Vm[:, h * 512:(h + 1) * 512].rearrange("(tt p) d -> p tt d", p=128))
                return P.tok(k)

            single_free = [[]]

            def load_head_single(h):
                fr = single_free[0]
                k = "mlds"
                P.dma("sp", k, KPp[0][:], Kp[:, h * 256:(h + 1) * 256].rearrange("(tt p) d -> p tt d", p=128), waits=fr)
                P.dma("sp", k, VPp[0][:], Vp[:, h * 512:(h + 1) * 512].rearrange("(tt p) d -> p tt d", p=128))
                P.dma("sp", k, GH[0][:], Gm[:, h * 512:(h + 1) * 512].rearrange("(tt p) d -> p tt d", p=128))
                return P.tok(k)

            reg_free = {}

            def rtake(key):
                fr = reg_free.get(key, [])
                reg_free[key] = []
                return fr
            for b in range(8):
                reg_free[("bank", b)] = ps_take(b)

            ld_tok = load_head(0)
            for h in range(8):
                s = h % 2
                lds_tok = load_head_single(h)
                nxt_tok = load_head(h + 1) if h + 1 < 8 else None
                LD = [ld_tok, lds_tok] + GATES
                state_tok = {0: [], 1: []}
                cb_read = {0: [], 1: []}
                last_use = []
                for k in range(18):
                    for d in range(2):
                        if k >= len(seqs[d]):
                            continue
                        i = seqs[d][k]
                        own = i < 8
                        first = (k == 0)
                        lastc = (k == len(seqs[d]) - 1)
                        bA, bB, bC, bD = 4 * d, 4 * d + 1, 4 * d + 2, 4 * d + 3
                        base_w = LD + (rtake(("bank", bA)) + rtake(("bank", bB)) + rtake(("bank", bC)) + rtake(("bank", bD)) if k == 0 else [])
                        if own:
                            K_tm, V_tm = KO[s][:, i, :], VO[s][:, i, :]
                        else:
                            j = i - 8
                            K_tm, V_tm = KPp[s][:, j, :], VPp[s][:, j, :]
                        wcol = WEX[:, d, i, h:h + 1]
                        wcolb = WEXb[:, d, i, h:h + 1]
                        dcol = DEC[:, d, i, h:h + 1]
                        lbcol = LB[:, d, i, h:h + 1]
                        sv, vp_t, frv = VPr[d].next()
                        t_vp = P.act(vp_t[:], V_tm, AF.Copy, scale=wcol, waits=base_w + frv)
                        vp_users = []
                        t_cb = None
                        if not first and own:
                            t_cb = P.act(Cb[d][:].rearrange("p a b -> p (a b)"), C32[d][:].rearrange("p a b -> p (a b)"),
                                         AF.Copy, scale=dcol, waits=state_tok[d] + cb_read[d])
                            t_nb = P.ts(nbf[d][:], n32[d][:], dcol, ALU.mult, waits=state_tok[d] + cb_read[d])
                            t_nb = P.fence("dve", nbf[d][:, 0:1])
                            cb_read[d] = []
                        if own:
                            c0 = i * 128
                            fr = rtake(("st", d))
                            for dc in range(2):
                                t_st = P.mm(ps[bA][:, 0:128], KT[s][:, dc, c0:c0 + 128], QT[s][:, dc, c0:c0 + 128],
                                            start=(dc == 0), stop=(dc == 1), waits=base_w + fr if dc == 0 else [], inc=(dc == 1))
                            ssm, sm_t, frs = SMr[d].next()
                            mask = maskU4[:, 0:128] if d == 0 else maskL4[:, 0:128]
                            t_sm = P.tt(sm_t[:], ps[bA][:, 0:128], mask, ALU.mult, waits=[t_st] + frs + CONST)
                            reg_free[("st", d)] = [t_sm]
                            fr = rtake(("in", d))
                            if not first:
                                for dc in range(2):
                                    P.mm(ps[bB][:, :], QT[s][:, dc, c0:c0 + 128], Cb[d][:, dc, :], start=(dc == 0), stop=False,
                                         waits=[t_cb] + fr if dc == 0 else [], inc=False)
                            t_in = P.mm(ps[bB][:, :], sm_t[:], vp_t[:], start=first, stop=True, waits=[t_sm, t_vp] + (fr if first else []))
                            fr = rtake(("den", d))
                            if not first:
                                for dc in range(2):
                                    P.mm(ps[bA][:, 128:129], QT[s][:, dc, c0:c0 + 128], nbf[d][:, dc:dc + 1], start=(dc == 0), stop=False,
                                         waits=[t_nb] + fr if dc == 0 else [], inc=False)
                            t_den = P.mm(ps[bA][:, 128:129], sm_t[:], wcolb, start=first, stop=True, waits=(fr if first else []))
                            SMr[d].release(ssm, t_den)
                            cb_read[d].append(t_den)
                            vp_users.append(t_in)
                            t_dm0 = P.ts(sml[d][:, 6:7], ps[bA][:, 128:129], -1.0, ALU.mult, lbcol, ALU.max, waits=[t_den] + ([P.tok(f"dbgden{d}")] if debug else []))
                            t_dm0f = P.fence("dve", sml[d][:, 6:7])
                            t_dm = P.tt(sml[d][:, 0:1], ps[bA][:, 128:129], sml[d][:, 6:7], ALU.max, waits=[t_dm0f])
                            t_dmf = P.fence("dve", sml[d][:, 0:1])
                            reg_free[("den", d)] = [t_dm]
                            t_r = P.recip(sml[d][:, 1:2], sml[d][:, 0:1], waits=[t_dmf])
                            t_r = P.fence("dve", sml[d][:, 1:2])
                            if debug:
                                t_cpd = P.cp(sml[d][:, 7:8], ps[bA][:, 128:129], waits=[t_den, t_r])
                                t_dbg = P.dma("sp", f"dbgden{d}", dbgden[h, d, i], sml[d][:, 0:8], waits=[t_cpd, t_r])
                                reg_free[("den", d)] = [t_dm, t_dbg]
                            if d == 0:
                                t_h = P.act(hF[:, i, :], ps[bB][:, :], AF.Copy, scale=sml[d][:, 1:2], waits=[t_in, t_r] + last_use)
                                reg_free[("in", d)] = [t_h]
                                hf_tok = t_h
                            else:
                                shs, hs_t, frh = HSr.next()
                                t_hs = P.stt(hs_t[:], ps[bB][:, :], sml[d][:, 1:2], hF[:, i, :], ALU.mult, ALU.add,
                                             waits=[t_in, t_r, P.tok("act")] + frh)
                                reg_free[("in", d)] = [t_hs]
                                t_sq = P.act(junk[:], hs_t[:], AF.Square, accum_out=sml[d][:, 2:3], waits=[t_hs])
                                t_sq = P.fence("act", sml[d][:, 2:3])
                                t_v = P.ts(sml[d][:, 3:4], sml[d][:, 2:3], 1.0 / 512.0, ALU.mult, EPS, ALU.add, waits=[t_sq])
                                t_v = P.fence("dve", sml[d][:, 3:4])
                                t_q = P.act(sml[d][:, 4:5], sml[d][:, 3:4], AF.Sqrt, waits=[t_v])
                                t_q = P.fence("act", sml[d][:, 4:5])
                                t_rs = P.recip(sml[d][:, 5:6], sml[d][:, 4:5], waits=[t_q])
                                t_rs = P.fence("dve", sml[d][:, 5:6])
                                shm, hm_t, frm = HMr.next()
                                t_hm = P.stt(hm_t[:], hs_t[:], sml[d][:, 5:6], GH[s][:, i, :], ALU.mult, ALU.mult, waits=[t_rs] + frm)
                                HSr.release(shs, t_hm)
                                last_use = [t_hm]
                                fr = rtake(("tr", d))
                                pb = ps[bA][:].bitcast(BF16)
                                for vc in range(4):
                                    t_tr = P.tr(pb[:, 512 + vc * 128:512 + (vc + 1) * 128], hm_t[:, vc * 128:(vc + 1) * 128], ident_b,
                                                waits=[t_hm] + fr if vc == 0 else [], inc=(vc == 3))
                                HMr.release(shm, t_tr)
                                t_ev = P.cp(hmTs[s][:, :, c0:c0 + 128], pb[:, 512:1024].rearrange("p (a b) -> p a b", a=4),
                                            waits=[t_tr] + hmTs_free[0])
                                hmTs_free[0] = []
                                reg_free[("tr", d)] = [t_ev]
                                hm_ev = t_ev
                        if not lastc:
                            fr = rtake(("upd", d))
                            for dc in range(2):
                                bU = bC if dc == 0 else bD
                                t_u = P.mm(ps[bU][:, :], K_tm[:, dc * 128:(dc + 1) * 128], vp_t[:], waits=[t_vp] + base_w + fr)
                            fr2 = rtake(("nu", d))
                            for dc in range(2):
                                t_nu = P.mm(ps[bA][:, 130 + 2 * dc:131 + 2 * dc], K_tm[:, dc * 128:(dc + 1) * 128], wcolb, waits=fr2 + base_w)
                            vp_users.append(t_u)
                            toks = []
                            if first:
                                toks.append(P.act(C32[d][:, 0, :], ps[bC][:, :], AF.Copy, waits=[t_u] + cb_read[d]))
                                toks.append(P.cp(C32[d][:, 1, :], ps[bD][:, :], waits=[t_u] + cb_read[d]))
                                toks.append(P.cp(n32[d][:].rearrange("p (a b) -> p a b", b=1), ps[bA][:, 130:134].rearrange("p (a b) -> p a b", b=2)[:, :, 0:1], waits=[t_nu]))
                            else:
                                wst = state_tok[d] + cb_read[d]
                                toks.append(P.stt(C32[d][:, 0, :], C32[d][:, 0, :], dcol, ps[bC][:, :], ALU.mult, ALU.add, waits=[t_u] + wst))
                                toks.append(P.stt(C32[d][:, 1, :], C32[d][:, 1, :], dcol, ps[bD][:, :], ALU.mult, ALU.add, waits=[t_u] + wst))
                                toks.append(P.stt(n32[d][:].rearrange("p (a b) -> p a b", b=1), n32[d][:].rearrange("p (a b) -> p a b", b=1), dcol,
                                                  ps[bA][:, 130:134].rearrange("p (a b) -> p a b", b=2)[:, :, 0:1], ALU.mult, ALU.add, waits=[t_nu] + wst))
                            cb_read[d] = []
                            state_tok[d] = toks
                            reg_free[("upd", d)] = [toks[0], toks[1]]
                            reg_free[("nu", d)] = [toks[2]]
                        for tkn in vp_users:
                            VPr[d].release(sv, tkn)
                td = P.dma("sp", f"sthm{s}", hmT[h * 512:(h + 1) * 512, :].rearrange("(vc p) t -> p vc t", p=128), hmTs[s][:], waits=[hm_ev])
                hmTs_free[0] = [td]
                hbuf_free[s] = [P.tok("pe"), P.tok("act"), P.tok("dve")]
                single_free[0] = [P.tok("pe"), P.tok("act"), P.tok("dve")]
                ld_tok = nxt_tok
            for b in range(8):
                psfree[b] = [P.tok("act"), P.tok("dve")]
            P.barrier()

        if stop_after == "M":
            P.emit([P.tok(k) for k in P.dma_keys if k.startswith("st")])
            return nc
        with contextlib.ExitStack() as sc:
            NK = T + 384
            KaS = [alloc(sc, f"a_K{i}", [128, NK], BF16) for i in range(2)]
            VaS = [alloc(sc, f"a_V{i}", [128, 11, 128], BF16) for i in range(2)]
            QaS = [alloc(sc, f"a_Q{i}", [128, 4, T], BF16) for i in range(2)]
            OgS = [alloc(sc, f"a_O{i}", [128, 4, T], BF16) for i in range(2)]
            og_free = [[], []]
            abuf_free = [[], []]
            es_bc = alloc(sc, "a_es", [128, 32])
            es_all = alloc(sc, "a_esall", [128, 8, 512])
            PTr = Ring([alloc(sc, f"a_pt{i}", [128, 512], BF16) for i in range(6)], "pt")
            DNr = Ring([alloc(sc, f"a_dn{i}", [128, 512]) for i in range(2)], "dn")
            t_es = P.dma("sp", "a_c", es_bc[:], sink.partition_broadcast(128))
            t_ex = P.act(es_bc[:], es_bc[:], AF.Exp, waits=[t_es])
            t_esa = None
            for hd in range(32):
                t_esa = P.ts(es_all[:, hd // 4, (hd % 4) * 128:(hd % 4 + 1) * 128], ones_f, es_bc[:, hd:hd + 1], ALU.mult, waits=[t_ex] + CONST)

            def load_g(g):
                s = g % 2
                k = f"ald{s}"
                P.dma("sp", k, KaS[s][:], KaT[g * 128:(g + 1) * 128, :], waits=abuf_free[s])
                P.dma("sp", k, VaS[s][:], Va[:, g * 128:(g + 1) * 128].rearrange("(tt p) d -> p tt d", p=128))
                P.dma("sp", k, QaS[s][:], QaT[g * 512:(g + 1) * 512, :].rearrange("(hq p) t -> p hq t", p=128))
                abuf_free[s] = []
                return P.tok(k)
            jobs = []
            for g in range(8):
                for n in range(8):
                    keys = []
                    if n >= 1:
                        keys.append((n - 1, maskL4))
                    keys.append((n, None))
                    keys.append((n + 1 if n < 7 else 8, maskU4))
                    keys.append((9, None))
                    keys.append((10, None))
                    for j, (kt, mask) in enumerate(keys):
                        jobs.append((g, n, j, len(keys), kt, mask))
            ld_tok = {0: load_g(0)}
            st = {}
            blk_state = {}
            blk_ctr = [0]

            def emit_S(i):
                g, n, j, nk, kt, mask = jobs[i]
                s = g % 2
                if j == 0 and n == 1 and g + 1 < 8:
                    ld_tok[g + 1] = load_g(g + 1)
                b = i % 4
                fr = ps_take(b)
                t_s = P.mm(ps[b][:, :].rearrange("p (a b) -> p a b", a=4), KaS[s][:, kt * 128:(kt + 1) * 128],
                           QaS[s][:, :, n * 128:(n + 1) * 128], waits=[ld_tok[g]] + fr)
                sp_, pt, frp = PTr.next()
                t_p = P.act(pt[:], ps[b][:, :], AF.Exp, waits=[t_s] + frp)
                psfree[b].append(t_p)
                if mask is not None:
                    t_p = P.tt(pt[:], pt[:], mask, ALU.mult, waits=[t_p] + CONST)
                st[i] = (t_p, pt, sp_)

            def emit_PV(i):
                g, n, j, nk, kt, mask = jobs[i]
                s = g % 2
                t_p, pt, sp_ = st.pop(i)
                if j == 0:
                    bo = 4 + (blk_ctr[0] % 2)
                    bd = 6 + (blk_ctr[0] % 2)
                    blk_ctr[0] += 1
                    blk_state[(g, n)] = (bo, bd, ps_take(bo), ps_take(bd))
                bo, bd, fro, frd = blk_state[(g, n)]
                t_o = P.mm(ps[bo][:, :], VaS[s][:, kt, :], pt[:], start=(j == 0), stop=(j == nk - 1),
                           waits=[t_p] + (fro if j == 0 else []))
                t_d = P.mm(ps[bd][:, :], ones_b, pt[:], start=(j == 0), stop=(j == nk - 1),
                           waits=(frd + CONST if j == 0 else []))
                PTr.release(sp_, t_d)
                if j == nk - 1:
                    sd, dn, frn = DNr.next()
                    t1_ = P.tt(dn[:], ps[bd][:, :], es_all[:, g, :], ALU.add, waits=[t_d, t_esa] + frn)
                    psfree[bd].append(t1_)
                    t2_ = P.act(dn[:], dn[:], AF.Ln, waits=[t1_])
                    t3_ = P.act(dn[:], dn[:], AF.Exp, scale=-1.0, waits=[t2_])
                    t4_ = P.tt(OgS[s][:, :, n * 128:(n + 1) * 128], ps[bo][:, :].rearrange("p (a b) -> p a b", a=4),
                               dn[:].rearrange("p (a b) -> p a b", a=4), ALU.mult, waits=[t_o, t3_] + og_free[s])
                    og_free[s] = []
                    psfree[bo].append(t4_)
                    DNr.release(sd, t4_)
                    if n == 7:
                        td = P.dma("sp", f"stha{s}", haT[g * 512:(g + 1) * 512, :].rearrange("(hq p) t -> p hq t", p=128), OgS[s][:], waits=[t4_])
                        og_free[s] = [td]
                        abuf_free[s] = [P.tok("pe")]
            LOOK = 2
            for i in range(min(LOOK, len(jobs))):
                emit_S(i)
            for i in range(len(jobs)):
                emit_PV(i)
                if i + LOOK < len(jobs):
                    emit_S(i + LOOK)
            P.barrier()

        if stop_after == "A":
            P.emit([P.tok(k) for k in P.dma_keys if k.startswith("st")])
            return nc

        segs2 = [[(0, 512, 0)], [(512, 512, 0)]]
        with contextlib.ExitStack() as sc:
            actM = alloc(sc, "y1_act", [128, KC, T], BF16)
            t_ld = P.dma("sp", "y1ld", actM[:], hmT.rearrange("(kc p) t -> p kc t", p=128))
            sgr = Ring([alloc(sc, f"y1_sg{i}", [128, 512]) for i in range(3)], "y1sg")
            stF = Stage(sc, "y1f", [128, 512], F32, 3)

            def ev_y1(b, ch, si, tok):
                ss_, sg, frs = sgr.next()
                tl_ = P.dma("sp", f"y1sg{ss_}", sg[:], sgmT[ch * 128:(ch + 1) * 128, si * 512:(si + 1) * 512], waits=frs)
                s_, ob, fr = stF.get()
                t = P.tt(ob[:], ps[b][:, :], sg[:], ALU.mult, waits=[tok, tl_] + fr)
                sgr.release(ss_, t)
                stF.store(s_, t1T[ch * 128:(ch + 1) * 128, si * 512:(si + 1) * 512], ob[:], t, "stT1")
                return [t]
            gemm_fm(w_out_m, 0, D, actM, segs2, [t_ld], ev_y1)
            P.barrier()

        with contextlib.ExitStack() as sc1:
            mixsT = alloc(sc1, "mixsT", [128, KC, T], BF16)
            with contextlib.ExitStack() as sc:
                actA = alloc(sc, "y2_act", [128, KC, T], BF16)
                t_ld = P.dma("sp", "y2ld", actA[:], haT.rearrange("(kc p) t -> p kc t", p=128))
                sgr = Ring([alloc(sc, f"y2_sg{i}", [128, 512]) for i in range(3)], "y2sg")
                t1r = Ring([alloc(sc, f"y2_t1{i}", [128, 512]) for i in range(3)], "y2t1")
                tmr = Ring([alloc(sc, f"y2_tm{i}", [128, 512]) for i in range(2)], "y2tm")

                def ev_y2(b, ch, si, tok):
                    ss_, sg, frs = sgr.next()
                    tl1 = P.dma("sp", f"y2sg{ss_}", sg[:], sgaT[ch * 128:(ch + 1) * 128, si * 512:(si + 1) * 512], waits=frs)
                    st_, t1b, frt = t1r.next()
                    tl2 = P.dma("sp", f"y2t1{st_}", t1b[:], t1T[ch * 128:(ch + 1) * 128, si * 512:(si + 1) * 512], waits=frt)
                    sm_, tm, frm = tmr.next()
                    ta = P.tt(tm[:], ps[b][:, :], sg[:], ALU.mult, waits=[tok, tl1] + frm)
                    sgr.release(ss_, ta)
                    tb = P.tt(mixsT[:, ch, si * 512:(si + 1) * 512], tm[:], t1b[:], ALU.add, waits=[ta, tl2])
                    t1r.release(st_, tb)
                    tmr.release(sm_, tb)
                    return [ta]
                gemm_fm(w_out_a, 0, D, actA, segs2, [t_ld], ev_y2)
            P.barrier()
            with contextlib.ExitStack() as sc:
                stF = Stage(sc, "o_f", [128, 512], F32, 4)

                def ev_o(b, c0, nb, tt, tok):
                    s_, ob, fr = stF.get()
                    if tt % 2 == 0:
                        t = P.act(ob[:, 0:nb], ps[b][:, 0:nb], AF.Copy, waits=[tok] + fr)
                    else:
                        t = P.cp(ob[:, 0:nb], ps[b][:, 0:nb], waits=[tok] + fr)
                    stF.store(s_, mixd[tt * 128:(tt + 1) * 128, c0:c0 + nb], ob[:, 0:nb], t, "stMix")
                    return [t]
                gemm_tm(w_o, 0, D, mixsT, range(NT), [P.tok("dve")], ev_o)
            P.barrier()

        if stop_after == "O":
            P.emit([P.tok(k) for k in P.dma_keys if k.startswith("st")])
            return nc

        def load_bc(tile, row_ap, waits=(), key="bcl"):
            return P.dma("sp", key, tile[:], row_ap.partition_broadcast(128), waits=list(waits))

        with contextlib.ExitStack() as sc1:
            act2 = alloc(sc1, "act2", [128, KC, T], BF16)
            with contextlib.ExitStack() as sc:
                g1n = alloc(sc, "r1_g1n", [128, D])
                a2 = alloc(sc, "r1_a2", [128, D])
                sh2 = alloc(sc, "r1_sh2", [128, D])
                mt = [alloc(sc, "r1_m0", [128, D])] * 2
                xt = [alloc(sc, "r1_x0", [128, D])] * 2
                ub = alloc(sc, "r1_ub", [128, D], BF16)
                sq = alloc(sc, "r1_sq", [128, 8])
                ta = load_bc(g1n, modrow[0, 2 * D:3 * D], [T_MOD])
                tb = load_bc(mt[0], norm_g[1, :])
                t_g1n = P.tt(g1n[:], g1n[:], mt[0][:], ALU.mult, waits=[ta, tb])
                ta = load_bc(a2, modrow[0, 4 * D:5 * D])
                tb = load_bc(xt[0], norm_g[2, :])
                t_a2 = P.stt(a2[:], a2[:], 1.0, xt[0][:], ALU.add, ALU.mult, waits=[ta, tb])
                t_sh2 = load_bc(sh2, modrow[0, 3 * D:4 * D])
                mfree = [[t_g1n], [t_g1n]]
                xfree = [[t_a2], [t_a2]]
                ub_free = []
                for tt in range(NT):
                    s = 0
                    tlm = P.dma("sp", f"r1m{s}", mt[s][:], mixd[tt * 128:(tt + 1) * 128, :], waits=mfree[s])
                    tlx = P.dma("sp", f"r1x{s}", xt[s][:], x_own[tt * 128:(tt + 1) * 128, :], waits=xfree[s])
                    tsq = P.act(ub[:], mt[s][:], AF.Square, accum_out=sq[:, 0:1], waits=[tlm] + ub_free)
                    tsq = P.fence("act", sq[:, 0:1])
                    tv = P.ts(sq[:, 1:2], sq[:, 0:1], 1.0 / D, ALU.mult, EPS, ALU.add, waits=[tsq])
                    tv = P.fence("dve", sq[:, 1:2])
                    tq = P.act(sq[:, 2:3], sq[:, 1:2], AF.Sqrt, waits=[tv])
                    tq = P.fence("act", sq[:, 2:3])
                    tr_ = P.recip(sq[:, 3:4], sq[:, 2:3], waits=[tq])
                    tr_ = P.fence("dve", sq[:, 3:4])
                    tm1 = P.stt(mt[s][:], mt[s][:], sq[:, 3:4], g1n[:], ALU.mult, ALU.mult, waits=[tr_, t_g1n])
                    th = P.tt(xt[s][:], xt[s][:], mt[s][:], ALU.add, waits=[tm1, tlx])
                    tst = P.dma("sp", "stH", hbuf[tt * 128:(tt + 1) * 128, :], xt[s][:], waits=[th])
                    tsq2 = P.act(ub[:], xt[s][:], AF.Square, accum_out=sq[:, 4:5], waits=[th])
                    tsq2 = P.fence("act", sq[:, 4:5])
                    tv2 = P.ts(sq[:, 5:6], sq[:, 4:5], 1.0 / D, ALU.mult, EPS, ALU.add, waits=[tsq2])
                    tv2 = P.fence("dve", sq[:, 5:6])
                    tq2 = P.act(sq[:, 6:7], sq[:, 5:6], AF.Sqrt, waits=[tv2])
                    tq2 = P.fence("act", sq[:, 6:7])
                    tr2 = P.recip(sq[:, 7:8], sq[:, 6:7], waits=[tq2])
                    tr2 = P.fence("dve", sq[:, 7:8])
                    tu1 = P.stt(mt[s][:], xt[s][:], sq[:, 7:8], a2[:], ALU.mult, ALU.mult, waits=[tr2, t_a2, tm1])
                    tu = P.tt(ub[:], mt[s][:], sh2[:], ALU.add, waits=[tu1, t_sh2, tsq2])
                    mfree[s] = [tu]
                    xfree[s] = [tst, tu1]
                    for q4 in range(4):
                        b, fr = next_bank()
                        pb = ps[b][:].bitcast(BF16)
                        for i in range(8):
                            kc = q4 * 8 + i
                            tp = P.tr(pb[:, i * 128:(i + 1) * 128], ub[:, kc * 128:(kc + 1) * 128], ident_b,
                                      waits=[tu] + fr + CONST if i == 0 else [], inc=(i == 7))
                        dst = act2[:, q4 * 8:(q4 + 1) * 8, tt * 128:(tt + 1) * 128]
                        srcp = pb.rearrange("p (a b) -> p a b", a=8)
                        te = P.act(dst, srcp, AF.Copy, waits=[tp]) if q4 % 2 == 0 else P.cp(dst, srcp, waits=[tp])
                        psfree[b].append(te)
                    ub_free = [tp]
            P.barrier()

            with contextlib.ExitStack() as sc:
                stB = Stage(sc, "f1b", [128, 512], BF16, 4)
                tmr = Ring([alloc(sc, f"f1_t{i}", [128, 512]) for i in range(3)], "f1t")

                def ev_f1(b, ch, si, tok):
                    sm_, tm, frm = tmr.next()
                    ta = P.act(tm[:], ps[b][:, :], AF.Relu, waits=[tok] + frm)
                    s_, ob, fr = stB.get()
                    tb = P.tt(ob[:], tm[:], tm[:], ALU.mult, waits=[ta] + fr)
                    tmr.release(sm_, tb)
                    stB.store(s_, h1T[ch * 128:(ch + 1) * 128, si * 512:(si + 1) * 512], ob[:], tb, "stH1")
                    return [ta]
                gemm_fm(w_ff1, 0, DFF, act2, segs2, [], ev_f1)
            P.barrier()

        with contextlib.ExitStack() as sc:
            acc = alloc(sc, "f2_acc", [128, NT, 2048])
            h1g = [alloc(sc, f"f2_h{i}", [128, 16, T], BF16) for i in range(2)]
            hfree = [[], []]
            gi = 0
            acc_tok = {}
            acc_rd = []
            for nh in range(2):
                for fg in range(8):
                    s = gi % 2
                    gi += 1
                    tlh = P.dma("sp", f"f2h{s}", h1g[s][:], h1T[fg * 2048:(fg + 1) * 2048, :].rearrange("(fc p) t -> p fc t", p=128),
                                waits=hfree[s] + [P.tok("stH1")])
                    tmm = None
                    for nbk in range(4):
                        c0 = nh * 2048 + nbk * 512
                        sw, wt, tw = WS.fetch(w_ff2[fg * 2048:(fg + 1) * 2048, c0:c0 + 512].rearrange("(fc p) n -> p fc n", p=128),
                                              lambda t: t[:].rearrange("p a b -> p (a b)").rearrange("p (c d) -> p c d", c=16))
                        wv = wt[:].rearrange("p a b -> p (a b)").rearrange("p (c d) -> p c d", c=16)
                        for tt in range(NT):
                            b, fr = next_bank()
                            for fc in range(16):
                                tmm = P.mm(ps[b][:, :], h1g[s][:, fc, tt * 128:(tt + 1) * 128], wv[:, fc, :], start=(fc == 0), stop=(fc == 15),
                                           waits=([tw, tlh] + fr) if fc == 0 else [], inc=(fc == 15))
                            dst = acc[:, tt, nbk * 512:(nbk + 1) * 512]
                            if fg == 0:
                                if tt % 2 == 0:
                                    t = P.act(dst, ps[b][:, :], AF.Copy, waits=[tmm] + acc_rd)
                                else:
                                    t = P.cp(dst, ps[b][:, :], waits=[tmm] + acc_rd)
                            else:
                                t = P.tt(dst, ps[b][:, :], dst, ALU.add, waits=[tmm, acc_tok[(tt, nbk)]])
                            acc_tok[(tt, nbk)] = t
                            psfree[b].append(t)
                        WS.ring.release(sw, tmm)
                    hfree[s] = [tmm]
                acc_rd = []
                for tt in range(NT):
                    td = P.dma("sp", "stFfn", ffn[tt * 128:(tt + 1) * 128, nh * 2048:(nh + 1) * 2048], acc[:, tt, :],
                               waits=[acc_tok[(tt, k)] for k in range(4)])
                    acc_rd.append(td)
            P.barrier()

        if stop_after == "F2":
            P.emit([P.tok(k) for k in P.dma_keys if k.startswith("st")])
            return nc

        with contextlib.ExitStack() as sc:
            g2n = alloc(sc, "r2_g2n", [128, D])
            ft = [alloc(sc, f"r2_f{i}", [128, D]) for i in range(2)]
            ht = [alloc(sc, f"r2_h{i}", [128, D]) for i in range(2)]
            ub = alloc(sc, "r2_ub", [128, D], BF16)
            sq = alloc(sc, "r2_sq", [128, 4])
            ta = load_bc(g2n, modrow[0, 5 * D:6 * D], [T_MOD], key="bcl2")
            tb = load_bc(ft[0], norm_g[3, :], key="bcl2")
            t_g2n = P.tt(g2n[:], g2n[:], ft[0][:], ALU.mult, waits=[ta, tb])
            ffree = [[t_g2n], []]
            hfree2 = [[], []]
            ub_free = []
            for tt in range(NT):
                s = tt % 2
                tlf = P.dma("sp", f"r2f{s}", ft[s][:], ffn[tt * 128:(tt + 1) * 128, :], waits=ffree[s])
                tlh = P.dma("sp", f"r2h{s}", ht[s][:], hbuf[tt * 128:(tt + 1) * 128, :], waits=hfree2[s] + [P.tok("stH")])
                tsq = P.act(ub[:], ft[s][:], AF.Square, accum_out=sq[:, 0:1], waits=[tlf])
                tsq = P.fence("act", sq[:, 0:1])
                tv = P.ts(sq[:, 1:2], sq[:, 0:1], 1.0 / D, ALU.mult, EPS, ALU.add, waits=[tsq])
                tv = P.fence("dve", sq[:, 1:2])
                tq = P.act(sq[:, 2:3], sq[:, 1:2], AF.Sqrt, waits=[tv])
                tq = P.fence("act", sq[:, 2:3])
                tr_ = P.recip(sq[:, 3:4], sq[:, 2:3], waits=[tq])
                tr_ = P.fence("dve", sq[:, 3:4])
                tm1 = P.stt(ft[s][:], ft[s][:], sq[:, 3:4], g2n[:], ALU.mult, ALU.mult, waits=[tr_, t_g2n])
                th = P.tt(ht[s][:], ht[s][:], ft[s][:], ALU.add, waits=[tm1, tlh])
                tst = P.dma("sp", f"stOut{s}", out[tt * 128:(tt + 1) * 128, :], ht[s][:], waits=[th])
                ffree[s] = [th]
                hfree2[s] = [tst]
        P.emit([P.tok("stOut0"), P.tok("stOut1")])
        return nc


def _consts():
    idx = np.arange(128)
    ident = np.eye(128, dtype=np.float32)
    triU = (idx[:, None] <= idx[None, :]).astype(np.float32)
    triL = (idx[:, None] >= idx[None, :]).astype(np.float32)
    ones = np.ones((128, 128), np.float32)
    R = np.zeros((128, 128), np.float32)
    for m in range(128):
        if (m % 64) < 32:
            R[m, m + 32] = -1.0
        else:
            R[m, m - 32] = 1.0
    RT = np.ascontiguousarray(R.T)
    cf = np.concatenate([ident, triU, triL, ones, RT], axis=1)
    bf = ml_dtypes.bfloat16
    maskU4 = np.tile(triU, (1, 4))
    maskL4 = np.tile(triL, (1, 4))
    cb = np.concatenate([ident, ones, maskU4, maskL4], axis=1).astype(bf)
    return cf, cb


def _rope_tables(pos):
    nf = 32
    freqs = (10000.0 ** (-np.arange(nf, dtype=np.float32) / nf)).astype(np.float32)
    row = (pos // 64).astype(np.float32)
    col = (pos % 64).astype(np.float32)
    ang_r = row[None, :] * freqs[:, None]
    ang_c = col[None, :] * freqs[:, None]
    ang = np.concatenate([ang_r, ang_r, ang_c, ang_c], axis=0)
    return np.stack([np.cos(ang), np.sin(ang)], axis=1).astype(np.float32)


def prep_inputs(inp, cores=range(8)):
    f = lambda a: np.ascontiguousarray(np.asarray(a, dtype=np.float32))
    x, c, ctx, c_ctx = f(inp["x"]), f(inp["c"]), f(inp["ctx"]), f(inp["c_ctx"])
    w_in = f(inp["w_in"])[0]
    shared = {
        "w_mod": f(inp["w_mod"])[0], "b_mod": f(inp["b_mod"])[0], "norm_g": f(inp["norm_g"])[0],
        "w_in": w_in, "m_norm_g": f(inp["m_norm_g"])[0], "sink": f(inp["attn_sink"])[0],
        "w_out_m": f(inp["w_out_m"])[0], "w_out_a": f(inp["w_out_a"])[0], "w_o": f(inp["w_o"])[0],
        "w_ff1": f(inp["w_ff1"])[0], "w_ff2": f(inp["w_ff2"])[0],
    }
    cf, cb = _consts()
    shared["cf"] = cf
    shared["cb"] = cb
    wg = w_in[:, O_GM:O_GM + 32]
    gb = f(inp["m_gate_b"])[0].reshape(32)
    swap = np.concatenate([np.arange(16, 32), np.arange(0, 16)])
    maps = []
    for core in cores:
        b, hh = core // 2, core % 2
        loc = np.arange(2048) if hh == 0 else (2047 - np.arange(2048))
        xb = x[b][loc]
        cx = ctx[b] if hh == 0 else ctx[b][::-1]
        m = dict(shared)
        m["x_own"] = np.ascontiguousarray(xb[:T])
        m["x_pre"] = np.ascontiguousarray(np.concatenate([xb[T:], cx], axis=0))
        m["cvec"] = np.ascontiguousarray(np.stack([c[b], c_ctx], axis=0).reshape(2, KC, 128).transpose(2, 1, 0))
        if hh == 0:
            m["w_gate"] = np.ascontiguousarray(wg)
            m["gate_b"] = np.ascontiguousarray(gb)
        else:
            m["w_gate"] = np.ascontiguousarray(wg[:, swap])
            m["gate_b"] = np.ascontiguousarray(gb[swap])
        m["ropec"] = np.ascontiguousarray(_rope_tables(loc[:T + 128]))
        maps.append(m)
    return maps


def kernel(**inputs):
    nc = build_program()
    maps = prep_inputs(inputs)
    res = run_bass_kernel_spmd(nc, maps, core_ids=list(range(8)))
    outp = np.empty((4, 2048, D), np.float32)
    for core in range(8):
        b, hh = core // 2, core % 2
        o = res.results[core]["out"]
        if hh == 0:
            outp[b, :T] = o
        else:
            outp[b, T:] = o[::-1]
    return outp
```

```python
import contextlib
import numpy as np
import ml_dtypes
import concourse.bass as bass
import concourse.mybir as mybir
from concourse.bass_utils import run_bass_kernel_spmd

F32 = mybir.dt.float32
BF16 = mybir.dt.bfloat16
AF = mybir.ActivationFunctionType
ALU = mybir.AluOpType
AX = mybir.AxisListType

ENGS = ("pe", "act", "dve", "pool", "sp")

D = 4096
KC = 32
T = 1024
NT = 8
TP = 1280
NTP = 10
DFF = 16384
EPS = 1e-6
O_QM, O_KM, O_VM, O_OM, O_GM, O_QA, O_KA, O_VA, O_BM, O_BA, O_END = (
    0, 2048, 4096, 8192, 12288, 12320, 16416, 17440, 18464, 22560, 26656)


class Prog:
    def __init__(self, nc):
        self.nc = nc
        self.q = {e: [] for e in ENGS}
        self.cnt = {}
        self.waited = {e: {} for e in ENGS}
        self.sems = {}
        self.dma_keys = []

    def _waits(self, eng, waits):
        out = []
        w = self.waited[eng]
        for t in waits:
            if t is None:
                continue
            k, v = t
            if v <= 0:
                continue
            if w.get(k, 0) >= v:
                continue
            w[k] = v
            out.append((k, v))
        return out

    def op(self, eng, fn, waits=(), inc=True):
        ws = self._waits(eng, waits)
        if inc:
            self.cnt[eng] = self.cnt.get(eng, 0) + 1
        self.q[eng].append((ws, fn, eng if inc else None, 1))
        return (eng, self.cnt.get(eng, 0))

    def dma(self, eng, semkey, out, in_, waits=()):
        ws = self._waits(eng, waits)
        if semkey not in self.cnt:
            self.cnt[semkey] = 0
            self.dma_keys.append(semkey)
        self.cnt[semkey] += 16

        def fn(e, out=out, in_=in_):
            return e.dma_start(out=out, in_=in_)
        self.q[eng].append((ws, fn, semkey, 16))
        return (semkey, self.cnt[semkey])


    def mm(self, out, lhsT, rhs, start=True, stop=True, waits=(), inc=True):
        return self.op("pe", lambda e: e.matmul(out, lhsT=lhsT, rhs=rhs, start=start, stop=stop), waits, inc)

    def tr(self, out, in_, ident, waits=(), inc=True):
        return self.op("pe", lambda e: e.transpose(out, in_, ident), waits, inc)

    def act(self, out, in_, func, scale=1.0, bias=0.0, accum_out=None, waits=(), inc=True):
        def fn(e):
            kw = {}
            if accum_out is not None:
                kw["accum_out"] = accum_out
            return e.activation(out=out, in_=in_, func=func, bias=bias, scale=scale, **kw)
        return self.op("act", fn, waits, inc)

    def tt(self, out, in0, in1, op, waits=(), eng="dve", inc=True):
        return self.op(eng, lambda e: e.tensor_tensor(out=out, in0=in0, in1=in1, op=op), waits, inc)

    def ts(self, out, in0, s1, op0, s2=None, op1=None, waits=(), eng="dve", inc=True, accum_out=None):
        def fn(e):
            kw = {}
            if op1 is not None:
                kw["op1"] = op1
            if accum_out is not None:
                kw["accum_out"] = accum_out
            return e.tensor_scalar(out=out, in0=in0, scalar1=s1, scalar2=s2, op0=op0, **kw)
        return self.op(eng, fn, waits, inc)

    def stt(self, out, in0, scalar, in1, op0, op1, waits=(), inc=True):
        return self.op("dve", lambda e: e.scalar_tensor_tensor(out=out, in0=in0, scalar=scalar, in1=in1, op0=op0, op1=op1), waits, inc)

    def cp(self, out, in_, waits=(), eng="dve", inc=True):
        return self.op(eng, lambda e: e.tensor_copy(out=out, in_=in_), waits, inc)

    def recip(self, out, in_, waits=(), inc=True):
        return self.op("dve", lambda e: e.reciprocal(out=out, in_=in_), waits, inc)

    def fence(self, eng, src):
        dst = self.fscr[eng]
        if eng == "act":
            return self.op("act", lambda e: e.activation(out=dst, in_=src, func=AF.Copy, bias=0.0, scale=1.0))
        return self.op(eng, lambda e: e.tensor_copy(out=dst, in_=src))

    def barrier(self):
        toks = [(k, v) for k, v in self.cnt.items()]
        for e in ENGS:
            ws = self._waits(e, toks)
            if ws:
                self.q[e].append((ws, None, None, 0))

    def tok(self, key):
        return (key, self.cnt.get(key, 0))

    def emit(self, final_waits):
        nc = self.nc
        keys = list(ENGS) + self.dma_keys
        with contextlib.ExitStack() as st:
            for k in keys:
                self.sems[k] = st.enter_context(nc.semaphore("s_" + k))
            block = st.enter_context(nc.Block())
            engmap = {"pe": block.tensor, "act": block.scalar, "dve": block.vector,
                      "pool": block.gpsimd, "sp": block.sync}
            self.q["sp"].append((self._waits("sp", final_waits), None, None, 0))

            def mk(ename):
                def body(e):
                    for ws, fn, sk, n in self.q[ename]:
                        for (k, v) in ws:
                            e.wait_ge(self.sems[k], v)
                        if fn is None:
                            continue
                        ins = fn(e)
                        if sk is not None:
                            ins.then_inc(self.sems[sk], n)
                return body
            for ename in ENGS:
                engmap[ename](mk(ename))


class Ring:
    def __init__(self, tiles, name):
        self.tiles = tiles
        self.free = [[] for _ in tiles]
        self.i = 0
        self.name = name

    def next(self):
        s = self.i % len(self.tiles)
        self.i += 1
        fr = self.free[s]
        self.free[s] = []
        return s, self.tiles[s], fr

    def release(self, s, tok):
        self.free[s].append(tok)


class WStream:
    def __init__(self, P, ring):
        self.P = P
        self.ring = ring
        self.pending = []

    def fetch(self, src_ap, view):
        s, t, fr = self.ring.next()
        tok = self.P.dma("pool", f"{self.ring.name}{s}", view(t), src_ap, waits=fr)
        return s, t, tok


def build_program(debug=False, stop_after=None):
    nc = bass.Bass("TRN2", target_bir_lowering=False)
    kind_s = "ExternalOutput" if debug else "Internal"

    early = stop_after in ("S1", "G1", "G2", "M", "A")
    skip_names = ("w_out_m", "w_out_a", "w_o", "w_ff1", "w_ff2") if early else ()

    def din(name, shape, dt=F32):
        if name in skip_names:
            nc.dram_tensor(name + "_dummy", [128, 128], dt, kind="ExternalInput")
            return None
        return nc.dram_tensor(name, list(shape), dt, kind="ExternalInput").ap()

    def dscr(name, shape, dt=F32):
        return nc.dram_tensor(name, list(shape), dt, kind=kind_s).ap()

    x_own = din("x_own", [T, D])
    x_pre = din("x_pre", [TP, D])
    cvec = din("cvec", [128, KC, 2])
    w_mod = din("w_mod", [D, 6 * D])
    b_mod = din("b_mod", [6 * D])
    norm_g = din("norm_g", [4, D])
    w_in = din("w_in", [D, O_END])
    w_gate = din("w_gate", [D, 32])
    gate_b = din("gate_b", [32])
    m_norm_g = din("m_norm_g", [D])
    sink = din("sink", [32])
    w_out_m = din("w_out_m", [D, D])
    w_out_a = din("w_out_a", [D, D])
    w_o = din("w_o", [D, D])
    w_ff1 = din("w_ff1", [D, DFF])
    w_ff2 = din("w_ff2", [DFF, D])
    cf = din("cf", [128, 5 * 128])
    cb = din("cb", [128, 2 * 128 + 2 * 512], BF16)
    ropec = din("ropec", [128, 2, T + 128])

    out = nc.dram_tensor("out", [T, D], F32, kind="ExternalOutput").ap()

    modrow = dscr("modrow", [2, 6 * D])
    QmT = dscr("QmT", [2048, T], BF16)
    KmT = dscr("KmT", [2048, T], BF16)
    Km = dscr("Km", [T, 2048], BF16)
    Vm = dscr("Vm", [T, D], BF16)
    Gm = dscr("Gm", [T, D])
    QaT = dscr("QaT", [D, T], BF16)
    KaT = dscr("KaT", [1024, T + 384], BF16)
    Va = dscr("Va", [T + 384, 1024], BF16)
    sgmT = dscr("sgmT", [D, T])
    sgaT = dscr("sgaT", [D, T])
    Kp = dscr("Kp", [TP, 2048], BF16)
    Vp = dscr("Vp", [TP, D], BF16)
    hmT = dscr("hmT", [D, T], BF16)
    haT = dscr("haT", [D, T], BF16)
    t1T = dscr("t1T", [D, T])
    mixd = dscr("mixd", [T, D])
    hbuf = dscr("hbuf", [T, D])
    h1T = dscr("h1T", [DFF, T], BF16)
    ffn = dscr("ffn", [T, D])
    dbgden = dscr("dbgden", [8, 2, 8, 128, 8]) if debug else None

    P = Prog(nc)
    st = contextlib.ExitStack()
    with st:
        def sb(name, shape, dt=F32):
            return st.enter_context(nc.sbuf_tensor(name, list(shape), dt))

        ps = [st.enter_context(nc.psum_tensor(f"ps{i}", [128, 512], F32)) for i in range(8)]
        psfree = [[] for _ in range(8)]

        def ps_take(i):
            fr = psfree[i]
            psfree[i] = []
            return fr

        cf_t = sb("cf_t", [128, 640])
        cb_t = sb("cb_t", [128, 1280], BF16)
        t_cf = P.dma("sp", "c0", cf_t[:], cf)
        t_cb = P.dma("sp", "c1", cb_t[:], cb)
        ident_f = cf_t[:, 0:128]
        triU = cf_t[:, 128:256]
        triL = cf_t[:, 256:384]
        ones_f = cf_t[:, 384:512]
        RT = cf_t[:, 512:640]
        ident_b = cb_t[:, 0:128]
        ones_b = cb_t[:, 128:256]
        maskU4 = cb_t[:, 256:768]
        maskL4 = cb_t[:, 768:1280]
        CONST = [t_cf, t_cb]
        fscr_t = sb("fscr", [128, 8])
        P.fscr = {"act": fscr_t[:, 0:1], "dve": fscr_t[:, 2:3], "pool": fscr_t[:, 4:5]}

        wring = Ring([sb(f"wr{i}", [128, KC, 256], BF16) for i in range(3)], "w")
        WS = WStream(P, wring)

        def wblock(W, c0, ncols=256, r0=0):
            src = W[r0:r0 + D, c0:c0 + ncols].rearrange("(kc p) n -> p kc n", p=128)
            return WS.fetch(src, lambda t: t[:, :, 0:ncols])

        sc0 = contextlib.ExitStack()
        sb0 = lambda name, shape, dt=F32: sc0.enter_context(nc.sbuf_tensor(name, list(shape), dt))
        sc32 = sb0("sc32", [128, KC, 2])
        scb = sb0("scb", [128, KC, 2], BF16)
        sig = sb0("sig_tmp", [128, KC, 2])
        t_c = P.dma("sp", "c2", sc32[:], cvec)
        t_sg = P.op("act", lambda e: e.activation(out=sig[:], in_=sc32[:], func=AF.Sigmoid), waits=[t_c])
        t_scb = P.op("dve", lambda e: e.tensor_tensor(out=scb[:], in0=sc32[:], in1=sig[:], op=ALU.mult), waits=[t_sg])
        mrow = [sb0(f"mrow{i}", [2, 512]) for i in range(2)]
        brow = [sb0(f"brow{i}", [2, 512]) for i in range(2)]
        mrow_free = [[], []]
        nblk = 6 * D // 256
        for nb in range(nblk):
            s, wt, tw = wblock(w_mod, nb * 256)
            half = nb % 2
            j = (nb // 2) % 2
            pi = j
            if half == 0:
                fr = ps_take(pi)
                tb = P.dma("sp", f"brow{j}", brow[j][:], b_mod[nb * 256:nb * 256 + 512].partition_broadcast(2),
                           waits=mrow_free[j])
            for kc in range(KC):
                tmm = P.op("pe", lambda e, pi=pi, wt=wt, kc=kc, half=half: e.matmul(
                    ps[pi][0:2, half * 256:(half + 1) * 256], lhsT=scb[:, kc, :], rhs=wt[:, kc, 0:256],
                    start=(kc == 0), stop=(kc == KC - 1)),
                    waits=([t_scb, tw] + (fr if half == 0 else [])) if kc == 0 else [], inc=(kc == KC - 1))
            wring.release(s, tmm)
            if half == 1:
                n0 = (nb - 1) * 256
                ta = P.op("dve", lambda e, pi=pi, j=j: e.tensor_tensor(out=mrow[j][:], in0=ps[pi][0:2, :], in1=brow[j][:], op=ALU.add),
                          waits=[tmm, tb] + mrow_free[j])
                psfree[pi].append(ta)
                td = P.dma("sp", "modst", modrow[:, n0:n0 + 512], mrow[j][:], waits=[ta])
                mrow_free[j] = [td]
        T_MOD = P.tok("modst")
        P.barrier()
        sc0.close()

        if stop_after == "S1":
            P.emit([T_MOD])
            return nc


        P.barrier()

        def alloc(sc, name, shape, dt=F32):
            return sc.enter_context(nc.sbuf_tensor(name, list(shape), dt))

        gates_all = sb("gates_all", [128, 18, 32])
        small = sb("small", [128, 64])
        gemm_banks = [0, 1, 2, 3]
        gb_i = [0]

        def next_bank(bset=gemm_banks, ctr=gb_i):
            b = bset[ctr[0] % len(bset)]
            ctr[0] += 1
            return b, ps_take(b)

        ev_i = [0]
        ev_banks = [4, 5, 6, 7]

        def build_uT(sc, actT, src, ntiles, mod_r, lat_tiles, sh_col, s_col, ng_row, pfx):
            a_bc = alloc(sc, pfx + "a_bc", [128, D])
            sh_bc = alloc(sc, pfx + "sh_bc", [128, D])
            xt = [alloc(sc, pfx + f"xt{i}", [128, D]) for i in range(2)]
            ub = alloc(sc, pfx + "ub", [128, D], BF16)
            ssq = alloc(sc, pfx + "ssq", [128, 4])
            xfree = [[], []]
            bc_tok = None
            ub_free = []
            last = []
            for tt in range(ntiles):
                r = 0 if tt < lat_tiles else 1
                if tt == 0 or tt == lat_tiles:
                    w0 = last + [T_MOD]
                    t1_ = P.dma("sp", pfx + "bc0", a_bc[:], modrow[r, s_col * D:(s_col + 1) * D].partition_broadcast(128), waits=w0)
                    t2_ = P.dma("sp", pfx + "bc1", sh_bc[:], modrow[r, sh_col * D:(sh_col + 1) * D].partition_broadcast(128), waits=w0)
                    t3_ = P.dma("sp", pfx + "bc2", xt[1][:], norm_g[ng_row, :].partition_broadcast(128), waits=w0 + xfree[1])
                    bc_tok = P.stt(a_bc[:], a_bc[:], 1.0, xt[1][:], ALU.add, ALU.mult, waits=[t1_, t3_])
                    xfree[1] = [bc_tok]
                    bc_tok2 = t2_
                xs = tt % 2
                tl = P.dma("sp", pfx + f"x{xs}", xt[xs][:], src[tt * 128:(tt + 1) * 128, :], waits=xfree[xs])
                tsq = P.act(ub[:], xt[xs][:], AF.Square, accum_out=ssq[:, 0:1], waits=[tl] + ub_free)
                tsq = P.fence("act", ssq[:, 0:1])
                tms = P.ts(ssq[:, 1:2], ssq[:, 0:1], 1.0 / D, ALU.mult, EPS, ALU.add, waits=[tsq])
                tms = P.fence("dve", ssq[:, 1:2])
                tq = P.act(ssq[:, 2:3], ssq[:, 1:2], AF.Sqrt, waits=[tms])
                tq = P.fence("act", ssq[:, 2:3])
                trc = P.recip(ssq[:, 3:4], ssq[:, 2:3], waits=[tq])
                trc = P.fence("dve", ssq[:, 3:4])
                tst = P.stt(xt[xs][:], xt[xs][:], ssq[:, 3:4], a_bc[:], ALU.mult, ALU.mult, waits=[bc_tok, trc])
                tu = P.tt(ub[:], xt[xs][:], sh_bc[:], ALU.add, waits=[bc_tok2, tst])
                xfree[xs] = [tu]
                for q4 in range(4):
                    b, fr = next_bank()
                    pb = ps[b][:].bitcast(BF16)
                    for i in range(8):
                        kc = q4 * 8 + i
                        tp = P.tr(pb[:, i * 128:(i + 1) * 128], ub[:, kc * 128:(kc + 1) * 128], ident_b,
                                  waits=[tu] + fr + CONST if i == 0 else [], inc=(i == 7))
                    dst = actT[:, q4 * 8:(q4 + 1) * 8, tt * 128:(tt + 1) * 128]
                    srcp = pb.rearrange("p (a b) -> p a b", a=8)
                    if q4 % 2 == 0:
                        te = P.act(dst, srcp, AF.Copy, waits=[tp])
                    else:
                        te = P.cp(dst, srcp, waits=[tp])
                    psfree[b].append(te)
                    last = [te]
                ub_free = [tp]
            return last

        class Stage:
            def __init__(self, sc, name, shape, dt, n):
                self.ring = Ring([alloc(sc, f"{name}{i}", shape, dt) for i in range(n)], name)

            def get(self):
                return self.ring.next()

            def store(self, s, dst, src, tok, key):
                td = P.dma("sp", f"st_{self.ring.name}{s}", dst, src, waits=[tok])
                self.ring.release(s, td)
                return td

        def gemm_fm(W, c0, ncols, actT, tsegs, act_toks, evac):
            nblk = (ncols + 255) // 256
            for bi in range(nblk):
                nb = min(256, ncols - bi * 256)
                s, wt, tw = wblock(W, c0 + bi * 256, nb)
                tlast = None
                for j in range(nb // 128):
                    for si, segs in enumerate(tsegs):
                        b, fr = next_bank()
                        first = True
                        for (t0, n, o0) in segs:
                            for kc in range(KC):
                                tlast = P.mm(ps[b][:, o0:o0 + n], wt[:, kc, j * 128:(j + 1) * 128], actT[:, kc, t0:t0 + n],
                                             start=(kc == 0), stop=(kc == KC - 1),
                                             waits=([tw] + fr + act_toks) if first else [], inc=(kc == KC - 1))
                                first = False
                        toks = evac(b, bi * 2 + j, si, tlast)
                        psfree[b].extend(toks)
                wring.release(s, tlast)

        def gemm_tm(W, c0, ncols, actT, tiles, act_toks, evac, blk=256):
            nblk = (ncols + blk - 1) // blk
            for bi in range(nblk):
                nb = min(blk, ncols - bi * blk)
                s, wt, tw = wblock(W, c0 + bi * blk, nb)
                tlast = None
                for tt in tiles:
                    b, fr = next_bank()
                    for kc in range(KC):
                        tlast = P.mm(ps[b][:, 0:nb], actT[:, kc, tt * 128:(tt + 1) * 128], wt[:, kc, 0:nb],
                                     start=(kc == 0), stop=(kc == KC - 1),
                                     waits=([tw] + fr + act_toks) if kc == 0 else [], inc=(kc == KC - 1))
                    toks = evac(b, bi * blk, nb, tt, tlast)
                    psfree[b].extend(toks)
                wring.release(s, tlast)

        def rope_evac(b, n, tok, xq_st, tmp_st, out_st, cs_t, c0, scale, dst, key):
            s1, xq, fr1 = xq_st.get()
            t_x = P.act(xq[:, 0:n], ps[b][:, 0:n], AF.Copy, scale=scale, waits=[tok] + fr1)
            eb = ev_banks[ev_i[0] % 4]
            ev_i[0] += 1
            fr = ps_take(eb)
            t_r = P.mm(ps[eb][:, 0:n], RT, xq[:, 0:n], waits=[t_x] + fr + CONST)
            s2, tmp, fr2 = tmp_st.get()
            t_a = P.tt(tmp[:, 0:n], xq[:, 0:n], cs_t[:, 0, c0:c0 + n], ALU.mult, waits=[t_x] + fr2)
            xq_st.ring.release(s1, t_r)
            xq_st.ring.release(s1, t_a)
            s3, tmp2, fr3 = tmp_st.get()
            t_b = P.tt(tmp2[:, 0:n], ps[eb][:, 0:n], cs_t[:, 1, c0:c0 + n], ALU.mult, waits=[t_r] + fr3)
            psfree[eb].append(t_b)
            s4, ob, fr4 = out_st.get()
            t_o = P.tt(ob[:, 0:n], tmp[:, 0:n], tmp2[:, 0:n], ALU.add, waits=fr4 + [t_a, t_b])
            tmp_st.ring.release(s2, t_o)
            tmp_st.ring.release(s3, t_o)
            out_st.store(s4, dst, ob[:, 0:n], t_o, key)
            return [t_x]

        with contextlib.ExitStack() as sc1:
            actP = alloc(sc1, "actP", [128, KC, TP], BF16)
            with contextlib.ExitStack() as sc:
                t_act = build_uT(sc, actP, x_pre, NTP, 0, 8, 0, 1, 0, "p_")
            P.barrier()
            with contextlib.ExitStack() as sc:
                stB = Stage(sc, "g1b", [128, 512], BF16, 6)
                xq_st = Stage(sc, "g1xq", [128, 512], F32, 2)
                tmp_st = Stage(sc, "g1tmp", [128, 512], F32, 4)
                cs_t = alloc(sc, "g1cs", [128, 2, T + 128])
                gbias = alloc(sc, "g1gb", [128, 32])
                t_cs = P.dma("sp", "g1c", cs_t[:], ropec)
                t_gb = P.dma("sp", "g1c", gbias[:], gate_b.partition_broadcast(128))
                T_G1C = P.tok("g1c")

                def ev_gate(b, c0, nb, tt, tok):
                    return [P.tt(gates_all[:, 8 + tt, :], ps[b][:, 0:32], gbias[:], ALU.add, waits=[tok, T_G1C])]
                gemm_tm(w_gate, 0, 32, actP, range(NTP), [], ev_gate)

                def ev_tm_bf(dst_dram, key, use_act=True):
                    def ev(b, c0, nb, tt, tok):
                        s, ob, fr = stB.get()
                        if (tt % 2) == 0:
                            t = P.act(ob[:, 0:nb], ps[b][:, 0:nb], AF.Copy, waits=[tok] + fr)
                        else:
                            t = P.cp(ob[:, 0:nb], ps[b][:, 0:nb], waits=[tok] + fr)
                        stB.store(s, dst_dram(tt, c0, nb), ob[:, 0:nb], t, key)
                        return [t]
                    return ev
                gemm_tm(w_in, O_KM, 2048, actP, range(NTP), [],
                        ev_tm_bf(lambda tt, c0, nb: Kp[tt * 128:(tt + 1) * 128, c0:c0 + nb], "stKp"))
                gemm_tm(w_in, O_VM, 4096, actP, range(NTP), [],
                        ev_tm_bf(lambda tt, c0, nb: Vp[tt * 128:(tt + 1) * 128, c0:c0 + nb], "stVp"))
                vmap = {0: 0, 8: 1, 9: 2}
                gemm_tm(w_in, O_VA, 1024, actP, [0, 8, 9], [],
                        ev_tm_bf(lambda tt, c0, nb: Va[T + vmap[tt] * 128:T + (vmap[tt] + 1) * 128, c0:c0 + nb], "stVa"))

                def ev_ka(b, ch, si, tok):
                    toks = rope_evac(b, 128, tok, xq_st, tmp_st, stB, cs_t, T, 1.0,
                                     KaT[ch * 128:(ch + 1) * 128, T:T + 128], "stKa")
                    s, ob, fr = stB.get()
                    t = P.cp(ob[:, 0:256], ps[b][:, 128:384], waits=[tok] + fr)
                    stB.store(s, KaT[ch * 128:(ch + 1) * 128, T + 128:T + 384], ob[:, 0:256], t, "stKa")
                    return toks + [t]
                gemm_fm(w_in, O_KA, 1024, actP, [[(0, 128, 0), (1024, 256, 128)]], [], ev_ka)
            P.barrier()

        if stop_after == "G1":
            P.emit([P.tok(k) for k in ("stKp", "stVp", "stVa", "stKa")])
            return nc

        with contextlib.ExitStack() as sc1:
            actO = alloc(sc1, "actO", [128, KC, T], BF16)
            with contextlib.ExitStack() as sc:
                build_uT(sc, actO, x_own, NT, 0, NT, 0, 1, 0, "o_")
            P.barrier()
            with contextlib.ExitStack() as sc:
                stB = Stage(sc, "g2b", [128, 512], BF16, 6)
                stF = Stage(sc, "g2f", [128, 512], F32, 6)
                xq_st = Stage(sc, "g2xq", [128, 512], F32, 2)
                tmp_st = Stage(sc, "g2tmp", [128, 512], F32, 4)
                cs_t = alloc(sc, "g2cs", [128, 2, T + 128])
                gbias = alloc(sc, "g2gb", [128, 32])
                mng = alloc(sc, "g2mng", [128, D])
                P.dma("sp", "g2c", cs_t[:], ropec)
                P.dma("sp", "g2c", gbias[:], gate_b.partition_broadcast(128))
                P.dma("sp", "g2c", mng[:], m_norm_g.partition_broadcast(128))
                T_G2C = P.tok("g2c")
                segs2 = [[(0, 512, 0)], [(512, 512, 0)]]

                def ev_gate2(b, c0, nb, tt, tok):
                    return [P.tt(gates_all[:, tt, :], ps[b][:, 0:32], gbias[:], ALU.add, waits=[tok, T_G2C])]
                gemm_tm(w_gate, 0, 32, actO, range(NT), [], ev_gate2)

                def ev_fm_bf(dstT, scale, key):
                    def ev(b, ch, si, tok):
                        s, ob, fr = stB.get()
                        if (ch + si) % 2 == 0:
                            t = P.act(ob[:], ps[b][:], AF.Copy, scale=scale, waits=[tok] + fr)
                        else:
                            t = P.ts(ob[:], ps[b][:], scale, ALU.mult, waits=[tok] + fr)
                        stB.store(s, dstT[ch * 128:(ch + 1) * 128, si * 512:(si + 1) * 512], ob[:], t, key)
                        return [t]
                    return ev

                def ev_fm_sig(dstT, key):
                    def ev(b, ch, si, tok):
                        s, ob, fr = stF.get()
                        t = P.act(ob[:], ps[b][:], AF.Sigmoid, waits=[tok] + fr)
                        stF.store(s, dstT[ch * 128:(ch + 1) * 128, si * 512:(si + 1) * 512], ob[:], t, key)
                        return [t]
                    return ev

                def ev_fm_rope(dstT, scale, key):
                    def ev(b, ch, si, tok):
                        return rope_evac(b, 512, tok, xq_st, tmp_st, stB, cs_t, si * 512, scale,
                                         dstT[ch * 128:(ch + 1) * 128, si * 512:(si + 1) * 512], key)
                    return ev

                def ev_tm_bf2(dst, key):
                    def ev(b, c0, nb, tt, tok):
                        s, ob, fr = stB.get()
                        if (tt % 2) == 0:
                            t = P.act(ob[:, 0:nb], ps[b][:, 0:nb], AF.Copy, waits=[tok] + fr)
                        else:
                            t = P.cp(ob[:, 0:nb], ps[b][:, 0:nb], waits=[tok] + fr)
                        stB.store(s, dst[tt * 128:(tt + 1) * 128, c0:c0 + nb], ob[:, 0:nb], t, key)
                        return [t]
                    return ev

                def ev_tm_og(b, c0, nb, tt, tok):
                    s, ob, fr = stF.get()
                    t = P.act(ob[:, 0:nb], ps[b][:, 0:nb], AF.Sigmoid, waits=[tok] + fr)
                    t2 = P.tt(ob[:, 0:nb], ob[:, 0:nb], mng[:, c0:c0 + nb], ALU.mult, waits=[t, T_G2C])
                    stF.store(s, Gm[tt * 128:(tt + 1) * 128, c0:c0 + nb], ob[:, 0:nb], t2, "stGm")
                    return [t]

                gemm_fm(w_in, O_QM, 2048, actO, segs2, [], ev_fm_bf(QmT, 1.0 / 16.0, "stQm"))
                gemm_fm(w_in, O_KM, 2048, actO, segs2, [], ev_fm_bf(KmT, 1.0, "stKmT"))
                gemm_tm(w_in, O_KM, 2048, actO, range(NT), [], ev_tm_bf2(Km, "stKm"))
                gemm_tm(w_in, O_VM, 4096, actO, range(NT), [], ev_tm_bf2(Vm, "stVm"))
                gemm_tm(w_in, O_OM, 4096, actO, range(NT), [], ev_tm_og)
                gemm_fm(w_in, O_QA, 4096, actO, segs2, [], ev_fm_rope(QaT, 128.0 ** -0.5, "stQa"))
                gemm_fm(w_in, O_KA, 1024, actO, segs2, [], ev_fm_rope(KaT, 1.0, "stKa"))
                gemm_tm(w_in, O_VA, 1024, actO, range(NT), [], ev_tm_bf2(Va, "stVa"))
                gemm_fm(w_in, O_BM, 4096, actO, segs2, [], ev_fm_sig(sgmT, "stSgm"))
                gemm_fm(w_in, O_BA, 4096, actO, segs2, [], ev_fm_sig(sgaT, "stSga"))
            P.barrier()

        if stop_after == "G2":
            P.emit([P.tok(k) for k in P.dma_keys if k.startswith("st")])
            return nc
        with contextlib.ExitStack() as sc:
            NG = 2 * 18 * 8
            L1 = alloc(sc, "m_L1", [128, 18, 32])
            Call = alloc(sc, "m_C", [128, 2, 18, 8])
            CLall = alloc(sc, "m_CL", [128, 2, 18, 8])
            Aall = alloc(sc, "m_A", [128, 2, 18, 8])
            MEAN = alloc(sc, "m_MEAN", [128, 2, 18, 8])
            MU = alloc(sc, "m_MU", [128, 2, 18, 8])
            DD = alloc(sc, "m_DD", [128, 2, 18, 8])
            WEX = alloc(sc, "m_WEX", [128, 2, 18, 8])
            WEXb = alloc(sc, "m_WEXb", [128, 2, 18, 8], BF16)
            LB = alloc(sc, "m_LB", [128, 2, 18, 8])
            DEC = alloc(sc, "m_DEC", [128, 2, 18, 8])
            mcur = [alloc(sc, f"m_mcur{i}", [128, 8]) for i in range(2)]
            flat = lambda t: t[:].rearrange("p a b c -> p (a b c)")
            t0_ = P.act(L1[:], gates_all[:], AF.Exp, scale=-1.0)
            t_l1 = P.act(L1[:], L1[:], AF.Ln, bias=1.0, waits=[t0_])
            t_z1 = P.op("dve", lambda e: e.memset(MU[:], 0.0))
            t_z2 = P.op("dve", lambda e: e.memset(DD[:], 0.0))
            frA = ps_take(0) + ps_take(1) + ps_take(2)
            tl = None
            for d in range(2):
                tri = triU if d == 0 else triL
                for i in range(18):
                    o = (d * 18 + i) * 8
                    rhs = L1[:, i, d * 16 + 8:d * 16 + 16]
                    P.mm(ps[0][:, o:o + 8], tri, rhs, waits=[t_l1] + frA + CONST, inc=False)
                    tl = P.mm(ps[1][:, o:o + 8], ones_f, rhs)
            t_c = P.cp(flat(Call), ps[0][:, 0:NG], waits=[tl])
            t_cl = P.cp(flat(CLall), ps[1][:, 0:NG], waits=[tl])
            ta_ = None
            for d in range(2):
                ta_ = P.tt(Aall[:, d, :, :], gates_all[:, :, d * 16:d * 16 + 8], Call[:, d, :, :], ALU.add, waits=[t_c])
            tmn = P.mm(ps[2][:, 0:NG], ones_f, flat(Aall), waits=[ta_])
            t_mean = P.ts(flat(MEAN), ps[2][:, 0:NG], 1.0 / 128.0, ALU.mult, waits=[tmn])
            psfree[0].append(t_c); psfree[1].append(t_cl); psfree[2].append(t_mean)
            seqs = {0: [16, 17] + list(range(8)), 1: [17, 16] + list(range(15, 7, -1)) + list(range(7, -1, -1))}
            tlast = [t_mean, t_cl, t_z1, t_z2]
            for d in range(2):
                tprev = P.op("dve", lambda e: e.memset(mcur[0][:], 0.0))
                cur = 0
                for i in seqs[d]:
                    tmu = P.tt(MU[:, d, i, :], mcur[cur][:], MEAN[:, d, i, :], ALU.max, waits=[tprev] + tlast)
                    P.tt(DD[:, d, i, :], mcur[cur][:], MU[:, d, i, :], ALU.subtract, waits=[tmu])
                    tprev = P.tt(mcur[1 - cur][:], MU[:, d, i, :], CLall[:, d, i, :], ALU.subtract, waits=[tmu])
                    cur = 1 - cur
                    tlast = []
            t_dd = P.tok("dve")
            t_s1 = P.tt(flat(WEX), flat(Aall), flat(MU), ALU.subtract, waits=[t_dd])
            t_s2 = P.tt(flat(LB), flat(Call), flat(MU), ALU.subtract, waits=[t_dd])
            t_e1 = P.act(flat(WEX), flat(WEX), AF.Exp, waits=[t_s1])
            t_e2 = P.act(flat(LB), flat(LB), AF.Exp, waits=[t_s2])
            t_e3 = P.act(flat(DEC), flat(DD), AF.Exp, waits=[t_dd])
            t_wb = P.cp(flat(WEXb), flat(WEX), waits=[t_e1])
            GATES = [t_e1, t_e2, t_e3, t_wb]

            QT = [alloc(sc, f"m_QT{i}", [128, 2, T], BF16) for i in range(2)]
            KT = [alloc(sc, f"m_KT{i}", [128, 2, T], BF16) for i in range(2)]
            KO = [alloc(sc, f"m_KO{i}", [128, NT, 256], BF16) for i in range(2)]
            VO = [alloc(sc, f"m_VO{i}", [128, NT, 512], BF16) for i in range(2)]
            KPp = [alloc(sc, "m_KP", [128, NTP, 256], BF16)] * 2
            VPp = [alloc(sc, "m_VP", [128, NTP, 512], BF16)] * 2
            GH = [alloc(sc, "m_G", [128, NT, 512])] * 2
            hbuf_free = [[], []]
            hF = alloc(sc, "m_hF", [128, NT, 512])
            hmTs = [alloc(sc, "m_hmTs", [128, 4, T], BF16)] * 2
            hmTs_free = [[]]
            C32 = [alloc(sc, f"m_C32{d}", [128, 2, 512]) for d in range(2)]
            Cb = [alloc(sc, f"m_Cb{d}", [128, 2, 512], BF16) for d in range(2)]
            n32 = [alloc(sc, f"m_n32{d}", [128, 2]) for d in range(2)]
            nbf = [alloc(sc, f"m_nb{d}", [128, 2], BF16) for d in range(2)]
            VPr = [Ring([alloc(sc, f"m_vp{d}{i}", [128, 512], BF16) for i in range(3)], f"vp{d}") for d in range(2)]
            SMr = [Ring([alloc(sc, f"m_sm{d}{i}", [128, 128], BF16) for i in range(2)], f"sm{d}") for d in range(2)]
            sml = [alloc(sc, f"m_sml{d}", [128, 8]) for d in range(2)]
            HSr = Ring([alloc(sc, f"m_hs{i}", [128, 512]) for i in range(2)], "hs")
            HMr = Ring([alloc(sc, f"m_hm{i}", [128, 512], BF16) for i in range(2)], "hm")
            junk = alloc(sc, "m_junk", [128, 512], BF16)

            def load_head(h):
                s = h % 2
                fr = hbuf_free[s]
                hbuf_free[s] = []
                k = f"mld{s}"
                P.dma("sp", k, QT[s][:], QmT[h * 256:(h + 1) * 256, :].rearrange("(dc p) t -> p dc t", p=128), waits=fr)
                P.dma("sp", k, KT[s][:], KmT[h * 256:(h + 1) * 256, :].rearrange("(dc p) t -> p dc t", p=128))
                P.dma("sp", k, KO[s][:], Km[:, h * 256:(h + 1) * 256].rearrange("(tt p) d -> p tt d", p=128))
                P.dma("sp", k, VO[s][:], Vm[:, h * 512:(h + 1) * 512].rearrange("(tt p) d -> p tt d", p=128))
                return P.tok(k)

            single_free = [[]]

            def load_head_single(h):
                fr = single_free[0]
                k = "mlds"
                P.dma("sp", k, KPp[0][:], Kp[:, h * 256:(h + 1) * 256].rearrange("(tt p) d -> p tt d", p=128), waits=fr)
                P.dma("sp", k, VPp[0][:], Vp[:, h * 512:(h + 1) * 512].rearrange("(tt p) d -> p tt d", p=128))
                P.dma("sp", k, GH[0][:], Gm[:, h * 512:(h + 1) * 512].rearrange("(tt p) d -> p tt d", p=128))
                return P.tok(k)

            reg_free = {}

            def rtake(key):
                fr = reg_free.get(key, [])
                reg_free[key] = []
                return fr
            for b in range(8):
                reg_free[("bank", b)] = ps_take(b)

            ld_tok = load_head(0)
            for h in range(8):
                s = h % 2
                lds_tok = load_head_single(h)
                nxt_tok = load_head(h + 1) if h + 1 < 8 else None
                LD = [ld_tok, lds_tok] + GATES
                state_tok = {0: [], 1: []}
                cb_read = {0: [], 1: []}
                last_use = []
                for k in range(18):
                    for d in range(2):
                        if k >= len(seqs[d]):
                            continue
                        i = seqs[d][k]
                        own = i < 8
                        first = (k == 0)
                        lastc = (k == len(seqs[d]) - 1)
                        bA, bB, bC, bD = 4 * d, 4 * d + 1, 4 * d + 2, 4 * d + 3
                        base_w = LD + (rtake(("bank", bA)) + rtake(("bank", bB)) + rtake(("bank", bC)) + rtake(("bank", bD)) if k == 0 else [])
                        if own:
                            K_tm, V_tm = KO[s][:, i, :], VO[s][:, i, :]
                        else:
                            j = i - 8
                            K_tm, V_tm = KPp[s][:, j, :], VPp[s][:, j, :]
                        wcol = WEX[:, d, i, h:h + 1]
                        wcolb = WEXb[:, d, i, h:h + 1]
                        dcol = DEC[:, d, i, h:h + 1]
                        lbcol = LB[:, d, i, h:h + 1]
                        sv, vp_t, frv = VPr[d].next()
                        t_vp = P.act(vp_t[:], V_tm, AF.Copy, scale=wcol, waits=base_w + frv)
                        vp_users = []
                        t_cb = None
                        if not first and own:
                            t_cb = P.act(Cb[d][:].rearrange("p a b -> p (a b)"), C32[d][:].rearrange("p a b -> p (a b)"),
                                         AF.Copy, scale=dcol, waits=state_tok[d] + cb_read[d])
                            t_nb = P.ts(nbf[d][:], n32[d][:], dcol, ALU.mult, waits=state_tok[d] + cb_read[d])
                            t_nb = P.fence("dve", nbf[d][:, 0:1])
                            cb_read[d] = []
                        if own:
                            c0 = i * 128
                            fr = rtake(("st", d))
                            for dc in range(2):
                                t_st = P.mm(ps[bA][:, 0:128], KT[s][:, dc, c0:c0 + 128], QT[s][:, dc, c0:c0 + 128],
                                            start=(dc == 0), stop=(dc == 1), waits=base_w + fr if dc == 0 else [], inc=(dc == 1))
                            ssm, sm_t, frs = SMr[d].next()
                            mask = maskU4[:, 0:128] if d == 0 else maskL4[:, 0:128]
                            t_sm = P.tt(sm_t[:], ps[bA][:, 0:128], mask, ALU.mult, waits=[t_st] + frs + CONST)
                            reg_free[("st", d)] = [t_sm]
                            fr = rtake(("in", d))
                            if not first:
                                for dc in range(2):
                                    P.mm(ps[bB][:, :], QT[s][:, dc, c0:c0 + 128], Cb[d][:, dc, :], start=(dc == 0), stop=False,
                                         waits=[t_cb] + fr if dc == 0 else [], inc=False)
                            t_in = P.mm(ps[bB][:, :], sm_t[:], vp_t[:], start=first, stop=True, waits=[t_sm, t_vp] + (fr if first else []))
                            fr = rtake(("den", d))
                            if not first:
                                for dc in range(2):
                                    P.mm(ps[bA][:, 128:129], QT[s][:, dc, c0:c0 + 128], nbf[d][:, dc:dc + 1], start=(dc == 0), stop=False,
                                         waits=[t_nb] + fr if dc == 0 else [], inc=False)
                            t_den = P.mm(ps[bA][:, 128:129], sm_t[:], wcolb, start=first, stop=True, waits=(fr if first else []))
                            SMr[d].release(ssm, t_den)
                            cb_read[d].append(t_den)
                            vp_users.append(t_in)
                            t_dm0 = P.ts(sml[d][:, 6:7], ps[bA][:, 128:129], -1.0, ALU.mult, lbcol, ALU.max, waits=[t_den] + ([P.tok(f"dbgden{d}")] if debug else []))
                            t_dm0f = P.fence("dve", sml[d][:, 6:7])
                            t_dm = P.tt(sml[d][:, 0:1], ps[bA][:, 128:129], sml[d][:, 6:7], ALU.max, waits=[t_dm0f])
                            t_dmf = P.fence("dve", sml[d][:, 0:1])
                            reg_free[("den", d)] = [t_dm]
                            t_r = P.recip(sml[d][:, 1:2], sml[d][:, 0:1], waits=[t_dmf])
                            t_r = P.fence("dve", sml[d][:, 1:2])
                            if debug:
                                t_cpd = P.cp(sml[d][:, 7:8], ps[bA][:, 128:129], waits=[t_den, t_r])
                                t_dbg = P.dma("sp", f"dbgden{d}", dbgden[h, d, i], sml[d][:, 0:8], waits=[t_cpd, t_r])
                                reg_free[("den", d)] = [t_dm, t_dbg]
                            if d == 0:
                                t_h = P.act(hF[:, i, :], ps[bB][:, :], AF.Copy, scale=sml[d][:, 1:2], waits=[t_in, t_r] + last_use)
                                reg_free[("in", d)] = [t_h]
                                hf_tok = t_h
                            else:
                                shs, hs_t, frh = HSr.next()
                                t_hs = P.stt(hs_t[:], ps[bB][:, :], sml[d][:, 1:2], hF[:, i, :], ALU.mult, ALU.add,
                                             waits=[t_in, t_r, P.tok("act")] + frh)
                                reg_free[("in", d)] = [t_hs]
                                t_sq = P.act(junk[:], hs_t[:], AF.Square, accum_out=sml[d][:, 2:3], waits=[t_hs])
                                t_sq = P.fence("act", sml[d][:, 2:3])
                                t_v = P.ts(sml[d][:, 3:4], sml[d][:, 2:3], 1.0 / 512.0, ALU.mult, EPS, ALU.add, waits=[t_sq])
                                t_v = P.fence("dve", sml[d][:, 3:4])
                                t_q = P.act(sml[d][:, 4:5], sml[d][:, 3:4], AF.Sqrt, waits=[t_v])
                                t_q = P.fence("act", sml[d][:, 4:5])
                                t_rs = P.recip(sml[d][:, 5:6], sml[d][:, 4:5], waits=[t_q])
                                t_rs = P.fence("dve", sml[d][:, 5:6])
                                shm, hm_t, frm = HMr.next()
                                t_hm = P.stt(hm_t[:], hs_t[:], sml[d][:, 5:6], GH[s][:, i, :], ALU.mult, ALU.mult, waits=[t_rs] + frm)
                                HSr.release(shs, t_hm)
                                last_use = [t_hm]
                                fr = rtake(("tr", d))
                                pb = ps[bA][:].bitcast(BF16)
                                for vc in range(4):
                                    t_tr = P.tr(pb[:, 512 + vc * 128:512 + (vc + 1) * 128], hm_t[:, vc * 128:(vc + 1) * 128], ident_b,
                                                waits=[t_hm] + fr if vc == 0 else [], inc=(vc == 3))
                                HMr.release(shm, t_tr)
                                t_ev = P.cp(hmTs[s][:, :, c0:c0 + 128], pb[:, 512:1024].rearrange("p (a b) -> p a b", a=4),
                                            waits=[t_tr] + hmTs_free[0])
                                hmTs_free[0] = []
                                reg_free[("tr", d)] = [t_ev]
                                hm_ev = t_ev
                        if not lastc:
                            fr = rtake(("upd", d))
                            for dc in range(2):
                                bU = bC if dc == 0 else bD
                                t_u = P.mm(ps[bU][:, :], K_tm[:, dc * 128:(dc + 1) * 128], vp_t[:], waits=[t_vp] + base_w + fr)
                            fr2 = rtake(("nu", d))
                            for dc in range(2):
                                t_nu = P.mm(ps[bA][:, 130 + 2 * dc:131 + 2 * dc], K_tm[:, dc * 128:(dc + 1) * 128], wcolb, waits=fr2 + base_w)
                            vp_users.append(t_u)
                            toks = []
                            if first:
                                toks.append(P.act(C32[d][:, 0, :], ps[bC][:, :], AF.Copy, waits=[t_u] + cb_read[d]))
                                toks.append(P.cp(C32[d][:, 1, :], ps[bD][:, :], waits=[t_u] + cb_read[d]))
                                toks.append(P.cp(n32[d][:].rearrange("p (a b) -> p a b", b=1), ps[bA][:, 130:134].rearrange("p (a b) -> p a b", b=2)[:, :, 0:1], waits=[t_nu]))
                            else:
                                wst = state_tok[d] + cb_read[d]
                                toks.append(P.stt(C32[d][:, 0, :], C32[d][:, 0, :], dcol, ps[bC][:, :], ALU.mult, ALU.add, waits=[t_u] + wst))
                                toks.append(P.stt(C32[d][:, 1, :], C32[d][:, 1, :], dcol, ps[bD][:, :], ALU.mult, ALU.add, waits=[t_u] + wst))
                                toks.append(P.stt(n32[d][:].rearrange("p (a b) -> p a b", b=1), n32[d][:].rearrange("p (a b) -> p a b", b=1), dcol,
                                                  ps[bA][:, 130:134].rearrange("p (a b) -> p a b", b=2)[:, :, 0:1], ALU.mult, ALU.add, waits=[t_nu] + wst))
                            cb_read[d] = []
                            state_tok[d] = toks
                            reg_free[("upd", d)] = [toks[0], toks[1]]
                            reg_free[("nu", d)] = [toks[2]]
                        for tkn in vp_users:
                            VPr[d].release(sv, tkn)
                td = P.dma("sp", f"sthm{s}", hmT[h * 512:(h + 1) * 512, :].rearrange("(vc p) t -> p vc t", p=128), hmTs[s][:], waits=[hm_ev])
                hmTs_free[0] = [td]
                hbuf_free[s] = [P.tok("pe"), P.tok("act"), P.tok("dve")]
                single_free[0] = [P.tok("pe"), P.tok("act"), P.tok("dve")]
                ld_tok = nxt_tok
            for b in range(8):
                psfree[b] = [P.tok("act"), P.tok("dve")]
            P.barrier()

        if stop_after == "M":
            P.emit([P.tok(k) for k in P.dma_keys if k.startswith("st")])
            return nc
        with contextlib.ExitStack() as sc:
            NK = T + 384
            KaS = [alloc(sc, f"a_K{i}", [128, NK], BF16) for i in range(2)]
            VaS = [alloc(sc, f"a_V{i}", [128, 11, 128], BF16) for i in range(2)]
            QaS = [alloc(sc, f"a_Q{i}", [128, 4, T], BF16) for i in range(2)]
            OgS = [alloc(sc, f"a_O{i}", [128, 4, T], BF16) for i in range(2)]
            og_free = [[], []]
            abuf_free = [[], []]
            es_bc = alloc(sc, "a_es", [128, 32])
            es_all = alloc(sc, "a_esall", [128, 8, 512])
            PTr = Ring([alloc(sc, f"a_pt{i}", [128, 512], BF16) for i in range(6)], "pt")
            DNr = Ring([alloc(sc, f"a_dn{i}", [128, 512]) for i in range(2)], "dn")
            t_es = P.dma("sp", "a_c", es_bc[:], sink.partition_broadcast(128))
            t_ex = P.act(es_bc[:], es_bc[:], AF.Exp, waits=[t_es])
            t_esa = None
            for hd in range(32):
                t_esa = P.ts(es_all[:, hd // 4, (hd % 4) * 128:(hd % 4 + 1) * 128], ones_f, es_bc[:, hd:hd + 1], ALU.mult, waits=[t_ex] + CONST)

            def load_g(g):
                s = g % 2
                k = f"ald{s}"
                P.dma("sp", k, KaS[s][:], KaT[g * 128:(g + 1) * 128, :], waits=abuf_free[s])
                P.dma("sp", k, VaS[s][:], Va[:, g * 128:(g + 1) * 128].rearrange("(tt p) d -> p tt d", p=128))
                P.dma("sp", k, QaS[s][:], QaT[g * 512:(g + 1) * 512, :].rearrange("(hq p) t -> p hq t", p=128))
                abuf_free[s] = []
                return P.tok(k)
            jobs = []
            for g in range(8):
                for n in range(8):
                    keys = []
                    if n >= 1:
                        keys.append((n - 1, maskL4))
                    keys.append((n, None))
                    keys.append((n + 1 if n < 7 else 8, maskU4))
                    keys.append((9, None))
                    keys.append((10, None))
                    for j, (kt, mask) in enumerate(keys):
                        jobs.append((g, n, j, len(keys), kt, mask))
            ld_tok = {0: load_g(0)}
            st = {}
            blk_state = {}
            blk_ctr = [0]

            def emit_S(i):
                g, n, j, nk, kt, mask = jobs[i]
                s = g % 2
                if j == 0 and n == 1 and g + 1 < 8:
                    ld_tok[g + 1] = load_g(g + 1)
                b = i % 4
                fr = ps_take(b)
                t_s = P.mm(ps[b][:, :].rearrange("p (a b) -> p a b", a=4), KaS[s][:, kt * 128:(kt + 1) * 128],
                           QaS[s][:, :, n * 128:(n + 1) * 128], waits=[ld_tok[g]] + fr)
                sp_, pt, frp = PTr.next()
                t_p = P.act(pt[:], ps[b][:, :], AF.Exp, waits=[t_s] + frp)
                psfree[b].append(t_p)
                if mask is not None:
                    t_p = P.tt(pt[:], pt[:], mask, ALU.mult, waits=[t_p] + CONST)
                st[i] = (t_p, pt, sp_)

            def emit_PV(i):
                g, n, j, nk, kt, mask = jobs[i]
                s = g % 2
                t_p, pt, sp_ = st.pop(i)
                if j == 0:
                    bo = 4 + (blk_ctr[0] % 2)
                    bd = 6 + (blk_ctr[0] % 2)
                    blk_ctr[0] += 1
                    blk_state[(g, n)] = (bo, bd, ps_take(bo), ps_take(bd))
                bo, bd, fro, frd = blk_state[(g, n)]
                t_o = P.mm(ps[bo][:, :], VaS[s][:, kt, :], pt[:], start=(j == 0), stop=(j == nk - 1),
                           waits=[t_p] + (fro if j == 0 else []))
                t_d = P.mm(ps[bd][:, :], ones_b, pt[:], start=(j == 0), stop=(j == nk - 1),
                           waits=(frd + CONST if j == 0 else []))
                PTr.release(sp_, t_d)
                if j == nk - 1:
                    sd, dn, frn = DNr.next()
                    t1_ = P.tt(dn[:], ps[bd][:, :], es_all[:, g, :], ALU.add, waits=[t_d, t_esa] + frn)
                    psfree[bd].append(t1_)
                    t2_ = P.act(dn[:], dn[:], AF.Ln, waits=[t1_])
                    t3_ = P.act(dn[:], dn[:], AF.Exp, scale=-1.0, waits=[t2_])
                    t4_ = P.tt(OgS[s][:, :, n * 128:(n + 1) * 128], ps[bo][:, :].rearrange("p (a b) -> p a b", a=4),
                               dn[:].rearrange("p (a b) -> p a b", a=4), ALU.mult, waits=[t_o, t3_] + og_free[s])
                    og_free[s] = []
                    psfree[bo].append(t4_)
                    DNr.release(sd, t4_)
                    if n == 7:
                        td = P.dma("sp", f"stha{s}", haT[g * 512:(g + 1) * 512, :].rearrange("(hq p) t -> p hq t", p=128), OgS[s][:], waits=[t4_])
                        og_free[s] = [td]
                        abuf_free[s] = [P.tok("pe")]
            LOOK = 2
            for i in range(min(LOOK, len(jobs))):
                emit_S(i)
            for i in range(len(jobs)):
                emit_PV(i)
                if i + LOOK < len(jobs):
                    emit_S(i + LOOK)
            P.barrier()

        if stop_after == "A":
            P.emit([P.tok(k) for k in P.dma_keys if k.startswith("st")])
            return nc

        segs2 = [[(0, 512, 0)], [(512, 512, 0)]]
        with contextlib.ExitStack() as sc:
            actM = alloc(sc, "y1_act", [128, KC, T], BF16)
            t_ld = P.dma("sp", "y1ld", actM[:], hmT.rearrange("(kc p) t -> p kc t", p=128))
            sgr = Ring([alloc(sc, f"y1_sg{i}", [128, 512]) for i in range(3)], "y1sg")
            stF = Stage(sc, "y1f", [128, 512], F32, 4)

            def ev_y1(b, ch, si, tok):
                ss_, sg, frs = sgr.next()
                tl_ = P.dma("sp", f"y1sg{ss_}", sg[:], sgmT[ch * 128:(ch + 1) * 128, si * 512:(si + 1) * 512], waits=frs)
                s_, ob, fr = stF.get()
                t = P.tt(ob[:], ps[b][:, :], sg[:], ALU.mult, waits=[tok, tl_] + fr)
                sgr.release(ss_, t)
                stF.store(s_, t1T[ch * 128:(ch + 1) * 128, si * 512:(si + 1) * 512], ob[:], t, "stT1")
                return [t]
            gemm_fm(w_out_m, 0, D, actM, segs2, [t_ld], ev_y1)
            P.barrier()

        with contextlib.ExitStack() as sc1:
            mixsT = alloc(sc1, "mixsT", [128, KC, T], BF16)
            with contextlib.ExitStack() as sc:
                actA = alloc(sc, "y2_act", [128, KC, T], BF16)
                t_ld = P.dma("sp", "y2ld", actA[:], haT.rearrange("(kc p) t -> p kc t", p=128))
                sgr = Ring([alloc(sc, f"y2_sg{i}", [128, 512]) for i in range(3)], "y2sg")
                t1r = Ring([alloc(sc, f"y2_t1{i}", [128, 512]) for i in range(3)], "y2t1")
                tmr = Ring([alloc(sc, f"y2_tm{i}", [128, 512]) for i in range(2)], "y2tm")

                def ev_y2(b, ch, si, tok):
                    ss_, sg, frs = sgr.next()
                    tl1 = P.dma("sp", f"y2sg{ss_}", sg[:], sgaT[ch * 128:(ch + 1) * 128, si * 512:(si + 1) * 512], waits=frs)
                    st_, t1b, frt = t1r.next()
                    tl2 = P.dma("sp", f"y2t1{st_}", t1b[:], t1T[ch * 128:(ch + 1) * 128, si * 512:(si + 1) * 512], waits=frt)
                    sm_, tm, frm = tmr.next()
                    ta = P.tt(tm[:], ps[b][:, :], sg[:], ALU.mult, waits=[tok, tl1] + frm)
                    sgr.release(ss_, ta)
                    tb = P.tt(mixsT[:, ch, si * 512:(si + 1) * 512], tm[:], t1b[:], ALU.add, waits=[ta, tl2])
                    t1r.release(st_, tb)
                    tmr.release(sm_, tb)
                    return [ta]
                gemm_fm(w_out_a, 0, D, actA, segs2, [t_ld], ev_y2)
            P.barrier()
            with contextlib.ExitStack() as sc:
                stF = Stage(sc, "o_f", [128, 512], F32, 6)

                def ev_o(b, c0, nb, tt, tok):
                    s_, ob, fr = stF.get()
                    if tt % 2 == 0:
                        t = P.act(ob[:, 0:nb], ps[b][:, 0:nb], AF.Copy, waits=[tok] + fr)
                    else:
                        t = P.cp(ob[:, 0:nb], ps[b][:, 0:nb], waits=[tok] + fr)
                    stF.store(s_, mixd[tt * 128:(tt + 1) * 128, c0:c0 + nb], ob[:, 0:nb], t, "stMix")
                    return [t]
                gemm_tm(w_o, 0, D, mixsT, range(NT), [P.tok("dve")], ev_o)
            P.barrier()

        if stop_after == "O":
            P.emit([P.tok(k) for k in P.dma_keys if k.startswith("st")])
            return nc

        def load_bc(tile, row_ap, waits=(), key="bcl"):
            return P.dma("sp", key, tile[:], row_ap.partition_broadcast(128), waits=list(waits))

        with contextlib.ExitStack() as sc1:
            act2 = alloc(sc1, "act2", [128, KC, T], BF16)
            with contextlib.ExitStack() as sc:
                g1n = alloc(sc, "r1_g1n", [128, D])
                a2 = alloc(sc, "r1_a2", [128, D])
                sh2 = alloc(sc, "r1_sh2", [128, D])
                mt = [alloc(sc, "r1_m0", [128, D])] * 2
                xt = [alloc(sc, "r1_x0", [128, D])] * 2
                ub = alloc(sc, "r1_ub", [128, D], BF16)
                sq = alloc(sc, "r1_sq", [128, 8])
                ta = load_bc(g1n, modrow[0, 2 * D:3 * D], [T_MOD])
                tb = load_bc(mt[0], norm_g[1, :])
                t_g1n = P.tt(g1n[:], g1n[:], mt[0][:], ALU.mult, waits=[ta, tb])
                ta = load_bc(a2, modrow[0, 4 * D:5 * D])
                tb = load_bc(xt[0], norm_g[2, :])
                t_a2 = P.stt(a2[:], a2[:], 1.0, xt[0][:], ALU.add, ALU.mult, waits=[ta, tb])
                t_sh2 = load_bc(sh2, modrow[0, 3 * D:4 * D])
                mfree = [[t_g1n], [t_g1n]]
                xfree = [[t_a2], [t_a2]]
                ub_free = []
                for tt in range(NT):
                    s = 0
                    tlm = P.dma("sp", f"r1m{s}", mt[s][:], mixd[tt * 128:(tt + 1) * 128, :], waits=mfree[s])
                    tlx = P.dma("sp", f"r1x{s}", xt[s][:], x_own[tt * 128:(tt + 1) * 128, :], waits=xfree[s])
                    tsq = P.act(ub[:], mt[s][:], AF.Square, accum_out=sq[:, 0:1], waits=[tlm] + ub_free)
                    tsq = P.fence("act", sq[:, 0:1])
                    tv = P.ts(sq[:, 1:2], sq[:, 0:1], 1.0 / D, ALU.mult, EPS, ALU.add, waits=[tsq])
                    tv = P.fence("dve", sq[:, 1:2])
                    tq = P.act(sq[:, 2:3], sq[:, 1:2], AF.Sqrt, waits=[tv])
                    tq = P.fence("act", sq[:, 2:3])
                    tr_ = P.recip(sq[:, 3:4], sq[:, 2:3], waits=[tq])
                    tr_ = P.fence("dve", sq[:, 3:4])
                    tm1 = P.stt(mt[s][:], mt[s][:], sq[:, 3:4], g1n[:], ALU.mult, ALU.mult, waits=[tr_, t_g1n])
                    th = P.tt(xt[s][:], xt[s][:], mt[s][:], ALU.add, waits=[tm1, tlx])
                    tst = P.dma("sp", "stH", hbuf[tt * 128:(tt + 1) * 128, :], xt[s][:], waits=[th])
                    tsq2 = P.act(ub[:], xt[s][:], AF.Square, accum_out=sq[:, 4:5], waits=[th])
                    tsq2 = P.fence("act", sq[:, 4:5])
                    tv2 = P.ts(sq[:, 5:6], sq[:, 4:5], 1.0 / D, ALU.mult, EPS, ALU.add, waits=[tsq2])
                    tv2 = P.fence("dve", sq[:, 5:6])
                    tq2 = P.act(sq[:, 6:7], sq[:, 5:6], AF.Sqrt, waits=[tv2])
                    tq2 = P.fence("act", sq[:, 6:7])
                    tr2 = P.recip(sq[:, 7:8], sq[:, 6:7], waits=[tq2])
                    tr2 = P.fence("dve", sq[:, 7:8])
                    tu1 = P.stt(mt[s][:], xt[s][:], sq[:, 7:8], a2[:], ALU.mult, ALU.mult, waits=[tr2, t_a2, tm1])
                    tu = P.tt(ub[:], mt[s][:], sh2[:], ALU.add, waits=[tu1, t_sh2, tsq2])
                    mfree[s] = [tu]
                    xfree[s] = [tst, tu1]
                    for q4 in range(4):
                        b, fr = next_bank()
                        pb = ps[b][:].bitcast(BF16)
                        for i in range(8):
                            kc = q4 * 8 + i
                            tp = P.tr(pb[:, i * 128:(i + 1) * 128], ub[:, kc * 128:(kc + 1) * 128], ident_b,
                                      waits=[tu] + fr + CONST if i == 0 else [], inc=(i == 7))
                        dst = act2[:, q4 * 8:(q4 + 1) * 8, tt * 128:(tt + 1) * 128]
                        srcp = pb.rearrange("p (a b) -> p a b", a=8)
                        te = P.act(dst, srcp, AF.Copy, waits=[tp]) if q4 % 2 == 0 else P.cp(dst, srcp, waits=[tp])
                        psfree[b].append(te)
                    ub_free = [tp]
            P.barrier()

            with contextlib.ExitStack() as sc:
                stB = Stage(sc, "f1b", [128, 512], BF16, 6)
                tmr = Ring([alloc(sc, f"f1_t{i}", [128, 512]) for i in range(3)], "f1t")

                def ev_f1(b, ch, si, tok):
                    sm_, tm, frm = tmr.next()
                    ta = P.act(tm[:], ps[b][:, :], AF.Relu, waits=[tok] + frm)
                    s_, ob, fr = stB.get()
                    tb = P.tt(ob[:], tm[:], tm[:], ALU.mult, waits=[ta] + fr)
                    tmr.release(sm_, tb)
                    stB.store(s_, h1T[ch * 128:(ch + 1) * 128, si * 512:(si + 1) * 512], ob[:], tb, "stH1")
                    return [ta]
                gemm_fm(w_ff1, 0, DFF, act2, segs2, [], ev_f1)
            P.barrier()

        with contextlib.ExitStack() as sc:
            acc = alloc(sc, "f2_acc", [128, NT, 2048])
            h1g = [alloc(sc, f"f2_h{i}", [128, 16, T], BF16) for i in range(2)]
            hfree = [[], []]
            gi = 0
            acc_tok = {}
            acc_rd = []
            for nh in range(2):
                for fg in range(8):
                    s = gi % 2
                    gi += 1
                    tlh = P.dma("sp", f"f2h{s}", h1g[s][:], h1T[fg * 2048:(fg + 1) * 2048, :].rearrange("(fc p) t -> p fc t", p=128),
                                waits=hfree[s] + [P.tok("stH1")])
                    tmm = None
                    for nbk in range(4):
                        c0 = nh * 2048 + nbk * 512
                        sw, wt, tw = WS.fetch(w_ff2[fg * 2048:(fg + 1) * 2048, c0:c0 + 512].rearrange("(fc p) n -> p fc n", p=128),
                                              lambda t: t[:].rearrange("p a b -> p (a b)").rearrange("p (c d) -> p c d", c=16))
                        wv = wt[:].rearrange("p a b -> p (a b)").rearrange("p (c d) -> p c d", c=16)
                        for tt in range(NT):
                            b, fr = next_bank()
                            for fc in range(16):
                                tmm = P.mm(ps[b][:, :], h1g[s][:, fc, tt * 128:(tt + 1) * 128], wv[:, fc, :], start=(fc == 0), stop=(fc == 15),
                                           waits=([tw, tlh] + fr) if fc == 0 else [], inc=(fc == 15))
                            dst = acc[:, tt, nbk * 512:(nbk + 1) * 512]
                            if fg == 0:
                                if tt % 2 == 0:
                                    t = P.act(dst, ps[b][:, :], AF.Copy, waits=[tmm] + acc_rd)
                                else:
                                    t = P.cp(dst, ps[b][:, :], waits=[tmm] + acc_rd)
                            else:
                                t = P.tt(dst, ps[b][:, :], dst, ALU.add, waits=[tmm, acc_tok[(tt, nbk)]])
                            acc_tok[(tt, nbk)] = t
                            psfree[b].append(t)
                        WS.ring.release(sw, tmm)
                    hfree[s] = [tmm]
                acc_rd = []
                for tt in range(NT):
                    td = P.dma("sp", "stFfn", ffn[tt * 128:(tt + 1) * 128, nh * 2048:(nh + 1) * 2048], acc[:, tt, :],
                               waits=[acc_tok[(tt, k)] for k in range(4)])
                    acc_rd.append(td)
            P.barrier()

        if stop_after == "F2":
            P.emit([P.tok(k) for k in P.dma_keys if k.startswith("st")])
            return nc

        with contextlib.ExitStack() as sc:
            g2n = alloc(sc, "r2_g2n", [128, D])
            ft = [alloc(sc, f"r2_f{i}", [128, D]) for i in range(2)]
            ht = [alloc(sc, f"r2_h{i}", [128, D]) for i in range(2)]
            ub = alloc(sc, "r2_ub", [128, D], BF16)
            sq = alloc(sc, "r2_sq", [128, 4])
            ta = load_bc(g2n, modrow[0, 5 * D:6 * D], [T_MOD], key="bcl2")
            tb = load_bc(ft[0], norm_g[3, :], key="bcl2")
            t_g2n = P.tt(g2n[:], g2n[:], ft[0][:], ALU.mult, waits=[ta, tb])
            ffree = [[t_g2n], []]
            hfree2 = [[], []]
            ub_free = []
            for tt in range(NT):
                s = tt % 2
                tlf = P.dma("sp", f"r2f{s}", ft[s][:], ffn[tt * 128:(tt + 1) * 128, :], waits=ffree[s])
                tlh = P.dma("sp", f"r2h{s}", ht[s][:], hbuf[tt * 128:(tt + 1) * 128, :], waits=hfree2[s] + [P.tok("stH")])
                tsq = P.act(ub[:], ft[s][:], AF.Square, accum_out=sq[:, 0:1], waits=[tlf])
                tsq = P.fence("act", sq[:, 0:1])
                tv = P.ts(sq[:, 1:2], sq[:, 0:1], 1.0 / D, ALU.mult, EPS, ALU.add, waits=[tsq])
                tv = P.fence("dve", sq[:, 1:2])
                tq = P.act(sq[:, 2:3], sq[:, 1:2], AF.Sqrt, waits=[tv])
                tq = P.fence("act", sq[:, 2:3])
                tr_ = P.recip(sq[:, 3:4], sq[:, 2:3], waits=[tq])
                tr_ = P.fence("dve", sq[:, 3:4])
                tm1 = P.stt(ft[s][:], ft[s][:], sq[:, 3:4], g2n[:], ALU.mult, ALU.mult, waits=[tr_, t_g2n])
                th = P.tt(ht[s][:], ht[s][:], ft[s][:], ALU.add, waits=[tm1, tlh])
                tst = P.dma("sp", f"stOut{s}", out[tt * 128:(tt + 1) * 128, :], ht[s][:], waits=[th])
                ffree[s] = [th]
                hfree2[s] = [tst]
        P.emit([P.tok("stOut0"), P.tok("stOut1")])
        return nc


def _consts():
    idx = np.arange(128)
    ident = np.eye(128, dtype=np.float32)
    triU = (idx[:, None] <= idx[None, :]).astype(np.float32)
    triL = (idx[:, None] >= idx[None, :]).astype(np.float32)
    ones = np.ones((128, 128), np.float32)
    R = np.zeros((128, 128), np.float32)
    for m in range(128):
        if (m % 64) < 32:
            R[m, m + 32] = -1.0
        else:
            R[m, m - 32] = 1.0
    RT = np.ascontiguousarray(R.T)
    cf = np.concatenate([ident, triU, triL, ones, RT], axis=1)
    bf = ml_dtypes.bfloat16
    maskU4 = np.tile(triU, (1, 4))
    maskL4 = np.tile(triL, (1, 4))
    cb = np.concatenate([ident, ones, maskU4, maskL4], axis=1).astype(bf)
    return cf, cb


def _rope_tables(pos):
    nf = 32
    freqs = (10000.0 ** (-np.arange(nf, dtype=np.float32) / nf)).astype(np.float32)
    row = (pos // 64).astype(np.float32)
    col = (pos % 64).astype(np.float32)
    ang_r = row[None, :] * freqs[:, None]
    ang_c = col[None, :] * freqs[:, None]
    ang = np.concatenate([ang_r, ang_r, ang_c, ang_c], axis=0)
    return np.stack([np.cos(ang), np.sin(ang)], axis=1).astype(np.float32)


def prep_inputs(inp, cores=range(8)):
    f = lambda a: np.ascontiguousarray(np.asarray(a, dtype=np.float32))
    x, c, ctx, c_ctx = f(inp["x"]), f(inp["c"]), f(inp["ctx"]), f(inp["c_ctx"])
    w_in = f(inp["w_in"])[0]
    shared = {
        "w_mod": f(inp["w_mod"])[0], "b_mod": f(inp["b_mod"])[0], "norm_g": f(inp["norm_g"])[0],
        "w_in": w_in, "m_norm_g": f(inp["m_norm_g"])[0], "sink": f(inp["attn_sink"])[0],
        "w_out_m": f(inp["w_out_m"])[0], "w_out_a": f(inp["w_out_a"])[0], "w_o": f(inp["w_o"])[0],
        "w_ff1": f(inp["w_ff1"])[0], "w_ff2": f(inp["w_ff2"])[0],
    }
    cf, cb = _consts()
    shared["cf"] = cf
    shared["cb"] = cb
    wg = w_in[:, O_GM:O_GM + 32]
    gb = f(inp["m_gate_b"])[0].reshape(32)
    swap = np.concatenate([np.arange(16, 32), np.arange(0, 16)])
    maps = []
    for core in cores:
        b, hh = core // 2, core % 2
        loc = np.arange(2048) if hh == 0 else (2047 - np.arange(2048))
        xb = x[b][loc]
        cx = ctx[b] if hh == 0 else ctx[b][::-1]
        m = dict(shared)
        m["x_own"] = np.ascontiguousarray(xb[:T])
        m["x_pre"] = np.ascontiguousarray(np.concatenate([xb[T:], cx], axis=0))
        m["cvec"] = np.ascontiguousarray(np.stack([c[b], c_ctx], axis=0).reshape(2, KC, 128).transpose(2, 1, 0))
        if hh == 0:
            m["w_gate"] = np.ascontiguousarray(wg)
            m["gate_b"] = np.ascontiguousarray(gb)
        else:
            m["w_gate"] = np.ascontiguousarray(wg[:, swap])
            m["gate_b"] = np.ascontiguousarray(gb[swap])
        m["ropec"] = np.ascontiguousarray(_rope_tables(loc[:T + 128]))
        maps.append(m)
    return maps


def kernel(**inputs):
    nc = build_program()
    maps = prep_inputs(inputs)
    res = run_bass_kernel_spmd(nc, maps, core_ids=list(range(8)))
    outp = np.empty((4, 2048, D), np.float32)
    for core in range(8):
        b, hh = core // 2, core % 2
        o = res.results[core]["out"]
        if hh == 0:
            outp[b, :T] = o
        else:
            outp[b, T:] = o[::-1]
    return outp
```

```python
import contextlib
import numpy as np
import ml_dtypes
import concourse.bass as bass
import concourse.mybir as mybir
from concourse.bass_utils import run_bass_kernel_spmd

F32 = mybir.dt.float32
BF16 = mybir.dt.bfloat16
AF = mybir.ActivationFunctionType
ALU = mybir.AluOpType
AX = mybir.AxisListType

ENGS = ("pe", "act", "dve", "pool", "sp")

D = 4096
KC = 32
T = 1024
NT = 8
TP = 1280
NTP = 10
DFF = 16384
EPS = 1e-6
O_QM, O_KM, O_VM, O_OM, O_GM, O_QA, O_KA, O_VA, O_BM, O_BA, O_END = (
    0, 2048, 4096, 8192, 12288, 12320, 16416, 17440, 18464, 22560, 26656)


class Prog:
    def __init__(self, nc):
        self.nc = nc
        self.q = {e: [] for e in ENGS}
        self.cnt = {}
        self.waited = {e: {} for e in ENGS}
        self.sems = {}
        self.dma_keys = []

    def _waits(self, eng, waits):
        out = []
        w = self.waited[eng]
        for t in waits:
            if t is None:
                continue
            k, v = t
            if v <= 0:
                continue
            if w.get(k, 0) >= v:
                continue
            w[k] = v
            out.append((k, v))
        return out

    def op(self, eng, fn, waits=(), inc=True):
        ws = self._waits(eng, waits)
        if inc:
            self.cnt[eng] = self.cnt.get(eng, 0) + 1
        self.q[eng].append((ws, fn, eng if inc else None, 1))
        return (eng, self.cnt.get(eng, 0))

    def dma(self, eng, semkey, out, in_, waits=()):
        ws = self._waits(eng, waits)
        if semkey not in self.cnt:
            self.cnt[semkey] = 0
            self.dma_keys.append(semkey)
        self.cnt[semkey] += 16

        def fn(e, out=out, in_=in_):
            return e.dma_start(out=out, in_=in_)
        self.q[eng].append((ws, fn, semkey, 16))
        return (semkey, self.cnt[semkey])


    def mm(self, out, lhsT, rhs, start=True, stop=True, waits=(), inc=True):
        return self.op("pe", lambda e: e.matmul(out, lhsT=lhsT, rhs=rhs, start=start, stop=stop), waits, inc)

    def tr(self, out, in_, ident, waits=(), inc=True):
        return self.op("pe", lambda e: e.transpose(out, in_, ident), waits, inc)

    def act(self, out, in_, func, scale=1.0, bias=0.0, accum_out=None, waits=(), inc=True):
        def fn(e):
            kw = {}
            if accum_out is not None:
                kw["accum_out"] = accum_out
            return e.activation(out=out, in_=in_, func=func, bias=bias, scale=scale, **kw)
        return self.op("act", fn, waits, inc)

    def tt(self, out, in0, in1, op, waits=(), eng="dve", inc=True):
        return self.op(eng, lambda e: e.tensor_tensor(out=out, in0=in0, in1=in1, op=op), waits, inc)

    def ts(self, out, in0, s1, op0, s2=None, op1=None, waits=(), eng="dve", inc=True, accum_out=None):
        def fn(e):
            kw = {}
            if op1 is not None:
                kw["op1"] = op1
            if accum_out is not None:
                kw["accum_out"] = accum_out
            return e.tensor_scalar(out=out, in0=in0, scalar1=s1, scalar2=s2, op0=op0, **kw)
        return self.op(eng, fn, waits, inc)

    def stt(self, out, in0, scalar, in1, op0, op1, waits=(), inc=True):
        return self.op("dve", lambda e: e.scalar_tensor_tensor(out=out, in0=in0, scalar=scalar, in1=in1, op0=op0, op1=op1), waits, inc)

    def cp(self, out, in_, waits=(), eng="dve", inc=True):
        return self.op(eng, lambda e: e.tensor_copy(out=out, in_=in_), waits, inc)

    def recip(self, out, in_, waits=(), inc=True):
        return self.op("dve", lambda e: e.reciprocal(out=out, in_=in_), waits, inc)

    def fence(self, eng, src):
        dst = self.fscr[eng]
        if eng == "act":
            return self.op("act", lambda e: e.activation(out=dst, in_=src, func=AF.Copy, bias=0.0, scale=1.0))
        return self.op(eng, lambda e: e.tensor_copy(out=dst, in_=src))

    def barrier(self):
        toks = [(k, v) for k, v in self.cnt.items()]
        for e in ENGS:
            ws = self._waits(e, toks)
            if ws:
                self.q[e].append((ws, None, None, 0))

    def tok(self, key):
        return (key, self.cnt.get(key, 0))

    def emit(self, final_waits):
        nc = self.nc
        keys = list(ENGS) + self.dma_keys
        with contextlib.ExitStack() as st:
            for k in keys:
                self.sems[k] = st.enter_context(nc.semaphore("s_" + k))
            block = st.enter_context(nc.Block())
            engmap = {"pe": block.tensor, "act": block.scalar, "dve": block.vector,
                      "pool": block.gpsimd, "sp": block.sync}
            self.q["sp"].append((self._waits("sp", final_waits), None, None, 0))

            def mk(ename):
                def body(e):
                    for ws, fn, sk, n in self.q[ename]:
                        for (k, v) in ws:
                            e.wait_ge(self.sems[k], v)
                        if fn is None:
                            continue
                        ins = fn(e)
                        if sk is not None:
                            ins.then_inc(self.sems[sk], n)
                return body
            for ename in ENGS:
                engmap[ename](mk(ename))


class Ring:
    def __init__(self, tiles, name):
        self.tiles = tiles
        self.free = [[] for _ in tiles]
        self.i = 0
        self.name = name

    def next(self):
        s = self.i % len(self.tiles)
        self.i += 1
        fr = self.free[s]
        self.free[s] = []
        return s, self.tiles[s], fr

    def release(self, s, tok):
        self.free[s].append(tok)


class WStream:
    def __init__(self, P, ring):
        self.P = P
        self.ring = ring
        self.pending = []

    def fetch(self, src_ap, view):
        s, t, fr = self.ring.next()
        tok = self.P.dma("pool", f"{self.ring.name}{s}", view(t), src_ap, waits=fr)
        return s, t, tok


def build_program(debug=False, stop_after=None):
    nc = bass.Bass("TRN2", target_bir_lowering=False)
    kind_s = "ExternalOutput" if debug else "Internal"

    early = stop_after in ("S1", "G1", "G2", "M", "A")
    skip_names = ("w_out_m", "w_out_a", "w_o", "w_ff1", "w_ff2") if early else ()

    def din(name, shape, dt=F32):
        if name in skip_names:
            nc.dram_tensor(name + "_dummy", [128, 128], dt, kind="ExternalInput")
            return None
        return nc.dram_tensor(name, list(shape), dt, kind="ExternalInput").ap()

    def dscr(name, shape, dt=F32):
        return nc.dram_tensor(name, list(shape), dt, kind=kind_s).ap()

    x_own = din("x_own", [T, D])
    x_pre = din("x_pre", [TP, D])
    cvec = din("cvec", [128, KC, 2])
    w_mod = din("w_mod", [D, 6 * D])
    b_mod = din("b_mod", [6 * D])
    norm_g = din("norm_g", [4, D])
    w_in = din("w_in", [D, O_END])
    w_gate = din("w_gate", [D, 32])
    gate_b = din("gate_b", [32])
    m_norm_g = din("m_norm_g", [D])
    sink = din("sink", [32])
    w_out_m = din("w_out_m", [D, D])
    w_out_a = din("w_out_a", [D, D])
    w_o = din("w_o", [D, D])
    w_ff1 = din("w_ff1", [D, DFF])
    w_ff2 = din("w_ff2", [DFF, D])
    cf = din("cf", [128, 5 * 128])
    cb = din("cb", [128, 2 * 128 + 2 * 512], BF16)
    ropec = din("ropec", [128, 2, T + 128])

    out = nc.dram_tensor("out", [T, D], F32, kind="ExternalOutput").ap()

    modrow = dscr("modrow", [2, 6 * D])
    QmT = dscr("QmT", [2048, T], BF16)
    KmT = dscr("KmT", [2048, T], BF16)
    Km = dscr("Km", [T, 2048], BF16)
    Vm = dscr("Vm", [T, D], BF16)
    Gm = dscr("Gm", [T, D])
    QaT = dscr("QaT", [D, T], BF16)
    KaT = dscr("KaT", [1024, T + 384], BF16)
    Va = dscr("Va", [T + 384, 1024], BF16)
    sgmT = dscr("sgmT", [D, T])
    sgaT = dscr("sgaT", [D, T])
    Kp = dscr("Kp", [TP, 2048], BF16)
    Vp = dscr("Vp", [TP, D], BF16)
    hmT = dscr("hmT", [D, T], BF16)
    haT = dscr("haT", [D, T], BF16)
    t1T = dscr("t1T", [D, T])
    mixd = dscr("mixd", [T, D])
    hbuf = dscr("hbuf", [T, D])
    h1T = dscr("h1T", [DFF, T], BF16)
    ffn = dscr("ffn", [T, D])
    dbgden = dscr("dbgden", [8, 2, 8, 128, 8]) if debug else None

    P = Prog(nc)
    st = contextlib.ExitStack()
    with st:
        def sb(name, shape, dt=F32):
            return st.enter_context(nc.sbuf_tensor(name, list(shape), dt))

        ps = [st.enter_context(nc.psum_tensor(f"ps{i}", [128, 512], F32)) for i in range(8)]
        psfree = [[] for _ in range(8)]

        def ps_take(i):
            fr = psfree[i]
            psfree[i] = []
            return fr

        cf_t = sb("cf_t", [128, 640])
        cb_t = sb("cb_t", [128, 1280], BF16)
        t_cf = P.dma("sp", "c0", cf_t[:], cf)
        t_cb = P.dma("sp", "c1", cb_t[:], cb)
        ident_f = cf_t[:, 0:128]
        triU = cf_t[:, 128:256]
        triL = cf_t[:, 256:384]
        ones_f = cf_t[:, 384:512]
        RT = cf_t[:, 512:640]
        ident_b = cb_t[:, 0:128]
        ones_b = cb_t[:, 128:256]
        maskU4 = cb_t[:, 256:768]
        maskL4 = cb_t[:, 768:1280]
        CONST = [t_cf, t_cb]
        fscr_t = sb("fscr", [128, 8])
        P.fscr = {"act": fscr_t[:, 0:1], "dve": fscr_t[:, 2:3], "pool": fscr_t[:, 4:5]}

        wring = Ring([sb(f"wr{i}", [128, KC, 256], BF16) for i in range(3)], "w")
        WS = WStream(P, wring)

        def wblock(W, c0, ncols=256, r0=0):
            src = W[r0:r0 + D, c0:c0 + ncols].rearrange("(kc p) n -> p kc n", p=128)
            return WS.fetch(src, lambda t: t[:, :, 0:ncols])

        sc0 = contextlib.ExitStack()
        sb0 = lambda name, shape, dt=F32: sc0.enter_context(nc.sbuf_tensor(name, list(shape), dt))
        sc32 = sb0("sc32", [128, KC, 2])
        scb = sb0("scb", [128, KC, 2], BF16)
        sig = sb0("sig_tmp", [128, KC, 2])
        t_c = P.dma("sp", "c2", sc32[:], cvec)
        t_sg = P.op("act", lambda e: e.activation(out=sig[:], in_=sc32[:], func=AF.Sigmoid), waits=[t_c])
        t_scb = P.op("dve", lambda e: e.tensor_tensor(out=scb[:], in0=sc32[:], in1=sig[:], op=ALU.mult), waits=[t_sg])
        mrow = [sb0(f"mrow{i}", [2, 512]) for i in range(2)]
        brow = [sb0(f"brow{i}", [2, 512]) for i in range(2)]
        mrow_free = [[], []]
        wview = lambda t: t[:].rearrange("p a b -> p (a b)").rearrange("p (c d) -> p c d", c=16)
        for u in range(6 * D // 512):
            j = u % 2
            pi = j
            fr = ps_take(pi)
            tb = P.dma("sp", f"brow{j}", brow[j][:], b_mod[u * 512:(u + 1) * 512].partition_broadcast(2),
                       waits=mrow_free[j])
            tmm = None
            for kh in range(2):
                src = w_mod[kh * 2048:(kh + 1) * 2048, u * 512:(u + 1) * 512].rearrange("(kc p) n -> p kc n", p=128)
                s, wt, tw = WS.fetch(src, wview)
                wv = wview(wt)
                for k2 in range(16):
                    kc = kh * 16 + k2
                    tmm = P.mm(ps[pi][0:2, :], scb[:, kc, :], wv[:, k2, :], start=(kc == 0), stop=(kc == KC - 1),
                               waits=([t_scb, tw] + (fr if kh == 0 else [])) if k2 == 0 else [], inc=(k2 == 15))
                wring.release(s, tmm)
            ta = P.tt(mrow[j][:], ps[pi][0:2, :], brow[j][:], ALU.add, waits=[tmm, tb] + mrow_free[j])
            psfree[pi].append(ta)
            td = P.dma("sp", "modst", modrow[:, u * 512:(u + 1) * 512], mrow[j][:], waits=[ta])
            mrow_free[j] = [td]
        T_MOD = P.tok("modst")
        P.barrier()
        sc0.close()

        if stop_after == "S1":
            P.emit([T_MOD])
            return nc


        P.barrier()

        def alloc(sc, name, shape, dt=F32):
            return sc.enter_context(nc.sbuf_tensor(name, list(shape), dt))

        gates_all = sb("gates_all", [128, 18, 32])
        small = sb("small", [128, 64])
        gemm_banks = [0, 1, 2, 3]
        gb_i = [0]

        def next_bank(bset=gemm_banks, ctr=gb_i):
            b = bset[ctr[0] % len(bset)]
            ctr[0] += 1
            return b, ps_take(b)

        ev_i = [0]
        ev_banks = [4, 5, 6, 7]

        def build_uT(sc, actT, src, ntiles, mod_r, lat_tiles, sh_col, s_col, ng_row, pfx):
            a_bc = alloc(sc, pfx + "a_bc", [128, D])
            sh_bc = alloc(sc, pfx + "sh_bc", [128, D])
            xt = [alloc(sc, pfx + f"xt{i}", [128, D]) for i in range(2)]
            ub = alloc(sc, pfx + "ub", [128, D], BF16)
            ssq = alloc(sc, pfx + "ssq", [128, 4])
            xfree = [[], []]
            bc_tok = None
            ub_free = []
            last = []
            for tt in range(ntiles):
                r = 0 if tt < lat_tiles else 1
                if tt == 0 or tt == lat_tiles:
                    w0 = last + [T_MOD]
                    t1_ = P.dma("sp", pfx + "bc0", a_bc[:], modrow[r, s_col * D:(s_col + 1) * D].partition_broadcast(128), waits=w0)
                    t2_ = P.dma("sp", pfx + "bc1", sh_bc[:], modrow[r, sh_col * D:(sh_col + 1) * D].partition_broadcast(128), waits=w0)
                    t3_ = P.dma("sp", pfx + "bc2", xt[1][:], norm_g[ng_row, :].partition_broadcast(128), waits=w0 + xfree[1])
                    bc_tok = P.stt(a_bc[:], a_bc[:], 1.0, xt[1][:], ALU.add, ALU.mult, waits=[t1_, t3_])
                    xfree[1] = [bc_tok]
                    bc_tok2 = t2_
                xs = tt % 2
                tl = P.dma("sp", pfx + f"x{xs}", xt[xs][:], src[tt * 128:(tt + 1) * 128, :], waits=xfree[xs])
                tsq = P.act(ub[:], xt[xs][:], AF.Square, accum_out=ssq[:, 0:1], waits=[tl] + ub_free)
                tsq = P.fence("act", ssq[:, 0:1])
                tms = P.ts(ssq[:, 1:2], ssq[:, 0:1], 1.0 / D, ALU.mult, EPS, ALU.add, waits=[tsq])
                tms = P.fence("dve", ssq[:, 1:2])
                tq = P.act(ssq[:, 2:3], ssq[:, 1:2], AF.Sqrt, waits=[tms])
                tq = P.fence("act", ssq[:, 2:3])
                trc = P.recip(ssq[:, 3:4], ssq[:, 2:3], waits=[tq])
                trc = P.fence("dve", ssq[:, 3:4])
                tst = P.stt(xt[xs][:], xt[xs][:], ssq[:, 3:4], a_bc[:], ALU.mult, ALU.mult, waits=[bc_tok, trc])
                tu = P.tt(ub[:], xt[xs][:], sh_bc[:], ALU.add, waits=[bc_tok2, tst])
                xfree[xs] = [tu]
                for q4 in range(4):
                    b, fr = next_bank()
                    pb = ps[b][:].bitcast(BF16)
                    for i in range(8):
                        kc = q4 * 8 + i
                        tp = P.tr(pb[:, i * 128:(i + 1) * 128], ub[:, kc * 128:(kc + 1) * 128], ident_b,
                                  waits=[tu] + fr + CONST if i == 0 else [], inc=(i == 7))
                    dst = actT[:, q4 * 8:(q4 + 1) * 8, tt * 128:(tt + 1) * 128]
                    srcp = pb.rearrange("p (a b) -> p a b", a=8)
                    if q4 % 2 == 0:
                        te = P.act(dst, srcp, AF.Copy, waits=[tp])
                    else:
                        te = P.cp(dst, srcp, waits=[tp])
                    psfree[b].append(te)
                    last = [te]
                ub_free = [tp]
            return last

        class Stage:
            def __init__(self, sc, name, shape, dt, n):
                self.ring = Ring([alloc(sc, f"{name}{i}", shape, dt) for i in range(n)], name)

            def get(self):
                return self.ring.next()

            def store(self, s, dst, src, tok, key):
                td = P.dma("sp", key, dst, src, waits=[tok])
                self.ring.release(s, td)
                return td

        def gemm_fm(W, c0, ncols, actT, tsegs, act_toks, evac):
            nblk = (ncols + 255) // 256
            for bi in range(nblk):
                nb = min(256, ncols - bi * 256)
                s, wt, tw = wblock(W, c0 + bi * 256, nb)
                tlast = None
                for j in range(nb // 128):
                    for si, segs in enumerate(tsegs):
                        b, fr = next_bank()
                        first = True
                        for (t0, n, o0) in segs:
                            for kc in range(KC):
                                tlast = P.mm(ps[b][:, o0:o0 + n], wt[:, kc, j * 128:(j + 1) * 128], actT[:, kc, t0:t0 + n],
                                             start=(kc == 0), stop=(kc == KC - 1),
                                             waits=([tw] + fr + act_toks) if first else [], inc=(kc == KC - 1))
                                first = False
                        toks = evac(b, bi * 2 + j, si, tlast)
                        psfree[b].extend(toks)
                wring.release(s, tlast)

        def gemm_tm(W, c0, ncols, actT, tiles, act_toks, evac, blk=256):
            nblk = (ncols + blk - 1) // blk
            for bi in range(nblk):
                nb = min(blk, ncols - bi * blk)
                s, wt, tw = wblock(W, c0 + bi * blk, nb)
                tlast = None
                for tt in tiles:
                    b, fr = next_bank()
                    for kc in range(KC):
                        tlast = P.mm(ps[b][:, 0:nb], actT[:, kc, tt * 128:(tt + 1) * 128], wt[:, kc, 0:nb],
                                     start=(kc == 0), stop=(kc == KC - 1),
                                     waits=([tw] + fr + act_toks) if kc == 0 else [], inc=(kc == KC - 1))
                    toks = evac(b, bi * blk, nb, tt, tlast)
                    psfree[b].extend(toks)
                wring.release(s, tlast)

        def rope_evac(b, n, tok, xq_st, tmp_st, out_st, cs_t, c0, scale, dst, key):
            s1, xq, fr1 = xq_st.get()
            t_x = P.act(xq[:, 0:n], ps[b][:, 0:n], AF.Copy, scale=scale, waits=[tok] + fr1)
            eb = ev_banks[ev_i[0] % 4]
            ev_i[0] += 1
            fr = ps_take(eb)
            t_r = P.mm(ps[eb][:, 0:n], RT, xq[:, 0:n], waits=[t_x] + fr + CONST)
            s2, tmp, fr2 = tmp_st.get()
            t_a = P.tt(tmp[:, 0:n], xq[:, 0:n], cs_t[:, 0, c0:c0 + n], ALU.mult, waits=[t_x] + fr2)
            xq_st.ring.release(s1, t_r)
            xq_st.ring.release(s1, t_a)
            s3, tmp2, fr3 = tmp_st.get()
            t_b = P.tt(tmp2[:, 0:n], ps[eb][:, 0:n], cs_t[:, 1, c0:c0 + n], ALU.mult, waits=[t_r] + fr3)
            psfree[eb].append(t_b)
            s4, ob, fr4 = out_st.get()
            t_o = P.tt(ob[:, 0:n], tmp[:, 0:n], tmp2[:, 0:n], ALU.add, waits=fr4 + [t_a, t_b])
            tmp_st.ring.release(s2, t_o)
            tmp_st.ring.release(s3, t_o)
            out_st.store(s4, dst, ob[:, 0:n], t_o, key)
            return [t_x]

        with contextlib.ExitStack() as sc1:
            actP = alloc(sc1, "actP", [128, KC, TP], BF16)
            with contextlib.ExitStack() as sc:
                t_act = build_uT(sc, actP, x_pre, NTP, 0, 8, 0, 1, 0, "p_")
            P.barrier()
            with contextlib.ExitStack() as sc:
                stB = Stage(sc, "g1b", [128, 512], BF16, 4)
                xq_st = Stage(sc, "g1xq", [128, 512], F32, 2)
                tmp_st = Stage(sc, "g1tmp", [128, 512], F32, 4)
                cs_t = alloc(sc, "g1cs", [128, 2, T + 128])
                gbias = alloc(sc, "g1gb", [128, 32])
                t_cs = P.dma("sp", "g1c", cs_t[:], ropec)
                t_gb = P.dma("sp", "g1c", gbias[:], gate_b.partition_broadcast(128))
                T_G1C = P.tok("g1c")

                def ev_gate(b, c0, nb, tt, tok):
                    return [P.tt(gates_all[:, 8 + tt, :], ps[b][:, 0:32], gbias[:], ALU.add, waits=[tok, T_G1C])]
                gemm_tm(w_gate, 0, 32, actP, range(NTP), [], ev_gate)

                def ev_tm_bf(dst_dram, key, use_act=True):
                    def ev(b, c0, nb, tt, tok):
                        s, ob, fr = stB.get()
                        if (tt % 2) == 0:
                            t = P.act(ob[:, 0:nb], ps[b][:, 0:nb], AF.Copy, waits=[tok] + fr)
                        else:
                            t = P.cp(ob[:, 0:nb], ps[b][:, 0:nb], waits=[tok] + fr)
                        stB.store(s, dst_dram(tt, c0, nb), ob[:, 0:nb], t, key)
                        return [t]
                    return ev
                gemm_tm(w_in, O_KM, 2048, actP, range(NTP), [],
                        ev_tm_bf(lambda tt, c0, nb: Kp[tt * 128:(tt + 1) * 128, c0:c0 + nb], "stKp"))
                gemm_tm(w_in, O_VM, 4096, actP, range(NTP), [],
                        ev_tm_bf(lambda tt, c0, nb: Vp[tt * 128:(tt + 1) * 128, c0:c0 + nb], "stVp"))
                vmap = {0: 0, 8: 1, 9: 2}
                gemm_tm(w_in, O_VA, 1024, actP, [0, 8, 9], [],
                        ev_tm_bf(lambda tt, c0, nb: Va[T + vmap[tt] * 128:T + (vmap[tt] + 1) * 128, c0:c0 + nb], "stVa"))

                def ev_ka(b, ch, si, tok):
                    toks = rope_evac(b, 128, tok, xq_st, tmp_st, stB, cs_t, T, 1.0,
                                     KaT[ch * 128:(ch + 1) * 128, T:T + 128], "stKa")
                    s, ob, fr = stB.get()
                    t = P.cp(ob[:, 0:256], ps[b][:, 128:384], waits=[tok] + fr)
                    stB.store(s, KaT[ch * 128:(ch + 1) * 128, T + 128:T + 384], ob[:, 0:256], t, "stKa")
                    return toks + [t]
                gemm_fm(w_in, O_KA, 1024, actP, [[(0, 128, 0), (1024, 256, 128)]], [], ev_ka)
            P.barrier()

        if stop_after == "G1":
            P.emit([P.tok(k) for k in ("stKp", "stVp", "stVa", "stKa")])
            return nc

        with contextlib.ExitStack() as sc1:
            actO = alloc(sc1, "actO", [128, KC, T], BF16)
            with contextlib.ExitStack() as sc:
                build_uT(sc, actO, x_own, NT, 0, NT, 0, 1, 0, "o_")
            P.barrier()
            with contextlib.ExitStack() as sc:
                stB = Stage(sc, "g2b", [128, 512], BF16, 4)
                stF = Stage(sc, "g2f", [128, 512], F32, 4)
                xq_st = Stage(sc, "g2xq", [128, 512], F32, 2)
                tmp_st = Stage(sc, "g2tmp", [128, 512], F32, 4)
                cs_t = alloc(sc, "g2cs", [128, 2, T + 128])
                gbias = alloc(sc, "g2gb", [128, 32])
                mng = alloc(sc, "g2mng", [128, D])
                P.dma("sp", "g2c", cs_t[:], ropec)
                P.dma("sp", "g2c", gbias[:], gate_b.partition_broadcast(128))
                P.dma("sp", "g2c", mng[:], m_norm_g.partition_broadcast(128))
                T_G2C = P.tok("g2c")
                segs2 = [[(0, 512, 0)], [(512, 512, 0)]]

                def ev_gate2(b, c0, nb, tt, tok):
                    return [P.tt(gates_all[:, tt, :], ps[b][:, 0:32], gbias[:], ALU.add, waits=[tok, T_G2C])]
                gemm_tm(w_gate, 0, 32, actO, range(NT), [], ev_gate2)

                def ev_fm_bf(dstT, scale, key):
                    def ev(b, ch, si, tok):
                        s, ob, fr = stB.get()
                        if (ch + si) % 2 == 0:
                            t = P.act(ob[:], ps[b][:], AF.Copy, scale=scale, waits=[tok] + fr)
                        else:
                            t = P.ts(ob[:], ps[b][:], scale, ALU.mult, waits=[tok] + fr)
                        stB.store(s, dstT[ch * 128:(ch + 1) * 128, si * 512:(si + 1) * 512], ob[:], t, key)
                        return [t]
                    return ev

                def ev_fm_sig(dstT, key):
                    def ev(b, ch, si, tok):
                        s, ob, fr = stF.get()
                        t = P.act(ob[:], ps[b][:], AF.Sigmoid, waits=[tok] + fr)
                        stF.store(s, dstT[ch * 128:(ch + 1) * 128, si * 512:(si + 1) * 512], ob[:], t, key)
                        return [t]
                    return ev

                def ev_fm_rope(dstT, scale, key):
                    def ev(b, ch, si, tok):
                        return rope_evac(b, 512, tok, xq_st, tmp_st, stB, cs_t, si * 512, scale,
                                         dstT[ch * 128:(ch + 1) * 128, si * 512:(si + 1) * 512], key)
                    return ev

                def ev_tm_bf2(dst, key):
                    def ev(b, c0, nb, tt, tok):
                        s, ob, fr = stB.get()
                        if (tt % 2) == 0:
                            t = P.act(ob[:, 0:nb], ps[b][:, 0:nb], AF.Copy, waits=[tok] + fr)
                        else:
                            t = P.cp(ob[:, 0:nb], ps[b][:, 0:nb], waits=[tok] + fr)
                        stB.store(s, dst[tt * 128:(tt + 1) * 128, c0:c0 + nb], ob[:, 0:nb], t, key)
                        return [t]
                    return ev

                def ev_tm_og(b, c0, nb, tt, tok):
                    s, ob, fr = stF.get()
                    t = P.act(ob[:, 0:nb], ps[b][:, 0:nb], AF.Sigmoid, waits=[tok] + fr)
                    t2 = P.tt(ob[:, 0:nb], ob[:, 0:nb], mng[:, c0:c0 + nb], ALU.mult, waits=[t, T_G2C])
                    stF.store(s, Gm[tt * 128:(tt + 1) * 128, c0:c0 + nb], ob[:, 0:nb], t2, "stGm")
                    return [t]

                gemm_fm(w_in, O_QM, 2048, actO, segs2, [], ev_fm_bf(QmT, 1.0 / 16.0, "stQm"))
                gemm_fm(w_in, O_KM, 2048, actO, segs2, [], ev_fm_bf(KmT, 1.0, "stKmT"))
                gemm_tm(w_in, O_KM, 2048, actO, range(NT), [], ev_tm_bf2(Km, "stKm"))
                gemm_tm(w_in, O_VM, 4096, actO, range(NT), [], ev_tm_bf2(Vm, "stVm"))
                gemm_tm(w_in, O_OM, 4096, actO, range(NT), [], ev_tm_og)
                gemm_fm(w_in, O_QA, 4096, actO, segs2, [], ev_fm_rope(QaT, 128.0 ** -0.5, "stQa"))
                gemm_fm(w_in, O_KA, 1024, actO, segs2, [], ev_fm_rope(KaT, 1.0, "stKa"))
                gemm_tm(w_in, O_VA, 1024, actO, range(NT), [], ev_tm_bf2(Va, "stVa"))
                gemm_fm(w_in, O_BM, 4096, actO, segs2, [], ev_fm_sig(sgmT, "stSgm"))
                gemm_fm(w_in, O_BA, 4096, actO, segs2, [], ev_fm_sig(sgaT, "stSga"))
            P.barrier()

        if stop_after == "G2":
            P.emit([P.tok(k) for k in P.dma_keys if k.startswith("st")])
            return nc
        with contextlib.ExitStack() as sc:
            NG = 2 * 18 * 8
            L1 = alloc(sc, "m_L1", [128, 18, 32])
            Call = alloc(sc, "m_C", [128, 2, 18, 8])
            CLall = alloc(sc, "m_CL", [128, 2, 18, 8])
            Aall = alloc(sc, "m_A", [128, 2, 18, 8])
            MEAN = alloc(sc, "m_MEAN", [128, 2, 18, 8])
            MU = alloc(sc, "m_MU", [128, 2, 18, 8])
            DD = alloc(sc, "m_DD", [128, 2, 18, 8])
            WEX = alloc(sc, "m_WEX", [128, 2, 18, 8])
            WEXb = alloc(sc, "m_WEXb", [128, 2, 18, 8], BF16)
            LB = alloc(sc, "m_LB", [128, 2, 18, 8])
            DEC = alloc(sc, "m_DEC", [128, 2, 18, 8])
            mcur = [alloc(sc, f"m_mcur{i}", [128, 8]) for i in range(2)]
            flat = lambda t: t[:].rearrange("p a b c -> p (a b c)")
            t0_ = P.act(L1[:], gates_all[:], AF.Exp, scale=-1.0)
            t_l1 = P.act(L1[:], L1[:], AF.Ln, bias=1.0, waits=[t0_])
            t_z1 = P.op("dve", lambda e: e.memset(MU[:], 0.0))
            t_z2 = P.op("dve", lambda e: e.memset(DD[:], 0.0))
            frA = ps_take(0) + ps_take(1) + ps_take(2)
            tl = None
            for d in range(2):
                tri = triU if d == 0 else triL
                for i in range(18):
                    o = (d * 18 + i) * 8
                    rhs = L1[:, i, d * 16 + 8:d * 16 + 16]
                    P.mm(ps[0][:, o:o + 8], tri, rhs, waits=[t_l1] + frA + CONST, inc=False)
                    tl = P.mm(ps[1][:, o:o + 8], ones_f, rhs)
            t_c = P.cp(flat(Call), ps[0][:, 0:NG], waits=[tl])
            t_cl = P.cp(flat(CLall), ps[1][:, 0:NG], waits=[tl])
            ta_ = None
            for d in range(2):
                ta_ = P.tt(Aall[:, d, :, :], gates_all[:, :, d * 16:d * 16 + 8], Call[:, d, :, :], ALU.add, waits=[t_c])
            tmn = P.mm(ps[2][:, 0:NG], ones_f, flat(Aall), waits=[ta_])
            t_mean = P.ts(flat(MEAN), ps[2][:, 0:NG], 1.0 / 128.0, ALU.mult, waits=[tmn])
            psfree[0].append(t_c); psfree[1].append(t_cl); psfree[2].append(t_mean)
            seqs = {0: [16, 17] + list(range(8)), 1: [17, 16] + list(range(15, 7, -1)) + list(range(7, -1, -1))}
            tlast = [t_mean, t_cl, t_z1, t_z2]
            for d in range(2):
                tprev = P.op("dve", lambda e: e.memset(mcur[0][:], 0.0))
                cur = 0
                for i in seqs[d]:
                    tmu = P.tt(MU[:, d, i, :], mcur[cur][:], MEAN[:, d, i, :], ALU.max, waits=[tprev] + tlast)
                    P.tt(DD[:, d, i, :], mcur[cur][:], MU[:, d, i, :], ALU.subtract, waits=[tmu])
                    tprev = P.tt(mcur[1 - cur][:], MU[:, d, i, :], CLall[:, d, i, :], ALU.subtract, waits=[tmu])
                    cur = 1 - cur
                    tlast = []
            t_dd = P.tok("dve")
            t_s1 = P.tt(flat(WEX), flat(Aall), flat(MU), ALU.subtract, waits=[t_dd])
            t_s2 = P.tt(flat(LB), flat(Call), flat(MU), ALU.subtract, waits=[t_dd])
            t_e1 = P.act(flat(WEX), flat(WEX), AF.Exp, waits=[t_s1])
            t_e2 = P.act(flat(LB), flat(LB), AF.Exp, waits=[t_s2])
            t_e3 = P.act(flat(DEC), flat(DD), AF.Exp, waits=[t_dd])
            t_wb = P.cp(flat(WEXb), flat(WEX), waits=[t_e1])
            GATES = [t_e1, t_e2, t_e3, t_wb]

            QT = [alloc(sc, f"m_QT{i}", [128, 2, T], BF16) for i in range(2)]
            KT = [alloc(sc, f"m_KT{i}", [128, 2, T], BF16) for i in range(2)]
            KO = [alloc(sc, f"m_KO{i}", [128, NT, 256], BF16) for i in range(2)]
            VO = [alloc(sc, f"m_VO{i}", [128, NT, 512], BF16) for i in range(2)]
            KPp = [alloc(sc, "m_KP", [128, NTP, 256], BF16)] * 2
            VPp = [alloc(sc, "m_VP", [128, NTP, 512], BF16)] * 2
            GH = [alloc(sc, "m_G", [128, NT, 512])] * 2
            hbuf_free = [[], []]
            hF = alloc(sc, "m_hF", [128, NT, 512])
            hmTs = [alloc(sc, "m_hmTs", [128, 4, T], BF16)] * 2
            hmTs_free = [[]]
            C32 = [alloc(sc, f"m_C32{d}", [128, 2, 512]) for d in range(2)]
            Cb = [alloc(sc, f"m_Cb{d}", [128, 2, 512], BF16) for d in range(2)]
            n32 = [alloc(sc, f"m_n32{d}", [128, 2]) for d in range(2)]
            nbf = [alloc(sc, f"m_nb{d}", [128, 2], BF16) for d in range(2)]
            VPr = [Ring([alloc(sc, f"m_vp{d}{i}", [128, 512], BF16) for i in range(3)], f"vp{d}") for d in range(2)]
            SMr = [Ring([alloc(sc, f"m_sm{d}{i}", [128, 128], BF16) for i in range(2)], f"sm{d}") for d in range(2)]
            sml = [alloc(sc, f"m_sml{d}", [128, 8]) for d in range(2)]
            HSr = Ring([alloc(sc, f"m_hs{i}", [128, 512]) for i in range(2)], "hs")
            HMr = Ring([alloc(sc, f"m_hm{i}", [128, 512], BF16) for i in range(2)], "hm")
            junk = alloc(sc, "m_junk", [128, 512], BF16)

            def load_head(h):
                s = h % 2
                fr = hbuf_free[s]
                hbuf_free[s] = []
                k = f"mld{s}"
                P.dma("sp", k, QT[s][:], QmT[h * 256:(h + 1) * 256, :].rearrange("(dc p) t -> p dc t", p=128), waits=fr)
                P.dma("sp", k, KT[s][:], KmT[h * 256:(h + 1) * 256, :].rearrange("(dc p) t -> p dc t", p=128))
                P.dma("sp", k, KO[s][:], Km[:, h * 256:(h + 1) * 256].rearrange("(tt p) d -> p tt d", p=128))
                P.dma("sp", k, VO[s][:], Vm[:, h * 512:(h + 1) * 512].rearrange("(tt p) d -> p tt d", p=128))
                return P.tok(k)

            single_free = [[]]

            def load_head_single(h):
                fr = single_free[0]
                k = "mlds"
                P.dma("sp", k, KPp[0][:], Kp[:, h * 256:(h + 1) * 256].rearrange("(tt p) d -> p tt d", p=128), waits=fr)
                P.dma("sp", k, VPp[0][:], Vp[:, h * 512:(h + 1) * 512].rearrange("(tt p) d -> p tt d", p=128))
                P.dma("sp", k, GH[0][:], Gm[:, h * 512:(h + 1) * 512].rearrange("(tt p) d -> p tt d", p=128))
                return P.tok(k)

            reg_free = {}

            def rtake(key):
                fr = reg_free.get(key, [])
                reg_free[key] = []
                return fr
            for b in range(8):
                reg_free[("bank", b)] = ps_take(b)

            ld_tok = load_head(0)
            for h in range(8):
                s = h % 2
                lds_tok = load_head_single(h)
                nxt_tok = load_head(h + 1) if h + 1 < 8 else None
                LD = [ld_tok, lds_tok] + GATES
                state_tok = {0: [], 1: []}
                cb_read = {0: [], 1: []}
                last_use = []
                for k in range(18):
                    for d in range(2):
                        if k >= len(seqs[d]):
                            continue
                        i = seqs[d][k]
                        own = i < 8
                        first = (k == 0)
                        lastc = (k == len(seqs[d]) - 1)
                        bA, bB, bC, bD = 4 * d, 4 * d + 1, 4 * d + 2, 4 * d + 3
                        base_w = LD + (rtake(("bank", bA)) + rtake(("bank", bB)) + rtake(("bank", bC)) + rtake(("bank", bD)) if k == 0 else [])
                        if own:
                            K_tm, V_tm = KO[s][:, i, :], VO[s][:, i, :]
                        else:
                            j = i - 8
                            K_tm, V_tm = KPp[s][:, j, :], VPp[s][:, j, :]
                        wcol = WEX[:, d, i, h:h + 1]
                        wcolb = WEXb[:, d, i, h:h + 1]
                        dcol = DEC[:, d, i, h:h + 1]
                        lbcol = LB[:, d, i, h:h + 1]
                        sv, vp_t, frv = VPr[d].next()
                        t_vp = P.act(vp_t[:], V_tm, AF.Copy, scale=wcol, waits=base_w + frv)
                        vp_users = []
                        t_cb = None
                        if not first and own:
                            t_cb = P.act(Cb[d][:].rearrange("p a b -> p (a b)"), C32[d][:].rearrange("p a b -> p (a b)"),
                                         AF.Copy, scale=dcol, waits=state_tok[d] + cb_read[d])
                            t_nb = P.ts(nbf[d][:], n32[d][:], dcol, ALU.mult, waits=state_tok[d] + cb_read[d])
                            t_nb = P.fence("dve", nbf[d][:, 0:1])
                            cb_read[d] = []
                        if own:
                            c0 = i * 128
                            fr = rtake(("st", d))
                            for dc in range(2):
                                t_st = P.mm(ps[bA][:, 0:128], KT[s][:, dc, c0:c0 + 128], QT[s][:, dc, c0:c0 + 128],
                                            start=(dc == 0), stop=(dc == 1), waits=base_w + fr if dc == 0 else [], inc=(dc == 1))
                            ssm, sm_t, frs = SMr[d].next()
                            mask = maskU4[:, 0:128] if d == 0 else maskL4[:, 0:128]
                            t_sm = P.tt(sm_t[:], ps[bA][:, 0:128], mask, ALU.mult, waits=[t_st] + frs + CONST)
                            reg_free[("st", d)] = [t_sm]
                            fr = rtake(("in", d))
                            if not first:
                                for dc in range(2):
                                    P.mm(ps[bB][:, :], QT[s][:, dc, c0:c0 + 128], Cb[d][:, dc, :], start=(dc == 0), stop=False,
                                         waits=[t_cb] + fr if dc == 0 else [], inc=False)
                            t_in = P.mm(ps[bB][:, :], sm_t[:], vp_t[:], start=first, stop=True, waits=[t_sm, t_vp] + (fr if first else []))
                            fr = rtake(("den", d))
                            if not first:
                                for dc in range(2):
                                    P.mm(ps[bA][:, 128:129], QT[s][:, dc, c0:c0 + 128], nbf[d][:, dc:dc + 1], start=(dc == 0), stop=False,
                                         waits=[t_nb] + fr if dc == 0 else [], inc=False)
                            t_den = P.mm(ps[bA][:, 128:129], sm_t[:], wcolb, start=first, stop=True, waits=(fr if first else []))
                            SMr[d].release(ssm, t_den)
                            cb_read[d].append(t_den)
                            vp_users.append(t_in)
                            t_dm0 = P.ts(sml[d][:, 6:7], ps[bA][:, 128:129], -1.0, ALU.mult, lbcol, ALU.max, waits=[t_den] + ([P.tok(f"dbgden{d}")] if debug else []))
                            t_dm0f = P.fence("dve", sml[d][:, 6:7])
                            t_dm = P.tt(sml[d][:, 0:1], ps[bA][:, 128:129], sml[d][:, 6:7], ALU.max, waits=[t_dm0f])
                            t_dmf = P.fence("dve", sml[d][:, 0:1])
                            reg_free[("den", d)] = [t_dm]
                            t_r = P.recip(sml[d][:, 1:2], sml[d][:, 0:1], waits=[t_dmf])
                            t_r = P.fence("dve", sml[d][:, 1:2])
                            if debug:
                                t_cpd = P.cp(sml[d][:, 7:8], ps[bA][:, 128:129], waits=[t_den, t_r])
                                t_dbg = P.dma("sp", f"dbgden{d}", dbgden[h, d, i], sml[d][:, 0:8], waits=[t_cpd, t_r])
                                reg_free[("den", d)] = [t_dm, t_dbg]
                            if d == 0:
                                t_h = P.act(hF[:, i, :], ps[bB][:, :], AF.Copy, scale=sml[d][:, 1:2], waits=[t_in, t_r] + last_use)
                                reg_free[("in", d)] = [t_h]
                                hf_tok = t_h
                            else:
                                shs, hs_t, frh = HSr.next()
                                t_hs = P.stt(hs_t[:], ps[bB][:, :], sml[d][:, 1:2], hF[:, i, :], ALU.mult, ALU.add,
                                             waits=[t_in, t_r, P.tok("act")] + frh)
                                reg_free[("in", d)] = [t_hs]
                                t_sq = P.act(junk[:], hs_t[:], AF.Square, accum_out=sml[d][:, 2:3], waits=[t_hs])
                                t_sq = P.fence("act", sml[d][:, 2:3])
                                t_v = P.ts(sml[d][:, 3:4], sml[d][:, 2:3], 1.0 / 512.0, ALU.mult, EPS, ALU.add, waits=[t_sq])
                                t_v = P.fence("dve", sml[d][:, 3:4])
                                t_q = P.act(sml[d][:, 4:5], sml[d][:, 3:4], AF.Sqrt, waits=[t_v])
                                t_q = P.fence("act", sml[d][:, 4:5])
                                t_rs = P.recip(sml[d][:, 5:6], sml[d][:, 4:5], waits=[t_q])
                                t_rs = P.fence("dve", sml[d][:, 5:6])
                                shm, hm_t, frm = HMr.next()
                                t_hm = P.stt(hm_t[:], hs_t[:], sml[d][:, 5:6], GH[s][:, i, :], ALU.mult, ALU.mult, waits=[t_rs] + frm)
                                HSr.release(shs, t_hm)
                                last_use = [t_hm]
                                fr = rtake(("tr", d))
                                pb = ps[bA][:].bitcast(BF16)
                                for vc in range(4):
                                    t_tr = P.tr(pb[:, 512 + vc * 128:512 + (vc + 1) * 128], hm_t[:, vc * 128:(vc + 1) * 128], ident_b,
                                                waits=[t_hm] + fr if vc == 0 else [], inc=(vc == 3))
                                HMr.release(shm, t_tr)
                                t_ev = P.cp(hmTs[s][:, :, c0:c0 + 128], pb[:, 512:1024].rearrange("p (a b) -> p a b", a=4),
                                            waits=[t_tr] + hmTs_free[0])
                                hmTs_free[0] = []
                                reg_free[("tr", d)] = [t_ev]
                                hm_ev = t_ev
                        if not lastc:
                            fr = rtake(("upd", d))
                            for dc in range(2):
                                bU = bC if dc == 0 else bD
                                t_u = P.mm(ps[bU][:, :], K_tm[:, dc * 128:(dc + 1) * 128], vp_t[:], waits=[t_vp] + base_w + fr)
                            fr2 = rtake(("nu", d))
                            for dc in range(2):
                                t_nu = P.mm(ps[bA][:, 130 + 2 * dc:131 + 2 * dc], K_tm[:, dc * 128:(dc + 1) * 128], wcolb, waits=fr2 + base_w)
                            vp_users.append(t_u)
                            toks = []
                            if first:
                                toks.append(P.act(C32[d][:, 0, :], ps[bC][:, :], AF.Copy, waits=[t_u] + cb_read[d]))
                                toks.append(P.cp(C32[d][:, 1, :], ps[bD][:, :], waits=[t_u] + cb_read[d]))
                                toks.append(P.cp(n32[d][:].rearrange("p (a b) -> p a b", b=1), ps[bA][:, 130:134].rearrange("p (a b) -> p a b", b=2)[:, :, 0:1], waits=[t_nu]))
                            else:
                                wst = state_tok[d] + cb_read[d]
                                toks.append(P.stt(C32[d][:, 0, :], C32[d][:, 0, :], dcol, ps[bC][:, :], ALU.mult, ALU.add, waits=[t_u] + wst))
                                toks.append(P.stt(C32[d][:, 1, :], C32[d][:, 1, :], dcol, ps[bD][:, :], ALU.mult, ALU.add, waits=[t_u] + wst))
                                toks.append(P.stt(n32[d][:].rearrange("p (a b) -> p a b", b=1), n32[d][:].rearrange("p (a b) -> p a b", b=1), dcol,
                                                  ps[bA][:, 130:134].rearrange("p (a b) -> p a b", b=2)[:, :, 0:1], ALU.mult, ALU.add, waits=[t_nu] + wst))
                            cb_read[d] = []
                            state_tok[d] = toks
                            reg_free[("upd", d)] = [toks[0], toks[1]]
                            reg_free[("nu", d)] = [toks[2]]
                        for tkn in vp_users:
                            VPr[d].release(sv, tkn)
                td = P.dma("sp", f"sthm{s}", hmT[h * 512:(h + 1) * 512, :].rearrange("(vc p) t -> p vc t", p=128), hmTs[s][:], waits=[hm_ev])
                hmTs_free[0] = [td]
                hbuf_free[s] = [P.tok("pe"), P.tok("act"), P.tok("dve")]
                single_free[0] = [P.tok("pe"), P.tok("act"), P.tok("dve")]
                ld_tok = nxt_tok
            for b in range(8):
                psfree[b] = [P.tok("act"), P.tok("dve")]
            P.barrier()

        if stop_after == "M":
            P.emit([P.tok(k) for k in P.dma_keys if k.startswith("st")])
            return nc
        with contextlib.ExitStack() as sc:
            NK = T + 384
            KaS = [alloc(sc, f"a_K{i}", [128, NK], BF16) for i in range(2)]
            VaS = [alloc(sc, f"a_V{i}", [128, 11, 128], BF16) for i in range(2)]
            QaS = [alloc(sc, f"a_Q{i}", [128, 4, T], BF16) for i in range(2)]
            OgS = [alloc(sc, f"a_O{i}", [128, 4, T], BF16) for i in range(2)]
            og_free = [[], []]
            abuf_free = [[], []]
            es_bc = alloc(sc, "a_es", [128, 32])
            es_all = alloc(sc, "a_esall", [128, 8, 512])
            PTr = Ring([alloc(sc, f"a_pt{i}", [128, 512], BF16) for i in range(6)], "pt")
            DNr = Ring([alloc(sc, f"a_dn{i}", [128, 512]) for i in range(2)], "dn")
            t_es = P.dma("sp", "a_c", es_bc[:], sink.partition_broadcast(128))
            t_ex = P.act(es_bc[:], es_bc[:], AF.Exp, waits=[t_es])
            t_esa = None
            for hd in range(32):
                t_esa = P.ts(es_all[:, hd // 4, (hd % 4) * 128:(hd % 4 + 1) * 128], ones_f, es_bc[:, hd:hd + 1], ALU.mult, waits=[t_ex] + CONST)

            def load_g(g):
                s = g % 2
                k = f"ald{s}"
                P.dma("sp", k, KaS[s][:], KaT[g * 128:(g + 1) * 128, :], waits=abuf_free[s])
                P.dma("sp", k, VaS[s][:], Va[:, g * 128:(g + 1) * 128].rearrange("(tt p) d -> p tt d", p=128))
                P.dma("sp", k, QaS[s][:], QaT[g * 512:(g + 1) * 512, :].rearrange("(hq p) t -> p hq t", p=128))
                abuf_free[s] = []
                return P.tok(k)
            jobs = []
            for g in range(8):
                for n in range(8):
                    keys = []
                    if n >= 1:
                        keys.append((n - 1, maskL4))
                    keys.append((n, None))
                    keys.append((n + 1 if n < 7 else 8, maskU4))
                    keys.append((9, None))
                    keys.append((10, None))
                    for j, (kt, mask) in enumerate(keys):
                        jobs.append((g, n, j, len(keys), kt, mask))
            ld_tok = {0: load_g(0)}
            st = {}
            blk_state = {}
            blk_ctr = [0]

            def emit_S(i):
                g, n, j, nk, kt, mask = jobs[i]
                s = g % 2
                if j == 0 and n == 1 and g + 1 < 8:
                    ld_tok[g + 1] = load_g(g + 1)
                b = i % 4
                fr = ps_take(b)
                t_s = P.mm(ps[b][:, :].rearrange("p (a b) -> p a b", a=4), KaS[s][:, kt * 128:(kt + 1) * 128],
                           QaS[s][:, :, n * 128:(n + 1) * 128], waits=[ld_tok[g]] + fr)
                sp_, pt, frp = PTr.next()
                t_p = P.act(pt[:], ps[b][:, :], AF.Exp, waits=[t_s] + frp)
                psfree[b].append(t_p)
                if mask is not None:
                    t_p = P.tt(pt[:], pt[:], mask, ALU.mult, waits=[t_p] + CONST)
                st[i] = (t_p, pt, sp_)

            def emit_PV(i):
                g, n, j, nk, kt, mask = jobs[i]
                s = g % 2
                t_p, pt, sp_ = st.pop(i)
                if j == 0:
                    bo = 4 + (blk_ctr[0] % 2)
                    bd = 6 + (blk_ctr[0] % 2)
                    blk_ctr[0] += 1
                    blk_state[(g, n)] = (bo, bd, ps_take(bo), ps_take(bd))
                bo, bd, fro, frd = blk_state[(g, n)]
                t_o = P.mm(ps[bo][:, :], VaS[s][:, kt, :], pt[:], start=(j == 0), stop=(j == nk - 1),
                           waits=[t_p] + (fro if j == 0 else []))
                t_d = P.mm(ps[bd][:, :], ones_b, pt[:], start=(j == 0), stop=(j == nk - 1),
                           waits=(frd + CONST if j == 0 else []))
                PTr.release(sp_, t_d)
                if j == nk - 1:
                    sd, dn, frn = DNr.next()
                    t1_ = P.tt(dn[:], ps[bd][:, :], es_all[:, g, :], ALU.add, waits=[t_d, t_esa] + frn)
                    psfree[bd].append(t1_)
                    t2_ = P.act(dn[:], dn[:], AF.Ln, waits=[t1_])
                    t3_ = P.act(dn[:], dn[:], AF.Exp, scale=-1.0, waits=[t2_])
                    t4_ = P.tt(OgS[s][:, :, n * 128:(n + 1) * 128], ps[bo][:, :].rearrange("p (a b) -> p a b", a=4),
                               dn[:].rearrange("p (a b) -> p a b", a=4), ALU.mult, waits=[t_o, t3_] + og_free[s])
                    og_free[s] = []
                    psfree[bo].append(t4_)
                    DNr.release(sd, t4_)
                    if n == 7:
                        td = P.dma("sp", f"stha{s}", haT[g * 512:(g + 1) * 512, :].rearrange("(hq p) t -> p hq t", p=128), OgS[s][:], waits=[t4_])
                        og_free[s] = [td]
                        abuf_free[s] = [P.tok("pe")]
            LOOK = 2
            for i in range(min(LOOK, len(jobs))):
                emit_S(i)
            for i in range(len(jobs)):
                emit_PV(i)
                if i + LOOK < len(jobs):
                    emit_S(i + LOOK)
            P.barrier()

        if stop_after == "A":
            P.emit([P.tok(k) for k in P.dma_keys if k.startswith("st")])
            return nc

        segs2 = [[(0, 512, 0)], [(512, 512, 0)]]
        with contextlib.ExitStack() as sc:
            actM = alloc(sc, "y1_act", [128, KC, T], BF16)
            t_ld = P.dma("sp", "y1ld", actM[:], hmT.rearrange("(kc p) t -> p kc t", p=128))
            sgr = Ring([alloc(sc, f"y1_sg{i}", [128, 512]) for i in range(3)], "y1sg")
            stF = Stage(sc, "y1f", [128, 512], F32, 3)

            def ev_y1(b, ch, si, tok):
                ss_, sg, frs = sgr.next()
                tl_ = P.dma("sp", f"y1sg{ss_}", sg[:], sgmT[ch * 128:(ch + 1) * 128, si * 512:(si + 1) * 512], waits=frs)
                s_, ob, fr = stF.get()
                t = P.tt(ob[:], ps[b][:, :], sg[:], ALU.mult, waits=[tok, tl_] + fr)
                sgr.release(ss_, t)
                stF.store(s_, t1T[ch * 128:(ch + 1) * 128, si * 512:(si + 1) * 512], ob[:], t, "stT1")
                return [t]
            gemm_fm(w_out_m, 0, D, actM, segs2, [t_ld], ev_y1)
            P.barrier()

        with contextlib.ExitStack() as sc1:
            mixsT = alloc(sc1, "mixsT", [128, KC, T], BF16)
            with contextlib.ExitStack() as sc:
                actA = alloc(sc, "y2_act", [128, KC, T], BF16)
                t_ld = P.dma("sp", "y2ld", actA[:], haT.rearrange("(kc p) t -> p kc t", p=128))
                sgr = Ring([alloc(sc, f"y2_sg{i}", [128, 512]) for i in range(3)], "y2sg")
                t1r = Ring([alloc(sc, f"y2_t1{i}", [128, 512]) for i in range(3)], "y2t1")
                tmr = Ring([alloc(sc, f"y2_tm{i}", [128, 512]) for i in range(2)], "y2tm")

                def ev_y2(b, ch, si, tok):
                    ss_, sg, frs = sgr.next()
                    tl1 = P.dma("sp", f"y2sg{ss_}", sg[:], sgaT[ch * 128:(ch + 1) * 128, si * 512:(si + 1) * 512], waits=frs)
                    st_, t1b, frt = t1r.next()
                    tl2 = P.dma("sp", f"y2t1{st_}", t1b[:], t1T[ch * 128:(ch + 1) * 128, si * 512:(si + 1) * 512], waits=frt)
                    sm_, tm, frm = tmr.next()
                    ta = P.tt(tm[:], ps[b][:, :], sg[:], ALU.mult, waits=[tok, tl1] + frm)
                    sgr.release(ss_, ta)
                    tb = P.tt(mixsT[:, ch, si * 512:(si + 1) * 512], tm[:], t1b[:], ALU.add, waits=[ta, tl2])
                    t1r.release(st_, tb)
                    tmr.release(sm_, tb)
                    return [ta]
                gemm_fm(w_out_a, 0, D, actA, segs2, [t_ld], ev_y2)
            P.barrier()
            with contextlib.ExitStack() as sc:
                stF = Stage(sc, "o_f", [128, 512], F32, 4)

                def ev_o(b, c0, nb, tt, tok):
                    s_, ob, fr = stF.get()
                    if tt % 2 == 0:
                        t = P.act(ob[:, 0:nb], ps[b][:, 0:nb], AF.Copy, waits=[tok] + fr)
                    else:
                        t = P.cp(ob[:, 0:nb], ps[b][:, 0:nb], waits=[tok] + fr)
                    stF.store(s_, mixd[tt * 128:(tt + 1) * 128, c0:c0 + nb], ob[:, 0:nb], t, "stMix")
                    return [t]
                gemm_tm(w_o, 0, D, mixsT, range(NT), [P.tok("dve")], ev_o)
            P.barrier()

        if stop_after == "O":
            P.emit([P.tok(k) for k in P.dma_keys if k.startswith("st")])
            return nc

        def load_bc(tile, row_ap, waits=(), key="bcl"):
            return P.dma("sp", key, tile[:], row_ap.partition_broadcast(128), waits=list(waits))

        with contextlib.ExitStack() as sc1:
            act2 = alloc(sc1, "act2", [128, KC, T], BF16)
            with contextlib.ExitStack() as sc:
                g1n = alloc(sc, "r1_g1n", [128, D])
                a2 = alloc(sc, "r1_a2", [128, D])
                sh2 = alloc(sc, "r1_sh2", [128, D])
                mt = [alloc(sc, "r1_m0", [128, D])] * 2
                xt = [alloc(sc, "r1_x0", [128, D])] * 2
                ub = alloc(sc, "r1_ub", [128, D], BF16)
                sq = alloc(sc, "r1_sq", [128, 8])
                ta = load_bc(g1n, modrow[0, 2 * D:3 * D], [T_MOD])
                tb = load_bc(mt[0], norm_g[1, :])
                t_g1n = P.tt(g1n[:], g1n[:], mt[0][:], ALU.mult, waits=[ta, tb])
                ta = load_bc(a2, modrow[0, 4 * D:5 * D])
                tb = load_bc(xt[0], norm_g[2, :])
                t_a2 = P.stt(a2[:], a2[:], 1.0, xt[0][:], ALU.add, ALU.mult, waits=[ta, tb])
                t_sh2 = load_bc(sh2, modrow[0, 3 * D:4 * D])
                mfree = [[t_g1n], [t_g1n]]
                xfree = [[t_a2], [t_a2]]
                ub_free = []
                for tt in range(NT):
                    s = 0
                    tlm = P.dma("sp", f"r1m{s}", mt[s][:], mixd[tt * 128:(tt + 1) * 128, :], waits=mfree[s])
                    tlx = P.dma("sp", f"r1x{s}", xt[s][:], x_own[tt * 128:(tt + 1) * 128, :], waits=xfree[s])
                    tsq = P.act(ub[:], mt[s][:], AF.Square, accum_out=sq[:, 0:1], waits=[tlm] + ub_free)
                    tsq = P.fence("act", sq[:, 0:1])
                    tv = P.ts(sq[:, 1:2], sq[:, 0:1], 1.0 / D, ALU.mult, EPS, ALU.add, waits=[tsq])
                    tv = P.fence("dve", sq[:, 1:2])
                    tq = P.act(sq[:, 2:3], sq[:, 1:2], AF.Sqrt, waits=[tv])
                    tq = P.fence("act", sq[:, 2:3])
                    tr_ = P.recip(sq[:, 3:4], sq[:, 2:3], waits=[tq])
                    tr_ = P.fence("dve", sq[:, 3:4])
                    tm1 = P.stt(mt[s][:], mt[s][:], sq[:, 3:4], g1n[:], ALU.mult, ALU.mult, waits=[tr_, t_g1n])
                    th = P.tt(xt[s][:], xt[s][:], mt[s][:], ALU.add, waits=[tm1, tlx])
                    tst = P.dma("sp", "stH", hbuf[tt * 128:(tt + 1) * 128, :], xt[s][:], waits=[th])
                    tsq2 = P.act(ub[:], xt[s][:], AF.Square, accum_out=sq[:, 4:5], waits=[th])
                    tsq2 = P.fence("act", sq[:, 4:5])
                    tv2 = P.ts(sq[:, 5:6], sq[:, 4:5], 1.0 / D, ALU.mult, EPS, ALU.add, waits=[tsq2])
                    tv2 = P.fence("dve", sq[:, 5:6])
                    tq2 = P.act(sq[:, 6:7], sq[:, 5:6], AF.Sqrt, waits=[tv2])
                    tq2 = P.fence("act", sq[:, 6:7])
                    tr2 = P.recip(sq[:, 7:8], sq[:, 6:7], waits=[tq2])
                    tr2 = P.fence("dve", sq[:, 7:8])
                    tu1 = P.stt(mt[s][:], xt[s][:], sq[:, 7:8], a2[:], ALU.mult, ALU.mult, waits=[tr2, t_a2, tm1])
                    tu = P.tt(ub[:], mt[s][:], sh2[:], ALU.add, waits=[tu1, t_sh2, tsq2])
                    mfree[s] = [tu]
                    xfree[s] = [tst, tu1]
                    for q4 in range(4):
                        b, fr = next_bank()
                        pb = ps[b][:].bitcast(BF16)
                        for i in range(8):
                            kc = q4 * 8 + i
                            tp = P.tr(pb[:, i * 128:(i + 1) * 128], ub[:, kc * 128:(kc + 1) * 128], ident_b,
                                      waits=[tu] + fr + CONST if i == 0 else [], inc=(i == 7))
                        dst = act2[:, q4 * 8:(q4 + 1) * 8, tt * 128:(tt + 1) * 128]
                        srcp = pb.rearrange("p (a b) -> p a b", a=8)
                        te = P.act(dst, srcp, AF.Copy, waits=[tp]) if q4 % 2 == 0 else P.cp(dst, srcp, waits=[tp])
                        psfree[b].append(te)
                    ub_free = [tp]
            P.barrier()

            with contextlib.ExitStack() as sc:
                stB = Stage(sc, "f1b", [128, 512], BF16, 4)
                tmr = Ring([alloc(sc, f"f1_t{i}", [128, 512]) for i in range(3)], "f1t")

                def ev_f1(b, ch, si, tok):
                    sm_, tm, frm = tmr.next()
                    ta = P.act(tm[:], ps[b][:, :], AF.Relu, waits=[tok] + frm)
                    s_, ob, fr = stB.get()
                    tb = P.tt(ob[:], tm[:], tm[:], ALU.mult, waits=[ta] + fr)
                    tmr.release(sm_, tb)
                    stB.store(s_, h1T[ch * 128:(ch + 1) * 128, si * 512:(si + 1) * 512], ob[:], tb, "stH1")
                    return [ta]
                gemm_fm(w_ff1, 0, DFF, act2, segs2, [], ev_f1)
            P.barrier()

        with contextlib.ExitStack() as sc:
            acc = alloc(sc, "f2_acc", [128, NT, 2048])
            h1g = [alloc(sc, f"f2_h{i}", [128, 16, T], BF16) for i in range(2)]
            hfree = [[], []]
            gi = 0
            acc_tok = {}
            acc_rd = []
            for nh in range(2):
                for fg in range(8):
                    s = gi % 2
                    gi += 1
                    tlh = P.dma("sp", f"f2h{s}", h1g[s][:], h1T[fg * 2048:(fg + 1) * 2048, :].rearrange("(fc p) t -> p fc t", p=128),
                                waits=hfree[s] + [P.tok("stH1")])
                    tmm = None
                    for nbk in range(4):
                        c0 = nh * 2048 + nbk * 512
                        sw, wt, tw = WS.fetch(w_ff2[fg * 2048:(fg + 1) * 2048, c0:c0 + 512].rearrange("(fc p) n -> p fc n", p=128),
                                              lambda t: t[:].rearrange("p a b -> p (a b)").rearrange("p (c d) -> p c d", c=16))
                        wv = wt[:].rearrange("p a b -> p (a b)").rearrange("p (c d) -> p c d", c=16)
                        for tt in range(NT):
                            b, fr = next_bank()
                            for fc in range(16):
                                tmm = P.mm(ps[b][:, :], h1g[s][:, fc, tt * 128:(tt + 1) * 128], wv[:, fc, :], start=(fc == 0), stop=(fc == 15),
                                           waits=([tw, tlh] + fr) if fc == 0 else [], inc=(fc == 15))
                            dst = acc[:, tt, nbk * 512:(nbk + 1) * 512]
                            if fg == 0:
                                if tt % 2 == 0:
                                    t = P.act(dst, ps[b][:, :], AF.Copy, waits=[tmm] + acc_rd)
                                else:
                                    t = P.cp(dst, ps[b][:, :], waits=[tmm] + acc_rd)
                            else:
                                t = P.tt(dst, ps[b][:, :], dst, ALU.add, waits=[tmm, acc_tok[(tt, nbk)]])
                            acc_tok[(tt, nbk)] = t
                            psfree[b].append(t)
                        WS.ring.release(sw, tmm)
                    hfree[s] = [tmm]
                acc_rd = []
                for tt in range(NT):
                    td = P.dma("sp", "stFfn", ffn[tt * 128:(tt + 1) * 128, nh * 2048:(nh + 1) * 2048], acc[:, tt, :],
                               waits=[acc_tok[(tt, k)] for k in range(4)])
                    acc_rd.append(td)
            P.barrier()

        if stop_after == "F2":
            P.emit([P.tok(k) for k in P.dma_keys if k.startswith("st")])
            return nc

        with contextlib.ExitStack() as sc:
            g2n = alloc(sc, "r2_g2n", [128, D])
            ft = [alloc(sc, f"r2_f{i}", [128, D]) for i in range(2)]
            ht = [alloc(sc, f"r2_h{i}", [128, D]) for i in range(2)]
            ub = alloc(sc, "r2_ub", [128, D], BF16)
            sq = alloc(sc, "r2_sq", [128, 4])
            ta = load_bc(g2n, modrow[0, 5 * D:6 * D], [T_MOD], key="bcl2")
            tb = load_bc(ft[0], norm_g[3, :], key="bcl2")
            t_g2n = P.tt(g2n[:], g2n[:], ft[0][:], ALU.mult, waits=[ta, tb])
            ffree = [[t_g2n], []]
            hfree2 = [[], []]
            ub_free = []
            for tt in range(NT):
                s = tt % 2
                tlf = P.dma("sp", f"r2f{s}", ft[s][:], ffn[tt * 128:(tt + 1) * 128, :], waits=ffree[s])
                tlh = P.dma("sp", f"r2h{s}", ht[s][:], hbuf[tt * 128:(tt + 1) * 128, :], waits=hfree2[s] + [P.tok("stH")])
                tsq = P.act(ub[:], ft[s][:], AF.Square, accum_out=sq[:, 0:1], waits=[tlf])
                tsq = P.fence("act", sq[:, 0:1])
                tv = P.ts(sq[:, 1:2], sq[:, 0:1], 1.0 / D, ALU.mult, EPS, ALU.add, waits=[tsq])
                tv = P.fence("dve", sq[:, 1:2])
                tq = P.act(sq[:, 2:3], sq[:, 1:2], AF.Sqrt, waits=[tv])
                tq = P.fence("act", sq[:, 2:3])
                tr_ = P.recip(sq[:, 3:4], sq[:, 2:3], waits=[tq])
                tr_ = P.fence("dve", sq[:, 3:4])
                tm1 = P.stt(ft[s][:], ft[s][:], sq[:, 3:4], g2n[:], ALU.mult, ALU.mult, waits=[tr_, t_g2n])
                th = P.tt(ht[s][:], ht[s][:], ft[s][:], ALU.add, waits=[tm1, tlh])
                tst = P.dma("sp", "stOut", out[tt * 128:(tt + 1) * 128, :], ht[s][:], waits=[th])
                ffree[s] = [th]
                hfree2[s] = [tst]
        P.emit([P.tok("stOut")])
        return nc


def _consts():
    idx = np.arange(128)
    ident = np.eye(128, dtype=np.float32)
    triU = (idx[:, None] <= idx[None, :]).astype(np.float32)
    triL = (idx[:, None] >= idx[None, :]).astype(np.float32)
    ones = np.ones((128, 128), np.float32)
    R = np.zeros((128, 128), np.float32)
    for m in range(128):
        if (m % 64) < 32:
            R[m, m + 32] = -1.0
        else:
            R[m, m - 32] = 1.0
    RT = np.ascontiguousarray(R.T)
    cf = np.concatenate([ident, triU, triL, ones, RT], axis=1)
    bf = ml_dtypes.bfloat16
    maskU4 = np.tile(triU, (1, 4))
    maskL4 = np.tile(triL, (1, 4))
    cb = np.concatenate([ident, ones, maskU4, maskL4], axis=1).astype(bf)
    return cf, cb


def _rope_tables(pos):
    nf = 32
    freqs = (10000.0 ** (-np.arange(nf, dtype=np.float32) / nf)).astype(np.float32)
    row = (pos // 64).astype(np.float32)
    col = (pos % 64).astype(np.float32)
    ang_r = row[None, :] * freqs[:, None]
    ang_c = col[None, :] * freqs[:, None]
    ang = np.concatenate([ang_r, ang_r, ang_c, ang_c], axis=0)
    return np.stack([np.cos(ang), np.sin(ang)], axis=1).astype(np.float32)


def prep_inputs(inp, cores=range(8)):
    f = lambda a: np.ascontiguousarray(np.asarray(a, dtype=np.float32))
    x, c, ctx, c_ctx = f(inp["x"]), f(inp["c"]), f(inp["ctx"]), f(inp["c_ctx"])
    w_in = f(inp["w_in"])[0]
    shared = {
        "w_mod": f(inp["w_mod"])[0], "b_mod": f(inp["b_mod"])[0], "norm_g": f(inp["norm_g"])[0],
        "w_in": w_in, "m_norm_g": f(inp["m_norm_g"])[0], "sink": f(inp["attn_sink"])[0],
        "w_out_m": f(inp["w_out_m"])[0], "w_out_a": f(inp["w_out_a"])[0], "w_o": f(inp["w_o"])[0],
        "w_ff1": f(inp["w_ff1"])[0], "w_ff2": f(inp["w_ff2"])[0],
    }
    cf, cb = _consts()
    shared["cf"] = cf
    shared["cb"] = cb
    wg = w_in[:, O_GM:O_GM + 32]
    gb = f(inp["m_gate_b"])[0].reshape(32)
    swap = np.concatenate([np.arange(16, 32), np.arange(0, 16)])
    maps = []
    for core in cores:
        b, hh = core // 2, core % 2
        loc = np.arange(2048) if hh == 0 else (2047 - np.arange(2048))
        xb = x[b][loc]
        cx = ctx[b] if hh == 0 else ctx[b][::-1]
        m = dict(shared)
        m["x_own"] = np.ascontiguousarray(xb[:T])
        m["x_pre"] = np.ascontiguousarray(np.concatenate([xb[T:], cx], axis=0))
        m["cvec"] = np.ascontiguousarray(np.stack([c[b], c_ctx], axis=0).reshape(2, KC, 128).transpose(2, 1, 0))
        if hh == 0:
            m["w_gate"] = np.ascontiguousarray(wg)
            m["gate_b"] = np.ascontiguousarray(gb)
        else:
            m["w_gate"] = np.ascontiguousarray(wg[:, swap])
            m["gate_b"] = np.ascontiguousarray(gb[swap])
        m["ropec"] = np.ascontiguousarray(_rope_tables(loc[:T + 128]))
        maps.append(m)
    return maps


def kernel(**inputs):
    nc = build_program()
    maps = prep_inputs(inputs)
    res = run_bass_kernel_spmd(nc, maps, core_ids=list(range(8)))
    outp = np.empty((4, 2048, D), np.float32)
    for core in range(8):
        b, hh = core // 2, core % 2
        o = res.results[core]["out"]
        if hh == 0:
            outp[b, :T] = o
        else:
            outp[b, T:] = o[::-1]
    return outp
```
